# Optimizing a Trainium2 kernel written in Bass

```python
import jax, jax.numpy as jnp
from jax import lax
import numpy as np

D_MODEL = 1024
BATCH = 4
SEQ = 4096
DEPTH = 1
DEC_BATCH = 32
DEC_SEQ = 1
PAST_LEN = 16384
PAGE_SIZE = 128

HEAD_DIM = 64
ROPE_DIM = HEAD_DIM // 4
ROPE_THETA = 500000.0
DSWA_GROUPS = ((128, 1), (512, 4), (2048, 16))
HEADS_PER_GROUP = 4
N_GROUPS = len(DSWA_GROUPS)
A_HEADS = N_GROUPS * HEADS_PER_GROUP
A_QKV = A_HEADS * HEAD_DIM
A_OUT = HEADS_PER_GROUP * HEAD_DIM
BAND = 128
M_HEADS = 4
M_INNER = D_MODEL
M_DK = M_INNER // M_HEADS
M_DV = M_INNER // M_HEADS
CONV_W = 4
M_CHUNK = 64
D_FF = -(-(8 * D_MODEL) // (3 * 256)) * 256
PLE_DIM = 256
EPS = 1e-6
NEG = -1e30

OFF_AQ = 0
OFF_AK = OFF_AQ + A_QKV
OFF_AV = OFF_AK + A_QKV
OFF_MQ = OFF_AV + A_QKV
OFF_MK = OFF_MQ + M_INNER
OFF_MV = OFF_MK + M_INNER
OFF_MO = OFF_MV + M_INNER
OFF_MI = OFF_MO + M_INNER
OFF_MF = OFF_MI + M_HEADS
OFF_GA = OFF_MF + M_HEADS
OFF_GB = OFF_GA + D_MODEL
IN_COLS = OFF_GB + D_MODEL

kernel_name = 'dilated_swa_mlstm_hybrid_step'


def _rmsnorm(x, g):
    xf = x.astype(jnp.float32)
    y = xf * lax.rsqrt(jnp.mean(xf * xf, axis=-1, keepdims=True) + EPS)
    return (y * g.astype(jnp.float32)).astype(x.dtype)


def _rope(x, pos):
    half = ROPE_DIM // 2
    inv = jnp.power(ROPE_THETA, -jnp.arange(half, dtype=jnp.float32) / half)
    ang = pos.astype(jnp.float32)[:, None] * inv[None, :]
    cos = jnp.cos(ang)[None, :, None, :]
    sin = jnp.sin(ang)[None, :, None, :]
    xf = x.astype(jnp.float32)
    x1, x2, rest = xf[..., :half], xf[..., half:ROPE_DIM], xf[..., ROPE_DIM:]
    out = jnp.concatenate([x1 * cos - x2 * sin, x2 * cos + x1 * sin, rest], axis=-1)
    return out.astype(x.dtype)


def _dilated_attn_prompt(q, k, v, dil):
    B, S, H, D = q.shape
    span = dil * BAND
    s_pad = -(-S // span) * span
    nsub = s_pad // dil
    nb = nsub // BAND

    def to_blocks(t):
        t = jnp.pad(t.astype(jnp.float32), ((0, 0), (0, s_pad - S), (0, 0), (0, 0)))
        t = t.reshape(B, nsub, dil, H, D).transpose(0, 2, 1, 3, 4)
        return t.reshape(B, dil, nb, BAND, H, D)

    def with_prev(t):
        prev = jnp.pad(t, ((0, 0), (0, 0), (1, 0), (0, 0), (0, 0), (0, 0)))[:, :, :-1]
        return jnp.concatenate([prev, t], axis=3)

    qb, kb, vb = to_blocks(q), to_blocks(k), to_blocks(v)
    kk, vv = with_prev(kb), with_prev(vb)
    s = jnp.einsum('brnqhd,brnkhd->brnhqk', qb, kk) * (HEAD_DIM ** -0.5)
    qi = jnp.arange(BAND)[:, None]
    kj = jnp.arange(2 * BAND)[None, :]
    dist = BAND + qi - kj
    blk = jnp.arange(nb)[:, None, None]
    valid = (dist >= 0) & (dist <= BAND) & ((blk > 0) | (kj >= BAND))
    s = jnp.where(valid[None, None, :, None], s, -jnp.inf)
    mx = jnp.max(s, axis=-1, keepdims=True)
    e = jnp.exp(s - mx)
    den = jnp.sum(e, axis=-1, keepdims=True)
    o = jnp.einsum('brnhqk,brnkhd->brnqhd', e / den, vv)
    lse = (mx + jnp.log(den))[..., 0]
    o = o.reshape(B, dil, nsub, H, D).transpose(0, 2, 1, 3, 4).reshape(B, s_pad, H, D)[:, :S]
    lse = lse.transpose(0, 1, 2, 4, 3).reshape(B, dil, nsub, H).transpose(0, 2, 1, 3).reshape(B, s_pad, H)[:, :S]
    return o, lse


def _dilated_attn_gather(q, k_ext, v_ext, dil, past):
    B, T, H, D = q.shape
    idx = past + jnp.arange(T)[:, None] - dil * jnp.arange(BAND + 1)[None, :]
    valid = idx >= 0
    flat = jnp.maximum(idx, 0).reshape(-1)
    kg = jnp.take(k_ext, flat, axis=1).reshape(B, T, BAND + 1, H, D).astype(jnp.float32)
    vg = jnp.take(v_ext, flat, axis=1).reshape(B, T, BAND + 1, H, D).astype(jnp.float32)
    s = jnp.einsum('bthd,btkhd->bthk', q.astype(jnp.float32), kg) * (HEAD_DIM ** -0.5)
    s = jnp.where(valid[None, :, None, :], s, -jnp.inf)
    mx = jnp.max(s, axis=-1, keepdims=True)
    e = jnp.exp(s - mx)
    den = jnp.sum(e, axis=-1, keepdims=True)
    o = jnp.einsum('bthk,btkhd->bthd', e / den, vg)
    lse = (mx + jnp.log(den))[..., 0]
    return o, lse


def _causal_conv(x, buf, w, b):
    T = x.shape[1]
    xp = jnp.concatenate([buf.astype(x.dtype), x], axis=1)
    y = b + w[0] * xp[:, 0:T]
    for j in range(1, CONV_W):
        y = y + w[j] * xp[:, j:j + T]
    return y, xp[:, T:]


def _mlstm(q, k, v, ig, lf, C0, n0, m0):
    B, T, H, DK = q.shape
    DV = v.shape[-1]
    L = min(M_CHUNK, T)
    nc = -(-T // L)
    pad = nc * L - T
    f32 = jnp.float32

    def chunks(t, fill):
        t = jnp.pad(t.astype(f32), ((0, 0), (0, pad)) + ((0, 0),) * (t.ndim - 2), constant_values=fill)
        t = t.reshape((B, nc, L) + t.shape[2:])
        return jnp.moveaxis(jnp.moveaxis(t, 1, 0), 2, 3)

    xs = (chunks(q, 0.0), chunks(k, 0.0), chunks(v, 0.0), chunks(ig, NEG), chunks(lf, 0.0))
    tril = jnp.tril(jnp.ones((L, L), dtype=bool))

    def step(carry, xc):
        C, n, m = carry
        qc, kc, vc, ic, fc = xc
        b = jnp.cumsum(fc, axis=-1)
        dmat = jnp.where(tril, b[..., :, None] - b[..., None, :] + ic[..., None, :], -jnp.inf)
        m_inter = b + m[..., None]
        m_t = jnp.maximum(m_inter, jnp.max(dmat, axis=-1))
        w_intra = jnp.einsum('bhtd,bhsd->bhts', qc, kc) * jnp.exp(dmat - m_t[..., None])
        w_inter = jnp.exp(m_inter - m_t)
        num = jnp.einsum('bhts,bhsv->bhtv', w_intra, vc) + w_inter[..., None] * jnp.einsum('bhtd,bhdv->bhtv', qc, C)
        den = jnp.sum(w_intra, axis=-1) + w_inter * jnp.einsum('bhtd,bhd->bht', qc, n)
        h = num / jnp.maximum(jnp.abs(den), jnp.exp(-m_t))[..., None]
        b_last = b[..., -1]
        g = b_last[..., None] - b + ic
        m_new = jnp.maximum(b_last + m, jnp.max(g, axis=-1))
        wk = jnp.exp(g - m_new[..., None])
        decay = jnp.exp(b_last + m - m_new)
        C_new = decay[..., None, None] * C + jnp.einsum('bhs,bhsd,bhsv->bhdv', wk, kc, vc)
        n_new = decay[..., None] * n + jnp.einsum('bhs,bhsd->bhd', wk, kc)
        return (C_new, n_new, m_new), h

    (C1, n1, m1), hs = lax.scan(step, (C0.astype(f32), n0.astype(f32), m0.astype(f32)), xs)
    h = jnp.moveaxis(hs, 0, 1)
    h = jnp.swapaxes(h, 2, 3).reshape(B, nc * L, H, DV)[:, :T]
    return h, C1, n1, m1


def _layer(h, p, pos0, kv_bufs, conv_buf, C0, n0, m0, norm_mix, w_in, conv_w, conv_b, b_igate, b_fgate, mh_norm, w_proj_a, w_proj_b, w_out, norm_ffn, w_gate, w_up, w_down, norm_ple, w_ple_gate, w_ple_proj):
    B, T, _ = h.shape
    f32 = jnp.float32
    xn = _rmsnorm(h, norm_mix)
    z = xn @ w_in
    pos = pos0 + jnp.arange(T, dtype=jnp.int32)

    qa = _rope(z[..., OFF_AQ:OFF_AK].reshape(B, T, A_HEADS, HEAD_DIM), pos)
    ka = _rope(z[..., OFF_AK:OFF_AV].reshape(B, T, A_HEADS, HEAD_DIM), pos)
    va = z[..., OFF_AV:OFF_MQ].reshape(B, T, A_HEADS, HEAD_DIM)
    outs, lses, new_kv = [], [], []
    for g, (win, dil) in enumerate(DSWA_GROUPS):
        sl = slice(g * HEADS_PER_GROUP, (g + 1) * HEADS_PER_GROUP)
        kv_new = jnp.stack([ka[:, :, sl], va[:, :, sl]], axis=2)
        if kv_bufs is None:
            o, lse = _dilated_attn_prompt(qa[:, :, sl], ka[:, :, sl], va[:, :, sl], dil)
            ext = kv_new
        else:
            past = kv_bufs[g].shape[1]
            ext = jnp.concatenate([kv_bufs[g].astype(kv_new.dtype), kv_new], axis=1)
            o, lse = _dilated_attn_gather(qa[:, :, sl], ext[:, :, 0], ext[:, :, 1], dil, past)
        outs.append(o)
        lses.append(lse)
        keep = min(win, ext.shape[1])
        new_kv.append(ext[:, ext.shape[1] - keep:])
    wgt = jax.nn.softmax(jnp.stack(lses, axis=0), axis=0)
    o_a = jnp.einsum('gbth,gbthd->bthd', wgt, jnp.stack(outs, axis=0))
    o_a = o_a.reshape(B, T, A_OUT).astype(h.dtype)

    qk, conv_new = _causal_conv(z[..., OFF_MQ:OFF_MV], conv_buf, conv_w, conv_b)
    qk = jax.nn.silu(qk)
    qm = qk[..., :M_INNER].reshape(B, T, M_HEADS, M_DK)
    km = qk[..., M_INNER:].reshape(B, T, M_HEADS, M_DK) * (M_DK ** -0.5)
    vm = z[..., OFF_MV:OFF_MO].reshape(B, T, M_HEADS, M_DV)
    og = jax.nn.sigmoid(z[..., OFF_MO:OFF_MI])
    ig = z[..., OFF_MI:OFF_MF].astype(f32) + b_igate.astype(f32)
    lf = jax.nn.log_sigmoid(z[..., OFF_MF:OFF_GA].astype(f32) + b_fgate.astype(f32))
    hm, C1, n1, m1 = _mlstm(qm, km, vm, ig, lf, C0, n0, m0)
    hm = hm * lax.rsqrt(jnp.mean(hm * hm, axis=-1, keepdims=True) + EPS) * mh_norm.astype(f32)
    o_b = hm.reshape(B, T, M_INNER).astype(h.dtype) * og

    ga = jax.nn.sigmoid(z[..., OFF_GA:OFF_GB])
    gb = jax.nn.sigmoid(z[..., OFF_GB:IN_COLS])
    mix = ga * (o_a @ w_proj_a) + gb * (o_b @ w_proj_b)
    h = h + mix @ w_out

    xf = _rmsnorm(h, norm_ffn)
    h = h + (jax.nn.silu(xf @ w_gate) * (xf @ w_up)) @ w_down

    xp = _rmsnorm(h, norm_ple)
    h = h + jax.nn.sigmoid(xp @ w_ple_gate) * (p.astype(h.dtype) @ w_ple_proj)
    return h, (new_kv[0], new_kv[1], new_kv[2], conv_new, C1.astype(C0.dtype), n1.astype(n0.dtype), m1.astype(m0.dtype))


def setup_inputs(seed: int = 0) -> dict:
    key = jax.random.key(seed)
    k = jax.random.split(key, 29)
    f32 = jnp.float32

    def nrm(i, shape, scale=1.0):
        return jax.random.normal(k[i], shape, f32) * scale

    def gain(i, shape):
        return 1.0 + 0.02 * jax.random.normal(k[i], shape, f32)

    win_len = [min(w, PAST_LEN) for w, _ in DSWA_GROUPS]

    def kv_shape(n):
        return (DEPTH, DEC_BATCH, n, 2, HEADS_PER_GROUP, HEAD_DIM)

    return {
        'x_prompt': nrm(0, (BATCH, SEQ, D_MODEL)),
        'x_sample': nrm(1, (DEC_BATCH, DEC_SEQ, D_MODEL)),
        'cache_kv_w128': nrm(2, kv_shape(win_len[0])),
        'cache_kv_w512': nrm(3, kv_shape(win_len[1])),
        'cache_kv_w2048': nrm(4, kv_shape(win_len[2])),
        'state_conv': nrm(5, (DEPTH, DEC_BATCH, CONV_W - 1, 2 * M_INNER)),
        'state_C': nrm(6, (DEPTH, DEC_BATCH, M_HEADS, M_DK, M_DV), M_DK ** -0.5),
        'state_n': nrm(7, (DEPTH, DEC_BATCH, M_HEADS, M_DK), M_DK ** -0.5),
        'state_m': nrm(8, (DEPTH, DEC_BATCH, M_HEADS)),
        'p_prompt': nrm(9, (DEPTH, BATCH, SEQ, PLE_DIM)),
        'p_sample': nrm(10, (DEPTH, DEC_BATCH, DEC_SEQ, PLE_DIM)),
        'norm_mix': gain(11, (DEPTH, D_MODEL)),
        'w_in': nrm(12, (DEPTH, D_MODEL, IN_COLS), D_MODEL ** -0.5),
        'conv_w': nrm(13, (DEPTH, CONV_W, 2 * M_INNER), CONV_W ** -0.5),
        'conv_b': nrm(14, (DEPTH, 2 * M_INNER), 0.02),
        'b_igate': nrm(15, (DEPTH, M_HEADS), 0.1),
        'b_fgate': jnp.linspace(3.0, 6.0, M_HEADS, dtype=f32)[None, :] + nrm(16, (DEPTH, M_HEADS), 0.1),
        'mh_norm': gain(17, (DEPTH, M_HEADS, M_DV)),
        'w_proj_a': nrm(18, (DEPTH, A_OUT, D_MODEL), A_OUT ** -0.5),
        'w_proj_b': nrm(19, (DEPTH, M_INNER, D_MODEL), M_INNER ** -0.5),
        'w_out': nrm(20, (DEPTH, D_MODEL, D_MODEL), D_MODEL ** -0.5),
        'norm_ffn': gain(21, (DEPTH, D_MODEL)),
        'w_gate': nrm(22, (DEPTH, D_MODEL, D_FF), D_MODEL ** -0.5),
        'w_up': nrm(23, (DEPTH, D_MODEL, D_FF), D_MODEL ** -0.5),
        'w_down': nrm(24, (DEPTH, D_FF, D_MODEL), D_FF ** -0.5),
        'norm_ple': gain(25, (DEPTH, D_MODEL)),
        'w_ple_gate': nrm(26, (DEPTH, D_MODEL, D_MODEL), D_MODEL ** -0.5),
        'w_ple_proj': nrm(27, (DEPTH, PLE_DIM, D_MODEL), PLE_DIM ** -0.5),
        'norm_final': gain(28, (D_MODEL,)),
    }


def reference(x_prompt, x_sample, cache_kv_w128, cache_kv_w512, cache_kv_w2048, state_conv, state_C, state_n, state_m, p_prompt, p_sample, norm_mix, w_in, conv_w, conv_b, b_igate, b_fgate, mh_norm, w_proj_a, w_proj_b, w_out, norm_ffn, w_gate, w_up, w_down, norm_ple, w_ple_gate, w_ple_proj, norm_final):
    caches = (cache_kv_w128, cache_kv_w512, cache_kv_w2048)
    bp = x_prompt.shape[0]
    hp, hs = x_prompt, x_sample
    st_p, st_s = [], []
    for l in range(DEPTH):
        lw = (norm_mix[l], w_in[l], conv_w[l], conv_b[l], b_igate[l], b_fgate[l], mh_norm[l],
              w_proj_a[l], w_proj_b[l], w_out[l], norm_ffn[l], w_gate[l], w_up[l], w_down[l],
              norm_ple[l], w_ple_gate[l], w_ple_proj[l])
        zc = jnp.zeros((bp, CONV_W - 1, 2 * M_INNER), x_prompt.dtype)
        zC = jnp.zeros((bp, M_HEADS, M_DK, M_DV), state_C.dtype)
        zn = jnp.zeros((bp, M_HEADS, M_DK), state_n.dtype)
        zm = jnp.zeros((bp, M_HEADS), state_m.dtype)
        hp, sp = _layer(hp, p_prompt[l], 0, None, zc, zC, zn, zm, *lw)
        hs, ss = _layer(hs, p_sample[l], PAST_LEN, (caches[0][l], caches[1][l], caches[2][l]),
                        state_conv[l], state_C[l], state_n[l], state_m[l], *lw)
        st_p.append(sp)
        st_s.append(ss)

    def stk(i, sts):
        return jnp.stack([s[i] for s in sts], axis=0)

    y_prompt = _rmsnorm(hp, norm_final)
    y_sample = _rmsnorm(hs, norm_final)
    return (y_prompt, y_sample, stk(0, st_p), stk(0, st_s), stk(1, st_p), stk(1, st_s), stk(2, st_p), stk(2, st_s), stk(3, st_p), stk(3, st_s), stk(4, st_p), stk(4, st_s), stk(5, st_p), stk(5, st_s), stk(6, st_p), stk(6, st_s))
```

```python
from contextlib import ExitStack
import numpy as np
import concourse.bass as bass
import concourse.mybir as mybir
from concourse.bass_utils import run_bass_kernel_spmd

F32 = mybir.dt.float32
BF16 = mybir.dt.bfloat16
AF = mybir.ActivationFunctionType
ALU = mybir.AluOpType

COMPUTE = ("pe", "act", "dve", "pool")
ROLL = 24000

D = 1024
SEQ = 4096
NT = 2048
HD = 64
DILS = (1, 4, 16)
WINS = (128, 512, 2048)
PAST = 16384
OFF_AQ, OFF_AK, OFF_AV = 0, 768, 1536
OFF_MQ, OFF_MK, OFF_MV, OFF_MO = 2304, 3328, 4352, 5376
OFF_MI, OFF_MF, OFF_GA, OFF_GB = 6400, 6404, 6408, 7432
IN_COLS = 8456
DFF = 2816
NJ = DFF // 128
EPS = 1e-6
NEGPAD = -30000.0
HB = (0, 1, 5)
RB = (0, 17, 37)


class Res:
    __slots__ = ("name", "w", "rd", "psum")

    def __init__(self, name, psum=False):
        self.name = name
        self.w = None
        self.rd = []
        self.psum = psum


class _Rec:
    def __init__(self):
        self.call = None

    def __getattr__(self, name):
        def f(*a, **k):
            self.call = (name, a, k)
            return None
        return f


class Prog:
    def __init__(self, nc, n_dma_sems=12):
        self.nc = nc
        self.scopes = [ExitStack()]
        self.streams = {e: [] for e in ("pe", "act", "dve", "pool", "sp")}
        self.sem = {}
        self.cnt = {}
        self.seen = {e: {} for e in self.streams}
        self.nsem = 0
        for e in COMPUTE:
            self._new_sem(e)
        self.dma_sems = {}
        self.dma_tot = {}
        self.dma_rr = {}
        for q in ("sp", "pool"):
            self.dma_sems[q] = [self._alloc_sem("d%s%d" % (q, i)) for i in range(n_dma_sems)]
            self.dma_tot[q] = [0] * n_dma_sems
            self.dma_rr[q] = 0
        self.res = {}
        self.nops = 0

    def _alloc_sem(self, name):
        self.nsem += 1
        return self.scopes[0].enter_context(self.nc.semaphore(name))

    def _new_sem(self, e):
        self.sem[e] = self._alloc_sem("c%s%d" % (e, self.nsem))
        self.cnt[e] = 0

    def sb(self, name, shape, dt):
        self.ntens = getattr(self, "ntens", 0) + 1
        return self.scopes[-1].enter_context(self.nc.sbuf_tensor("s%d_%s" % (self.ntens, name), list(shape), dt))

    def ps(self, name, shape, dt):
        return self.scopes[0].enter_context(self.nc.psum_tensor("p_" + name, list(shape), dt))

    def push(self):
        self.scopes.append(ExitStack())

    def pop(self):
        self.flush()
        self.scopes.pop().close()

    def R(self, name, psum=False):
        r = self.res.get(name)
        if r is None:
            r = Res(name, psum)
            self.res[name] = r
        return r

    def _need(self, e, tok, deps, same_raw):
        if tok is None:
            return
        we, sem, val = tok
        if we == e and e == "pe":
            return
        deps.append((sem, val))

    def _waits(self, e, reads, writes):
        deps = []
        for r in reads:
            self._need(e, r.w, deps, True)
            if r.psum:
                for t in r.rd:
                    self._need(e, t, deps, False)
        for w in writes:
            self._need(e, w.w, deps, w.psum)
            for t in w.rd:
                self._need(e, t, deps, False)
        seen = self.seen[e]
        best = {}
        for sem, val in deps:
            k = id(sem)
            if seen.get(k, 0) >= val:
                continue
            if k not in best or best[k][1] < val:
                best[k] = (sem, val)
        for k, (sem, val) in best.items():
            seen[k] = val
            self.streams[e].append(("wait", sem, val))

    def _commit(self, tok, reads, writes):
        for r in reads:
            r.rd = [t for t in r.rd if not (t[1] is tok[1])] + [tok]
        for w in writes:
            w.w = tok
            w.rd = []

    def _rl(self, xs):
        return [self.R(x) if isinstance(x, str) else x for x in xs]

    def op(self, e, fn, reads=(), writes=(), signal=True):
        reads = self._rl(reads)
        writes = self._rl(writes)
        self._waits(e, reads, writes)
        self.nops += 1
        rec = _Rec()
        fn(rec)
        fn = rec.call
        if signal:
            if self.cnt[e] >= ROLL:
                self._new_sem(e)
            self.cnt[e] += 1
            tok = (e, self.sem[e], self.cnt[e])
            self.streams[e].append(("op", fn, self.sem[e]))
        else:
            tok = (e, self.sem[e], self.cnt[e] + 1)
            self.streams[e].append(("op", fn, None))
        self._commit(tok, reads, writes)
        return tok

    def dma(self, q, out_ap, in_ap, reads=(), writes=(), **kw):
        reads = self._rl(reads)
        writes = self._rl(writes)
        self._waits(q, reads, writes)
        i = self.dma_rr[q]
        self.dma_rr[q] = (i + 1) % len(self.dma_sems[q])
        sem = self.dma_sems[q][i]
        prev = self.dma_tot[q][i]
        if prev > 0 and self.seen[q].get(id(sem), 0) < prev:
            self.seen[q][id(sem)] = prev
            self.streams[q].append(("wait", sem, prev))
        self.dma_tot[q][i] = prev + 16
        tok = ("dma", sem, prev + 16)
        self.streams[q].append(("dma", out_ap, in_ap, sem, kw))
        self._commit(tok, reads, writes)
        return tok

    def barrier(self):
        toks = []
        for e in COMPUTE:
            if self.cnt[e] > 0:
                toks.append((self.sem[e], self.cnt[e]))
        for q in self.dma_sems:
            for sem, tot in zip(self.dma_sems[q], self.dma_tot[q]):
                if tot > 0:
                    toks.append((sem, tot))
        for e in self.streams:
            for sem, val in toks:
                if e in COMPUTE and sem is self.sem[e]:
                    continue
                if self.seen[e].get(id(sem), 0) < val:
                    self.seen[e][id(sem)] = val
                    self.streams[e].append(("wait", sem, val))

    def flush(self):
        self.barrier()
        nc = self.nc
        streams = self.streams

        def replay(eng, items):
            for it in items:
                if it[0] == "wait":
                    eng.wait_ge(it[1], it[2])
                elif it[0] == "op":
                    ins = getattr(eng, it[1][0])(*it[1][1], **it[1][2])
                    if it[2] is not None:
                        ins.then_inc(it[2], 1)
                else:
                    _, o, i, sem, kw = it
                    eng.dma_start(out=o, in_=i, **kw).then_inc(sem, 16)

        with nc.Block() as block:
            @block.sync
            def _(sync):
                replay(sync, streams["sp"])

            @block.tensor
            def _(t):
                replay(t, streams["pe"])

            @block.scalar
            def _(s):
                replay(s, streams["act"])

            @block.vector
            def _(v):
                replay(v, streams["dve"])

            @block.gpsimd
            def _(g):
                replay(g, streams["pool"])
        for e in streams:
            streams[e] = []

    def finish(self):
        self.flush()
        while self.scopes:
            self.scopes.pop().close()


def ssl(start, n, step):
    return slice(start, start + (n - 1) * step + 1, step)


class _Stop(Exception):
    pass


STOP_AFTER = None


def build():
    nc = bass.Bass("TRN2", target_bir_lowering=False)
    state = {}
    try:
        _build_inner(nc, state)
    except _Stop:
        state["P"].finish()
    return nc


def _build_inner(nc, state):
    I, O = {}, {}

    def din(n, shape):
        I[n] = nc.dram_tensor(n, list(shape), F32, kind="ExternalInput").ap()

    def dout(n, shape):
        O[n] = nc.dram_tensor(n, list(shape), F32, kind="ExternalOutput").ap()

    din("x2", [2 * NT, D]); din("xs", [128, D]); din("po", [NT, 256]); din("psm", [128, 256])
    din("ck0", [4, 128, 512]); din("ck1", [4, 512, 512]); din("ck2", [4, 2048, 512])
    din("sconvT", [128, 4, 16, 3]); din("sconv", [4, 3, 2048])
    din("sC", [4, 4, 256, 256]); din("sn", [4, 4, 256]); din("smT", [4, 4])
    din("w_in_b", [33, 128, 8, 256]); din("w_gates", [128, 8, 8]); din("cw", [128, 16, 4]); din("cb", [128, 16])
    din("b_i", [4, 1]); din("b_f", [4, 1]); din("mhn_col", [128, 8])
    din("w_pa", [256, D]); din("w_pb", [D, D]); din("w_out", [D, D])
    din("g_mix", [1, D]); din("g_ffn", [1, D]); din("g_ple", [1, D]); din("g_fin", [1, D])
    din("wgu_b", [NJ, 128, 8, 256]); din("w_down", [DFF, D])
    din("w_pg", [D, D]); din("w_pp", [256, D])
    din("ident", [128, 128]); din("mprev", [128, 128]); din("mcur", [128, 128]); din("mbias", [128, 128])
    din("sel4", [4, 512]); din("rope", [128, 69, 16]); din("rope_s", [128, 16]); din("flag", [128, 1])

    dout("y", [NT, D]); dout("ys", [4, D])
    dout("kvp0", [128, 512]); dout("kvp1", [512, 512]); dout("kvp2", [2048, 512])
    dout("kvs0", [4, 128, 512]); dout("kvs1", [4, 512, 512]); dout("kvs2", [4, 2048, 512])
    dout("convp", [3, 2048]); dout("convs", [4, 3, 2048])
    dout("Cp", [4, 256, 256]); dout("np", [4, 256]); dout("mp", [4, 1])
    dout("Cs", [4, 4, 256, 256]); dout("ns", [4, 4, 256]); dout("ms", [4, 4])

    P = Prog(nc)
    state["P"] = P

    def phase_done(i):
        if STOP_AFTER is not None and i >= STOP_AFTER:
            raise _Stop()

    ps0 = P.ps("ps0", [128, 512], F32)
    ps1 = P.ps("ps1", [128, 512], F32)
    psA = P.ps("psA", [128, 1024], F32)
    ps4 = P.ps("ps4", [128, 512], F32)
    psTf = P.ps("psT", [128, 512], F32)
    psT = psTf[:, :].bitcast(BF16)
    ps6 = P.ps("ps6", [128, 512], F32)
    ps7 = P.ps("ps7", [128, 512], F32)
    for n in ("ps0", "ps1", "psA0", "psA1", "ps4", "psT", "ps6", "ps7"):
        P.R(n, True)
    banks = [(ps0, "ps0"), (ps1, "ps1"), (psA[:, 0:512], "psA0"), (psA[:, 512:1024], "psA1"),
             (ps4, "ps4"), (ps6, "ps6"), (ps7, "ps7"), (psTf, "psT")]

    identf = P.sb("identf", [128, 128], F32)
    ident = P.sb("ident", [128, 128], BF16)
    mask2 = P.sb("mask2", [128, 2, 128], BF16)
    mask2f = P.sb("mask2f", [128, 2, 128], BF16)
    mbias = P.sb("mbias", [128, 128], BF16)
    sel4 = P.sb("sel4", [4, 4, 128], F32)
    rope = P.sb("rope", [128, 69, 16], F32)
    rope_s = P.sb("rope_s", [128, 16], F32)
    flag = P.sb("flag", [128, 1], F32)
    ones = P.sb("ones", [128, 512], F32)
    C32 = P.sb("C32", [128, 4, 2, 257], F32)
    cbnd = P.sb("cbnd", [128, 4, 4, 3], F32)
    MPh = P.sb("MPh", [4, 1], F32)
    cw = P.sb("cw", [128, 16, 4], F32)
    cb = P.sb("cb", [128, 16], F32)
    b_i = P.sb("b_i", [4, 1], F32)
    nb_f = P.sb("nb_f", [4, 1], F32)
    wgt8 = P.sb("wgt8", [128, 8, 8], BF16)
    nln16 = P.sb("nln16", [4, 1], F32)

    P.push()
    mtmp = P.sb("mtmp", [128, 2, 128], F32)
    P.dma("sp", identf[:], I["ident"][:, :], writes=["identf"])
    P.dma("sp", mtmp[:, 0, :], I["mprev"][:, :], writes=["mtmp"])
    P.dma("sp", mtmp[:, 1, :], I["mcur"][:, :], writes=["mtmp"])
    P.dma("pool", mbias[:], I["mbias"][:, :], writes=["mbias"])
    P.dma("sp", sel4[:], I["sel4"].rearrange("k (h m) -> k h m", h=4), writes=["sel4"])
    P.dma("sp", rope[:], I["rope"][:, :, :], writes=["rope"])
    P.dma("sp", rope_s[:], I["rope_s"][:, :], writes=["rope_s"])
    P.dma("sp", flag[:], I["flag"][:, :], writes=["flag"])
    P.dma("sp", cw[:], I["cw"][:, :, :], writes=["cw"])
    P.dma("sp", cb[:], I["cb"][:, :], writes=["cb"])
    P.dma("sp", b_i[:], I["b_i"][:, :], writes=["b_i"])
    P.dma("sp", nb_f[:], I["b_f"][:, :], writes=["nb_f"])
    P.dma("pool", wgt8[:], I["w_gates"][:, :, :], writes=["wgt8"])
    P.op("dve", lambda e: e.tensor_copy(ident[:], identf[:]), reads=["identf"], writes=["ident"])
    P.op("dve", lambda e: e.tensor_copy(mask2[:], mtmp[:]), reads=["mtmp"], writes=["mask2"])
    P.op("dve", lambda e: e.tensor_copy(mask2f[:, 1, :], mtmp[:, 1, :]), reads=["mtmp"], writes=["mask2f"])
    P.op("dve", lambda e: e.tensor_scalar(mask2f[:, 0, :], mtmp[:, 0, :], flag[:, 0:1], None, ALU.mult),
         reads=["mtmp", "flag", "mask2f"], writes=["mask2f"])
    P.op("dve", lambda e: e.memset(ones[:], 1.0), writes=["ones"])
    P.op("dve", lambda e: e.memset(nln16[:], -2.7725887222), writes=["nln16"])
    P.op("dve", lambda e: e.tensor_scalar(nb_f[:], nb_f[:], -1.0, None, ALU.mult), reads=["nb_f"], writes=["nb_f"])
    P.op("dve", lambda e: e.memset(C32[:], 0.0), writes=["C32"])
    P.op("dve", lambda e: e.memset(cbnd[:], 0.0), writes=["cbnd"])
    P.pop()


    def load_w(dst, src_view, res):
        P.dma("pool", dst, src_view, writes=[res])

    def rms_rstd(src_ap, junk, ss, rs, srcres, tag):
        P.op("act", lambda e: e.activation(junk, src_ap, AF.Square, accum_out=ss), reads=srcres, writes=["junk" + tag, "ss" + tag])
        P.op("dve", lambda e: e.tensor_scalar(rs, ss, 1.0 / D, EPS, ALU.mult, ALU.add), reads=["ss" + tag], writes=["rs" + tag])
        P.op("act", lambda e: e.activation(rs, rs, AF.Sqrt), reads=["rs" + tag], writes=["rs" + tag])
        P.op("dve", lambda e: e.reciprocal(rs, rs), reads=["rs" + tag], writes=["rs" + tag])

    def run_pipeline(items):
        ns = max(len(t) for t in items)
        for n_ in range(len(items) + ns - 1):
            for k in range(ns):
                idx = n_ - k
                if 0 <= idx < len(items) and k < len(items[idx]):
                    items[idx][k]()

    def make_stage(with_xt=True):
        xt = [P.sb("xt%d" % i, [128, D], F32) for i in range(2)] if with_xt else None
        junk = [P.sb("junk%d" % i, [128, D], BF16) for i in range(2)]
        ss = [P.sb("ss%d" % i, [128, 1], F32) for i in range(2)]
        rs = [P.sb("rs%d" % i, [128, 1], F32) for i in range(2)]
        xnb = [P.sb("xnb%d" % i, [128, D], BF16) for i in range(2)]
        return xt, junk, ss, rs, xnb

    def norm_T(st, i, src, srcres, np_, gbc, gres, dstT, dst_res):
        _, junk, ss, rs, xnb = st
        t = "n%d" % i
        rms_rstd(src, junk[i][0:np_, :], ss[i][0:np_, :], rs[i][0:np_, :], [srcres], t)
        P.op("dve", lambda e: e.scalar_tensor_tensor(xnb[i][0:np_, :], src, rs[i][0:np_, 0:1], gbc[0:np_, :], ALU.mult, ALU.mult),
             reads=[srcres, "rs" + t, gres], writes=["xnb%d" % i])
        for k in range(8):
            P.op("pe", lambda e, k=k: e.transpose(psT[:, k * 128:k * 128 + np_], xnb[i][0:np_, k * 128:(k + 1) * 128], ident[0:np_, 0:np_]),
                 reads=["xnb%d" % i, "ident"], writes=["psT"], signal=(k == 7))
        P.op("act", lambda e: e.copy(dstT, psT[:, :].rearrange("p (k t) -> p k t", k=8)[:, :, 0:np_]),
             reads=["psT"], writes=[dst_res])

    P.push()
    xnT = P.sb("xnT", [128, 8, NT], BF16)
    xnSc = P.sb("xnSc", [128, 8, 4], BF16)
    mixS = P.sb("mixS", [128, 8, 4], BF16)
    P.push()
    oaT = P.sb("oaT", [64, 4, NT], BF16)
    oaTs = P.sb("oaTs", [64, 4, 4], BF16)
    P.push()
    KTh = P.sb("KTh", [128, 2, 21, 128], BF16)
    VAh = P.sb("VAh", [128, 21, 4, 65], BF16)
    P.op("dve", lambda e: e.memset(VAh[:], 1.0), writes=["VAh"])

    def phase_norm(row0, with_samples):
        P.push()
        gmix = P.sb("gmix", [128, D], F32)
        P.dma("sp", gmix[:], I["g_mix"][0, :].partition_broadcast(128), writes=["gmix"])
        st = make_stage()
        def nitem(blk):
            i = blk % 2
            _, junk, ss, rs, xnb = st
            t = "n%d" % i

            def n1():
                P.dma("sp", st[0][i][:, :], I["x2"][row0 + blk * 128:row0 + (blk + 1) * 128, :], writes=["xt%d" % i])
                rms_rstd(st[0][i][:, :], junk[i][:, :], ss[i][:, :], rs[i][:, :], ["xt%d" % i], t)
                P.op("dve", lambda e: e.scalar_tensor_tensor(xnb[i][:, :], st[0][i][:, :], rs[i][:, 0:1], gmix[:, :], ALU.mult, ALU.mult),
                     reads=["xt%d" % i, "rs" + t, "gmix"], writes=["xnb%d" % i])

            def n2():
                for k in range(8):
                    P.op("pe", lambda e, k=k: e.transpose(psT[:, k * 128:(k + 1) * 128], xnb[i][:, k * 128:(k + 1) * 128], ident[:, :]),
                         reads=["xnb%d" % i, "ident"], writes=["psT"], signal=(k == 7))
                P.op("act", lambda e: e.copy(xnT[:, :, blk * 128:(blk + 1) * 128], psT[:, :].rearrange("p (k t) -> p k t", k=8)),
                     reads=["psT"], writes=["xnT"])
            return (n1, n2)
        run_pipeline([nitem(blk) for blk in range(16)])
        if with_samples:
            P.dma("sp", st[0][0][0:4, :], I["xs"][0:4, :], writes=["xt0"])
            norm_T(st, 0, st[0][0][0:4, :], "xt0", 4, gmix, "gmix", xnSc[:, :, :], "xnSc")
        P.pop()

    phase_norm(0, True)
    def attn_scope():
        A = {}
        A["wg"] = P.sb("wg", [128, 8, 768], BF16)
        A["qkv32"] = [P.sb("qkv32_%d" % i, [128, 768], F32) for i in range(3)]
        A["qkb"] = [P.sb("qkb%d" % i, [128, 512], BF16) for i in range(2)]
        A["rt"] = [P.sb("rt%d" % i, [128, 4, 8, 8], F32) for i in range(2)]
        return A

    def load_wg(A, g):
        for j, off in enumerate((OFF_AQ, OFF_AK, OFF_AV)):
            load_w(A["wg"][:, :, j * 256:(j + 1) * 256], I["w_in_b"][g * 3 + j, :, :, :], "wg")

    def attn_p1(A, it, lhs_fn, lhs_res, VA_dst, VA_res):
        i3 = it % 3
        i = it % 2
        wg = A["wg"]
        qkv32 = A["qkv32"][i3]; qres = "qkv32_%d" % i3
        pqk, pqkr = (ps0, "ps0") if i == 0 else (ps6, "ps6")
        pvv, pvr = (ps1, "ps1") if i == 0 else (ps7, "ps7")
        for k in range(8):
            P.op("pe", lambda e, k=k: e.matmul(pqk[:, :], lhs_fn(k), wg[:, k, 0:512], start=(k == 0), stop=(k == 7)),
                 reads=[lhs_res, "wg"], writes=[pqkr], signal=(k == 7))
        for k in range(8):
            P.op("pe", lambda e, k=k: e.matmul(pvv[:, 0:256], lhs_fn(k), wg[:, k, 512:768], start=(k == 0), stop=(k == 7)),
                 reads=[lhs_res, "wg"], writes=[pvr], signal=(k == 7))
        P.op("act", lambda e: e.copy(qkv32[:, 0:512], pqk[:, :]), reads=[pqkr], writes=[qres])
        P.op("act", lambda e: e.copy(qkv32[:, 512:768], pvv[:, 0:256]), reads=[pvr], writes=[qres])
        P.op("act", lambda e: e.copy(VA_dst[:, :, 0:64], pvv[:, 0:256].rearrange("p (h d) -> p h d", d=64)), reads=[pvr], writes=[VA_res])

    def attn_p2(A, it, ropetab, roperes, kv_out=None, kv_rows=128):
        i3 = it % 3
        i = it % 2
        qkv32 = A["qkv32"][i3]; qres = "qkv32_%d" % i3
        qkb = A["qkb"][i]; bres = "qkb%d" % i
        rt = A["rt"][i]; rres = "rt%d" % i
        cosb = ropetab[:, 0:8].unsqueeze(1).broadcast_to([128, 8, 8])
        sinb = ropetab[:, 8:16].unsqueeze(1).broadcast_to([128, 8, 8])
        q3 = qkv32[:, 0:512].rearrange("p (h d) -> p h d", d=64)
        x1 = q3[:, :, 0:8]
        x2 = q3[:, :, 8:16]
        P.op("dve", lambda e: e.tensor_tensor(rt[:, 0], x1, cosb, ALU.mult), reads=[qres, roperes], writes=[rres])
        P.op("dve", lambda e: e.tensor_tensor(rt[:, 1], x2, sinb, ALU.mult), reads=[qres, roperes], writes=[rres])
        P.op("dve", lambda e: e.tensor_tensor(rt[:, 2], x2, cosb, ALU.mult), reads=[qres, roperes], writes=[rres])
        P.op("dve", lambda e: e.tensor_tensor(rt[:, 3], x1, sinb, ALU.mult), reads=[qres, roperes], writes=[rres])
        P.op("dve", lambda e: e.tensor_tensor(x1, rt[:, 0], rt[:, 1], ALU.subtract), reads=[rres, qres], writes=[qres])
        P.op("dve", lambda e: e.tensor_tensor(x2, rt[:, 2], rt[:, 3], ALU.add), reads=[rres, qres], writes=[qres])
        P.op("dve", lambda e: e.tensor_copy(qkb[:, :], qkv32[:, 0:512]), reads=[qres], writes=[bres])
        if kv_out is not None:
            P.dma("sp", kv_out, qkv32[0:kv_rows, 256:768], reads=[qres], writes=["kvout"])

    def attn_p3(A, it, KT_dst, KT_res, QT_dst=None, QT_res=None):
        i = it % 2
        qkb = A["qkb"][i]; bres = "qkb%d" % i
        for j in range(4):
            P.op("pe", lambda e, j=j: e.transpose(psT[:, j * 128:(j + 1) * 128], qkb[:, j * 128:(j + 1) * 128], ident[:, :]),
                 reads=[bres, "ident"], writes=["psT"], signal=(j == 3))
        pT3 = psT[:, 0:512].rearrange("p (j t) -> p j t", j=4)
        if QT_dst is not None:
            P.op("act", lambda e: e.copy(QT_dst, pT3[:, 0:2, :]), reads=["psT"], writes=[QT_res])
        P.op("act", lambda e: e.copy(KT_dst, pT3[:, 2:4, :]), reads=["psT"], writes=[KT_res])

    def attn_c1(it, QT, QT_res, KTp, KTp_res, KTc, KTc_res, PT):
        i = it % 2
        pt = PT[i]; pres = "PT%d" % i
        pS = psA[:, :].rearrange("p (a c b q) -> p a c b q", a=2, c=2, b=2)
        for h in range(4):
            hp, po = h // 2, (h % 2) * 64
            bank = "psA%d" % (h % 2)
            P.op("pe", lambda e, h=h, hp=hp, po=po: e.matmul(pS[:, h % 2, h // 2, 0, :], KTp[po:po + 64, hp, :], QT[po:po + 64, hp, :], start=True, stop=True),
                 reads=[KTp_res, QT_res], writes=[bank], signal=False)
            P.op("pe", lambda e, h=h, hp=hp, po=po: e.matmul(pS[:, h % 2, h // 2, 1, :], KTc[po:po + 64, hp, :], QT[po:po + 64, hp, :], start=True, stop=True),
                 reads=[KTc_res, QT_res], writes=[bank], signal=(h >= 2))
        P.op("act", lambda e: e.activation(pt[:, 0, :, :, :], pS[:, 0, :, :, :], AF.Exp, scale=0.125), reads=["psA0"], writes=[pres])
        P.op("act", lambda e: e.activation(pt[:, 1, :, :, :], pS[:, 1, :, :, :], AF.Exp, scale=0.125), reads=["psA1"], writes=[pres])

    def attn_c2(it, VAp, VAp_res, VAc, VAc_res, msk, PT, acc_dst, acc_res, first, ncol):
        i = it % 2
        pt = PT[i]; pres = "PT%d" % i
        mb = msk[:, :, :].unsqueeze(1).broadcast_to([128, 4, 2, 128])
        pt4 = pt[:, :, :, :, :].rearrange("p a c b q -> p (a c) b q")
        P.op("dve", lambda e: e.tensor_tensor(pt4, pt4, mb, ALU.mult), reads=[pres, "mask2", "mask2f"], writes=[pres])
        pO = ps4[0:65, :].rearrange("p (h q) -> p h q", h=4)
        for h in range(4):
            P.op("pe", lambda e, h=h: e.matmul(pO[:, h, :], VAp[:, h, :], pt[:, h % 2, h // 2, 0, :], start=True, stop=False),
                 reads=[VAp_res, pres], writes=["ps4"], signal=False)
            P.op("pe", lambda e, h=h: e.matmul(pO[:, h, :], VAc[:, h, :], pt[:, h % 2, h // 2, 1, :], start=False, stop=True),
                 reads=[VAc_res, pres], writes=["ps4"], signal=(h == 3))
        if first:
            P.op("act", lambda e: e.copy(acc_dst, pO[:, :, 0:ncol]), reads=["ps4"], writes=[acc_res])
        else:
            P.op("dve", lambda e: e.tensor_tensor(acc_dst, acc_dst, pO[:, :, 0:ncol], ALU.add), reads=["ps4", acc_res], writes=[acc_res])

    P.push()
    A = attn_scope()

    def hist_item(g, r, it):
        d = DILS[g]
        hb = HB[g] + r
        start = NT - 128 * d + r

        def s1():
            if r == 0:
                load_wg(A, g)
            attn_p1(A, it, lambda k: xnT[:, k, ssl(start, 128, d)], "xnT", VAh[:, hb, :, :], "VAh")

        def s2():
            attn_p2(A, it, rope[:, RB[g] + r, :], "rope")

        def s3():
            attn_p3(A, it, KTh[:, :, hb, :], "KTh%d" % hb)
        return (s1, s2, s3)

    items = []
    it = 0
    for g in range(3):
        for r in range(DILS[g]):
            items.append(hist_item(g, r, it))
            it += 1
    run_pipeline(items)
    P.pop()
    phase_done(2)


    def mlstm_scope(nt, full):
        nch = nt // 128
        M = {"nt": nt, "nch": nch, "full": full}
        M["wm"] = P.sb("wm", [128, 8, 1024], BF16)
        for n in ("IG", "LF", "Bc", "bb", "Aa", "Mr", "WK", "tmpr"):
            M[n] = P.sb(n, [4, 512], F32)
        M["MP"] = P.sb("MP", [4, nch + 1], F32)
        M["MN"] = P.sb("MN", [4, nch], F32)
        M["DEC"] = P.sb("DEC", [4, nch], F32)
        M["COL"] = P.sb("COL", [128, nch, 3, 4], F32)
        M["DECB"] = P.sb("DECB", [128, 4, nch], F32)
        nloc = min(nt, 1024)
        M["nloc"] = nloc
        M["kT"] = P.sb("kT", [128, 2, nloc], BF16)
        nfc = 4 if full else 2
        M["xpre"] = P.sb("xpre", [128, nfc, 515], F32)
        M["bnd"] = P.sb("bnd", [128, nfc, 3], F32)
        M["cacc"] = [P.sb("cacc%d" % i, [128, 512], F32) for i in range(2)]
        M["ktmp"] = P.sb("ktmp", [128, 512], F32)
        M["Vaug"] = P.sb("Vaug", [128, nloc // 128, 257], BF16)
        M["kwa"] = P.sb("kwa", [128, 8, 256], BF16)
        if full:
            M["EM"] = P.sb("EM", [4, 512], F32)
            M["NEGM"] = P.sb("NEGM", [4, nt], F32)
            M["WI"] = P.sb("WI", [4, nt], F32)
            M["Cbf"] = [P.sb("Cbf%d" % i, [128, 2, 257], BF16) for i in range(2)]
            M["wTa"] = P.sb("wTa", [128, 8, 128], BF16)
            M["qpTa"] = P.sb("qpTa", [128, 8, 2, 128], BF16)
            if nt == 512:
                M["Csb"] = [P.sb("Csb%d" % i, [128, 2, 257], F32) for i in range(2)]
            M["qT"] = P.sb("qT", [128, 2, nloc], BF16)
            M["G"] = P.sb("G", [128, nloc // 128, 256], BF16)
            M["mhn"] = P.sb("mhncol", [128, 8], F32)
            M["Dt"] = [P.sb("Dt%d" % i, [128, 128], F32) for i in range(2)]
            M["sm"] = [P.sb("sm%d" % i, [128, 8], F32) for i in range(2)]
            M["jk"] = [P.sb("jk%d" % i, [128, 256], BF16) for i in range(2)]
            M["ob"] = [P.sb("ob%d" % i, [128, 256], BF16) for i in range(2)]
            P.dma("sp", M["mhn"][:], I["mhn_col"][:, :], writes=["mhncol"])
        P.op("dve", lambda e: e.memset(M["Vaug"][:], 1.0), writes=["Vaug"])
        return M

    def mlstm_gates(M, Xf, Xres, mode):
        nt, nch, full = M["nt"], M["nch"], M["full"]
        IG, LF, Bc, bb, Aa, Mr, WK, tmpr = M["IG"], M["LF"], M["Bc"], M["bb"], M["Aa"], M["Mr"], M["WK"], M["tmpr"]
        MP, MN, DEC = M["MP"], M["MN"], M["DEC"]
        samp = (mode == "samp")
        B3 = Bc[:, :].rearrange("p (c t) -> p c t", t=128)
        b3 = bb[:, :].rearrange("p (c t) -> p c t", t=128)
        A3 = Aa[:, :].rearrange("p (c t) -> p c t", t=128)
        M3 = Mr[:, :].rearrange("p (c t) -> p c t", t=128)
        t3 = tmpr[:, :].rearrange("p (c t) -> p c t", t=128)
        for tt in range(nt // 512):
            c0 = tt * 4
            cs = slice(tt * 512, (tt + 1) * 512)
            ncol = 4 if samp else 512
            for k in range(8):
                P.op("pe", lambda e, k=k: e.matmul(ps0[0:4, 0:ncol], wgt8[:, k, 0:4], Xf(k, cs), start=(k == 0), stop=(k == 7)),
                     reads=["wgt8", Xres], writes=["ps0"], signal=(k == 7))
            for k in range(8):
                P.op("pe", lambda e, k=k: e.matmul(ps1[0:4, 0:ncol], wgt8[:, k, 4:8], Xf(k, cs), start=(k == 0), stop=(k == 7)),
                     reads=["wgt8", Xres], writes=["ps1"], signal=(k == 7))
            if samp:
                P.op("dve", lambda e: e.memset(IG[:], NEGPAD), writes=["IG"])
                P.op("dve", lambda e: e.memset(LF[:], 0.0), writes=["LF"])
                P.op("act", lambda e: e.activation(IG[:, 0:512:128], ps0[0:4, 0:4], AF.Identity, bias=b_i[:, 0:1]), reads=["ps0", "b_i", "IG"], writes=["IG"])
                P.op("act", lambda e: e.activation(tmpr[:, 0:4], ps1[0:4, 0:4], AF.Exp, bias=nb_f[:, 0:1], scale=-1.0), reads=["ps1", "nb_f"], writes=["tmpr"])
                P.op("act", lambda e: e.activation(tmpr[:, 0:4], tmpr[:, 0:4], AF.Ln, bias=1.0), reads=["tmpr"], writes=["tmpr"])
                P.op("dve", lambda e: e.tensor_scalar(LF[:, 0:512:128], tmpr[:, 0:4], -1.0, None, ALU.mult), reads=["tmpr", "LF"], writes=["LF"])
            else:
                P.op("act", lambda e: e.activation(IG[:, :], ps0[0:4, :], AF.Identity, bias=b_i[:, 0:1]), reads=["ps0", "b_i"], writes=["IG"])
                P.op("act", lambda e: e.activation(tmpr[:, :], ps1[0:4, :], AF.Exp, bias=nb_f[:, 0:1], scale=-1.0), reads=["ps1", "nb_f"], writes=["tmpr"])
                P.op("act", lambda e: e.activation(tmpr[:, :], tmpr[:, :], AF.Ln, bias=1.0), reads=["tmpr"], writes=["tmpr"])
                P.op("dve", lambda e: e.tensor_scalar(LF[:, :], tmpr[:, :], -1.0, None, ALU.mult), reads=["tmpr"], writes=["LF"])
            P.op("dve", lambda e: e.tensor_tensor_scan(Bc[:, :], ones[0:4, 0:512], LF[:, :], 0.0, ALU.mult, ALU.add),
                 reads=["LF", "ones"], writes=["Bc"])
            P.op("dve", lambda e: e.tensor_copy(b3[:, 0, :], B3[:, 0, :]), reads=["Bc"], writes=["bb"])
            P.op("dve", lambda e: e.tensor_tensor(b3[:, 1:4, :], B3[:, 1:4, :], B3[:, 0:3, 127:128].broadcast_to([4, 3, 128]), ALU.subtract),
                 reads=["Bc", "bb"], writes=["bb"])
            P.op("dve", lambda e: e.tensor_tensor(Aa[:, :], IG[:, :], bb[:, :], ALU.subtract), reads=["IG", "bb"], writes=["Aa"])
            for cl in range(4):
                c = c0 + cl
                P.op("dve", lambda e, c=c, cl=cl: e.tensor_tensor_scan(M3[:, cl, :], A3[:, cl, :], A3[:, cl, :], MP[:, c:c + 1], ALU.max, ALU.max),
                     reads=["Aa", "MP", "Mr"], writes=["Mr"])
                dst = MN[:, c:c + 1] if samp else MP[:, c + 1:c + 2]
                P.op("dve", lambda e, cl=cl, dst=dst: e.tensor_tensor(dst, b3[:, cl, 127:128], M3[:, cl, 127:128], ALU.add),
                     reads=["bb", "Mr", "MP", "MN"], writes=["MN" if samp else "MP"])
            if not samp:
                P.op("dve", lambda e, c0=c0: e.tensor_copy(MN[:, c0:c0 + 4], MP[:, c0 + 1:c0 + 5]), reads=["MP", "MN"], writes=["MN"])
            mpb = MP[:, c0:c0 + 4].unsqueeze(2).broadcast_to([4, 4, 128])
            mlb = M3[:, :, 127:128].broadcast_to([4, 4, 128])
            if full:
                P.op("dve", lambda e, cs=cs: e.tensor_scalar(M["NEGM"][:, cs], Mr[:, :], -1.0, None, ALU.mult), reads=["Mr", "NEGM"], writes=["NEGM"])
                P.op("dve", lambda e: e.tensor_tensor(t3, mpb, M3, ALU.subtract), reads=["MP", "Mr", "tmpr"], writes=["tmpr"])
                P.op("act", lambda e, cs=cs: e.activation(M["WI"][:, cs], tmpr[:, :], AF.Exp), reads=["tmpr", "WI"], writes=["WI"])
                P.op("dve", lambda e: e.tensor_tensor(tmpr[:, :], bb[:, :], Mr[:, :], ALU.add), reads=["bb", "Mr", "tmpr"], writes=["tmpr"])
                P.op("act", lambda e: e.activation(M["EM"][:, :], tmpr[:, :], AF.Exp, scale=-2.0), reads=["tmpr"], writes=["EM"])
            P.op("dve", lambda e: e.tensor_tensor(t3, A3, mlb, ALU.subtract), reads=["Aa", "Mr", "tmpr"], writes=["tmpr"])
            P.op("act", lambda e: e.activation(WK[:, :], tmpr[:, :], AF.Exp, bias=nln16[:, 0:1]), reads=["tmpr", "nln16"], writes=["WK"])
            P.op("dve", lambda e, c0=c0: e.tensor_tensor(DEC[:, c0:c0 + 4], MP[:, c0:c0 + 4], M3[:, :, 127], ALU.subtract), reads=["MP", "Mr", "DEC"], writes=["DEC"])
            P.op("act", lambda e, c0=c0: e.activation(DEC[:, c0:c0 + 4], DEC[:, c0:c0 + 4], AF.Exp), reads=["DEC"], writes=["DEC"])
            pc = ps0[:, 0:48].rearrange("p (c j h) -> p c j h", j=3, h=4)
            srcs = (Aa, WK, M["EM"]) if full else (Aa, WK)
            nj = len(srcs)
            n = 0
            for cl in range(4):
                for j in range(nj):
                    n += 1
                    P.op("pe", lambda e, cl=cl, j=j: e.transpose(pc[:, cl, j, :], srcs[j][0:4, cl * 128:(cl + 1) * 128], identf[0:4, 0:4]),
                         reads=["Aa", "WK", "EM", "identf"], writes=["ps0"], signal=(n == 4 * nj))
            P.op("act", lambda e, c0=c0: e.copy(M["COL"][:, c0:c0 + 4, 0:nj, :], pc[:, :, 0:nj, :]), reads=["ps0"], writes=["COL"])
            P.op("dve", lambda e, c0=c0: e.tensor_scalar(M["COL"][:, c0:c0 + 4, 0, :], M["COL"][:, c0:c0 + 4, 0, :], -2.7725887222, None, ALU.add), reads=["COL"], writes=["COL"])
            pd = ps1[:, 0:16].rearrange("p (h c) -> p h c", h=4)
            for h in range(4):
                P.op("pe", lambda e, h=h, c0=c0: e.matmul(pd[:, h, :], sel4[0:4, h, :], DEC[0:4, c0:c0 + 4], start=True, stop=True),
                     reads=["sel4", "DEC"], writes=["ps1"], signal=(h == 3))
            P.op("act", lambda e, c0=c0: e.copy(M["DECB"][:, :, c0:c0 + 4], pd), reads=["ps1"], writes=["DECB"])

    def mlstm_head(M, h, Xcols, Xres, mode, obT_dst=None, obT_res=None):
        nt, nch, full, nloc = M["nt"], M["nch"], M["full"], M["nloc"]
        samp = mode == "samp"
        wm = M["wm"]
        kT = M["kT"]; xpre = M["xpre"]; bnd = M["bnd"]; Vaug = M["Vaug"]; COL = M["COL"]; DECB = M["DECB"]
        nfc = 4 if full else 2
        W = 128 if samp else 512
        C3 = C32[:, h, :, :]
        C3 = C32[:, h, :, :]
        if mode == "hist":
            for fc in (0, 1):
                for k in range(8):
                    P.op("pe", lambda e, k=k, fc=fc: e.matmul(ps0[:, 0:128], wm[:, k, fc * 128:(fc + 1) * 128], Xcols(k, nt - 128, 128), start=(k == 0), stop=(k == 7)),
                         reads=["wm_q", Xres], writes=["ps0"], signal=(k == 7))
                P.op("act", lambda e, fc=fc: e.copy(cbnd[:, h, fc, :], ps0[:, 125:128]), reads=["ps0"], writes=["cbnd"])
        else:
            cst = M["ktmp"]
            npr = 4 if samp else 3
            lsel = (lambda k: xnSc[:, k, :]) if samp else (lambda k: Xcols(k, nt - 3, 3))
            for k in range(8):
                P.op("pe", lambda e, k=k: e.matmul(ps0[0:npr, :], lsel(k), wm[:, k, 0:512], start=(k == 0), stop=(k == 7)),
                     reads=["wm_q", "wm_k", Xres], writes=["ps0"], signal=(k == 7))
            P.op("act", lambda e: e.copy(cst[0:npr, :], ps0[0:npr, :]), reads=["ps0"], writes=["ktmp"])
            if samp:
                P.dma("sp", O["convs"][:, 2, h * 256:(h + 1) * 256], cst[0:4, 0:256], reads=["ktmp"], writes=["convsout"])
                P.dma("sp", O["convs"][:, 2, 1024 + h * 256:1024 + (h + 1) * 256], cst[0:4, 256:512], reads=["ktmp"], writes=["convsout"])
            else:
                P.dma("sp", O["convp"][:, h * 256:(h + 1) * 256], cst[0:3, 0:256], reads=["ktmp"], writes=["convpout"])
                P.dma("sp", O["convp"][:, 1024 + h * 256:1024 + (h + 1) * 256], cst[0:3, 256:512], reads=["ktmp"], writes=["convpout"])
        for hf in range(nt // nloc):
            def b_chunks(cls):
                for cl in cls:
                    c = hf * (nloc // 128) + cl
                    pz, pzr = (ps0, "ps0") if cl % 2 == 0 else (ps1, "ps1")
                    ncols = 512 if full else 256
                    for k in range(8):
                        P.op("pe", lambda e, k=k, c=c, pz=pz: e.matmul(pz[:, 0:ncols], Xcols(k, c * 128, 128), wm[:, k, 512:512 + ncols], start=(k == 0), stop=(k == 7)),
                             reads=["wm_v", "wm_o", Xres], writes=[pzr], signal=(k == 7))
                    P.op("act", lambda e, cl=cl, pz=pz: e.copy(Vaug[:, cl, 0:256], pz[:, 0:256]), reads=[pzr], writes=["Vaug"])
                    if full:
                        P.op("act", lambda e, pz=pz, cl=cl: e.activation(M["G"][:, cl, :], pz[:, 256:512], AF.Sigmoid), reads=[pzr], writes=["G"])

            for tl in range(nloc // W):
                tt = hf * (nloc // W) + tl
                t0 = tt * W
                l0 = tl * W
                for fi in range(nfc):
                    fc = fi + (4 - nfc)
                    chunk = (fc // 2) * 8 + h * 2 + fc % 2
                    pz, pzr = (ps0, "ps0") if fi % 2 == 0 else (ps1, "ps1")
                    xp = xpre[:, fi, :]
                    xr = "xpre%d" % fi
                    wres = "wm_q" if fc < 2 else "wm_k"
                    if samp:
                        P.op("dve", lambda e, xp=xp, chunk=chunk, tt=tt: e.tensor_copy(xp[:, 0:3], M["scT"][:, tt, chunk, :]), reads=["scT"], writes=[xr])
                    elif tt > 0:
                        P.op("dve", lambda e, fi=fi, xp=xp: e.tensor_copy(bnd[:, fi, :], xp[:, 512:515]), reads=[xr], writes=["bnd%d" % fi])
                        P.op("dve", lambda e, fi=fi, xp=xp: e.tensor_copy(xp[:, 0:3], bnd[:, fi, :]), reads=["bnd%d" % fi], writes=[xr])
                    elif mode == "hist":
                        P.op("dve", lambda e, xp=xp: e.memset(xp[:, 0:3], 0.0), writes=[xr])
                    else:
                        P.op("dve", lambda e, xp=xp, fc=fc: e.tensor_scalar(xp[:, 0:3], cbnd[:, h, fc, :], flag[:, 0:1], None, ALU.mult),
                             reads=["cbnd", "flag"], writes=[xr])
                    for k in range(8):
                        P.op("pe", lambda e, k=k, pz=pz, fc=fc, t0=t0: e.matmul(pz[:, 0:W], wm[:, k, fc * 128:(fc + 1) * 128], Xcols(k, t0, W), start=(k == 0), stop=(k == 7)),
                             reads=[wres, Xres], writes=[pzr], signal=(k == 7))
                    P.op("act", lambda e, pz=pz, xp=xp: e.copy(xp[:, 3:3 + W], pz[:, 0:W]), reads=[pzr], writes=[xr])
                b_chunks(range(tl * (W // 128), (tl + 1) * (W // 128)))
                for fi in range(nfc):
                    fc = fi + (4 - nfc)
                    chunk = (fc // 2) * 8 + h * 2 + fc % 2
                    xp = xpre[:, fi, :]
                    xr = "xpre%d" % fi
                    ca = M["cacc"][fi % 2]; car = "cacc%d" % (fi % 2)
                    P.op("dve", lambda e, xp=xp, chunk=chunk, ca=ca: e.tensor_scalar(ca[:, 0:W], xp[:, 3:3 + W], cw[:, chunk, 3:4], cb[:, chunk:chunk + 1], ALU.mult, ALU.add),
                         reads=[xr, "cw", "cb"], writes=[car])
                    for j in (2, 1, 0):
                        P.op("dve", lambda e, xp=xp, chunk=chunk, j=j, ca=ca: e.scalar_tensor_tensor(ca[:, 0:W], xp[:, j:j + W], cw[:, chunk, j:j + 1], ca[:, 0:W], ALU.mult, ALU.add),
                             reads=[xr, "cw", car], writes=[car])
                    if fc < 2:
                        P.op("act", lambda e, fc=fc, l0=l0, ca=ca: e.activation(M["qT"][:, fc, l0:l0 + W], ca[:, 0:W], AF.Silu), reads=[car], writes=["qT"])
                    else:
                        P.op("act", lambda e, ca=ca, fc=fc, l0=l0: e.activation(kT[:, fc - 2, l0:l0 + W], ca[:, 0:W], AF.Silu), reads=[car], writes=["kT"])
                    if mode == "hist" and t0 + W == nt:
                        P.op("dve", lambda e, xp=xp, fc=fc: e.tensor_copy(cbnd[:, h, fc, :], xp[:, 512:515]), reads=[xr], writes=["cbnd"])
            nlc = nloc // 128
            for cl in range(nlc):
                c = hf * nlc + cl
                ls = slice(cl * 128, (cl + 1) * 128)
                gs = slice(c * 128, (c + 1) * 128)
                par = cl % 2
                if full:
                    qT = M["qT"]
                    pS, pSr = (psA[:, 0:128], "psA0") if par == 0 else (ps6[:, 0:128], "ps6")
                    pB, pBr = (psA[:, 512:640], "psA1") if par == 0 else (ps7[:, 0:128], "ps7")
                    pW, pWr = (ps0[:, 0:128], "ps0") if par == 0 else (ps1[:, 0:128], "ps1")
                    Dt = M["Dt"][par]
                    for k2 in range(2):
                        P.op("pe", lambda e, k2=k2: e.matmul(pS, kT[:, k2, ls], qT[:, k2, ls], start=(k2 == 0), stop=(k2 == 1)),
                             reads=["kT", "qT"], writes=[pSr], signal=(k2 == 1))
                    P.op("pe", lambda e: e.matmul(pB, sel4[0:4, h, :], M["NEGM"][0:4, gs], start=True, stop=False),
                         reads=["sel4", "NEGM"], writes=[pBr], signal=False)
                    P.op("pe", lambda e: e.matmul(pB, ident[:, :], mbias[:, :], start=False, stop=True),
                         reads=["ident", "mbias"], writes=[pBr])
                    P.op("pe", lambda e: e.matmul(pW, sel4[0:4, h, :], M["WI"][0:4, gs], start=True, stop=True),
                         reads=["sel4", "WI"], writes=[pWr])
                    P.op("act", lambda e, c=c: e.activation(Dt[:, :], pB, AF.Exp, bias=COL[:, c, 0, h:h + 1]), reads=[pBr, "COL"], writes=["Dt%d" % par])
                    P.op("dve", lambda e, cl=cl: e.tensor_tensor(M["wTa"][:, cl, :], Dt[:, :], pS, ALU.mult), reads=["Dt%d" % par, pSr], writes=["wT%d" % cl])
                    for k2 in range(2):
                        P.op("dve", lambda e, k2=k2, cl=cl: e.tensor_tensor(M["qpTa"][:, cl, k2, :], qT[:, k2, ls], pW, ALU.mult), reads=["qT", pWr], writes=["qpT%d" % cl])
                for k2 in range(2):
                    P.op("pe", lambda e, k2=k2: e.transpose(psT[:, par * 256 + k2 * 128:par * 256 + (k2 + 1) * 128], kT[:, k2, ls], ident[:, :]),
                         reads=["kT", "ident"], writes=["psT"], signal=(k2 == 1))
                P.op("act", lambda e, c=c, cl=cl: e.activation(M["kwa"][:, cl, :], psT[:, par * 256:(par + 1) * 256], AF.Copy, scale=COL[:, c, 1, h:h + 1]),
                     reads=["psT", "COL"], writes=["kw%d" % cl])
            def stageA(cl):
                c = hf * nlc + cl
                i = cl % 2
                if samp:
                    C3 = M["Csb"][i][:, :, :]
                    C3r = "Csb%d" % i
                    P.dma("sp", C3[:, :, 0:256], I["sC"][c, h, :, :].rearrange("(k p) v -> p k v", p=128), writes=[C3r])
                    P.dma("sp", C3[:, :, 256:257], I["sn"][c, h, :].rearrange("(k p o) -> p k o", p=128, o=1), writes=[C3r], allow_slow_non_contiguous=True)
                else:
                    C3 = C32[:, h, :, :]
                    C3r = "C32"
                pCs = ((ps6, "ps6"), (ps7, "ps7")) if i == 0 else ((ps0, "ps0"), (ps1, "ps1"))
                for k2, (pC, pCr) in enumerate(pCs):
                    P.op("pe", lambda e, k2=k2, pC=pC: e.matmul(pC[:, 0:257], M["kwa"][:, cl, k2 * 128:(k2 + 1) * 128], Vaug[:, cl, :], start=True, stop=True),
                         reads=["kw%d" % cl, "Vaug"], writes=[pCr])
                if full:
                    Cbf = M["Cbf"][i]; Cbr = "Cbf%d" % i
                    pN, pNr = (ps4[:, 0:257], "ps4") if i == 0 else (psA[:, 0:257], "psA0")
                    P.op("act", lambda e: e.copy(Cbf[:, :, :], C3), reads=[C3r], writes=[Cbr])
                    P.op("pe", lambda e: e.matmul(pN, M["wTa"][:, cl, :], Vaug[:, cl, :], start=True, stop=False), reads=["wT%d" % cl, "Vaug"], writes=[pNr], signal=False)
                    P.op("pe", lambda e: e.matmul(pN, M["qpTa"][:, cl, 0, :], Cbf[:, 0, :], start=False, stop=False), reads=["qpT%d" % cl, Cbr], writes=[pNr], signal=False)
                    P.op("pe", lambda e: e.matmul(pN, M["qpTa"][:, cl, 1, :], Cbf[:, 1, :], start=False, stop=True), reads=["qpT%d" % cl, Cbr], writes=[pNr])
                for k2, (pC, pCr) in enumerate(pCs):
                    P.op("dve", lambda e, k2=k2, pC=pC: e.scalar_tensor_tensor(C3[:, k2, :], C3[:, k2, :], DECB[:, h, c:c + 1], pC[:, 0:257], ALU.mult, ALU.add),
                         reads=[C3r, "DECB", pCr], writes=[C3r])
                if samp:
                    P.dma("sp", O["Cs"][c, h, :, :].rearrange("(k p) v -> p k v", p=128), C3[:, :, 0:256], reads=[C3r], writes=["Csout"])
                    P.dma("sp", O["ns"][c, h, :].rearrange("(k p o) -> p k o", p=128, o=1), C3[:, :, 256:257], reads=[C3r], writes=["nsout"], allow_slow_non_contiguous=True)

            def stageB(cl):
                c = hf * nlc + cl
                i = cl % 2
                sm = M["sm"][i]; jk = M["jk"][i]; ob = M["ob"][i]
                smr = "sm%d" % i
                pN, pNr = (ps4[:, 0:257], "ps4") if i == 0 else (psA[:, 0:257], "psA0")
                P.op("act", lambda e: e.activation(jk[:, :], pN[:, 0:256], AF.Square, scale=1.0 / 16.0, accum_out=sm[:, 0:1]),
                     reads=[pNr], writes=["jk%d" % i, smr])
                P.op("dve", lambda e: e.scalar_tensor_tensor(sm[:, 1:2], pN[:, 256:257], pN[:, 256:257], COL[:, c, 2, h:h + 1], ALU.mult, ALU.max),
                     reads=[pNr, "COL", smr], writes=[smr])
                P.op("dve", lambda e: e.scalar_tensor_tensor(sm[:, 2:3], sm[:, 1:2], EPS, sm[:, 0:1], ALU.mult, ALU.add), reads=[smr], writes=[smr])
                P.op("act", lambda e: e.activation(sm[:, 2:3], sm[:, 2:3], AF.Sqrt), reads=[smr], writes=[smr])
                P.op("dve", lambda e: e.reciprocal(sm[:, 3:4], sm[:, 2:3]), reads=[smr], writes=[smr])
                P.op("dve", lambda e: e.scalar_tensor_tensor(ob[:, :], pN[:, 0:256], sm[:, 3:4], M["G"][:, cl, :], ALU.mult, ALU.mult),
                     reads=[pNr, smr, "G"], writes=["ob%d" % i])

            def stageC(cl):
                c = hf * nlc + cl
                i = cl % 2
                ob = M["ob"][i]
                gs = slice(c * 128, (c + 1) * 128)
                for j in range(2):
                    P.op("pe", lambda e, j=j: e.transpose(psT[:, 512 + i * 256 + j * 128:512 + i * 256 + (j + 1) * 128], ob[:, j * 128:(j + 1) * 128], ident[:, :]),
                         reads=["ob%d" % i, "ident"], writes=["psT"], signal=(j == 1))
                pob = psT[:, 512 + i * 256:512 + (i + 1) * 256].rearrange("p (j t) -> p j t", j=2)
                for j in range(2):
                    mcol = M["mhn"][:, h * 2 + j:h * 2 + j + 1]
                    if samp:
                        P.op("act", lambda e, j=j, mcol=mcol: e.activation(obT_dst[:, h * 2 + j, c:c + 1], pob[:, j, 0:1], AF.Copy, scale=mcol), reads=["psT", "mhncol"], writes=[obT_res])
                    else:
                        P.op("act", lambda e, j=j, mcol=mcol: e.activation(obT_dst[:, h * 2 + j, gs], pob[:, j, :], AF.Copy, scale=mcol), reads=["psT", "mhncol"], writes=[obT_res])

            for step in range(nlc + 2):
                if step < nlc:
                    stageA(step)
                if full and 0 <= step - 1 < nlc:
                    stageB(step - 1)
                if full and 0 <= step - 2 < nlc:
                    stageC(step - 2)
        if mode == "hist":
            P.op("dve", lambda e: e.tensor_scalar(C3, C3, flag[:, 0:1], None, ALU.mult), reads=["C32", "flag"], writes=["C32"])
        elif mode == "own":
            P.dma("sp", O["Cp"][h, :, :].rearrange("(k p) v -> p k v", p=128), C3[:, :, 0:256], reads=["C32"], writes=["Cpout"])
            P.dma("sp", O["np"][h, :].rearrange("(k p o) -> p k o", p=128, o=1), C3[:, :, 256:257], reads=["C32"], writes=["npout"], allow_slow_non_contiguous=True)

    def load_wm(M, h):
        for j, off in enumerate((OFF_MQ, OFF_MK, OFF_MV, OFF_MO)):
            load_w(M["wm"][:, :, j * 256:(j + 1) * 256], I["w_in_b"][9 + h * 4 + j, :, :, :], ("wm_q", "wm_k", "wm_v", "wm_o")[j])

    xn_cols = lambda k, s, n: xnT[:, k, s:s + n]

    P.push()
    M = mlstm_scope(NT, False)
    P.op("dve", lambda e: e.memset(M["MP"][:, 0:1], 0.0), writes=["MP"])
    mlstm_gates(M, lambda k, cs: xnT[:, k, cs], "xnT", "hist")
    P.op("dve", lambda e: e.tensor_scalar(MPh[:, :], M["MP"][:, 16:17], flag[0:4, 0:1], None, ALU.mult), reads=["MP", "flag"], writes=["MPh"])
    for h in range(4):
        load_wm(M, h)
        mlstm_head(M, h, xn_cols, "xnT", "hist")
    P.pop()
    phase_done(3)

    phase_norm(NT, False)
    phase_done(4)

    P.push()
    A = attn_scope()
    KTo = P.sb("KTo", [128, 2, 16, 128], BF16)
    QTo = P.sb("QTo", [128, 2, 16, 128], BF16)
    VAo = P.sb("VAo", [128, 16, 4, 65], BF16)
    acc = P.sb("acc", [65, 4, NT], F32)
    accs = P.sb("accs", [65, 4, 4], F32)
    PT = [P.sb("PT%d" % i, [128, 2, 2, 2, 128], BF16) for i in range(2)]
    ckst = [P.sb("ckst%d" % i, [128, 512], F32) for i in range(2)]
    ckb = [P.sb("ckb%d" % i, [128, 256], BF16) for i in range(2)]
    KTs = [P.sb("KTs%d" % i, [128, 2, 128], BF16) for i in range(2)]
    VAs = [P.sb("VAs%d" % i, [128, 4, 65], BF16) for i in range(2)]
    KTn = [P.sb("KTn%d" % i, [128, 2, 128], BF16) for i in range(2)]
    QTn = [P.sb("QTn%d" % i, [128, 2, 128], BF16) for i in range(2)]
    VAn = [P.sb("VAn%d" % i, [128, 4, 65], BF16) for i in range(2)]
    xnSb = [P.sb("xnSb%d" % i, [128, 8, 128], BF16) for i in range(2)]
    P.op("dve", lambda e: e.memset(VAo[:], 1.0), writes=["VAo%d" % k for k in range(16)])
    for i in range(2):
        P.op("dve", lambda e, i=i: e.memset(VAs[i][:], 1.0), writes=["VAs%d" % i])
        P.op("dve", lambda e, i=i: e.memset(VAn[i][:], 1.0), writes=["VAn%d" % i])
        P.op("dve", lambda e, i=i: e.memset(xnSb[i][:], 0.0), writes=["xnSb%d" % i])
    def own_item(g, s, r, it):
        d = DILS[g]
        span = 128 * d
        nsp = NT // span
        ob_ = s * d + r
        start = s * span + r
        kv_out = O["kvp%d" % g][ssl(r, 128, d), :] if s == nsp - 1 else None
        if s == 0:
            hb = HB[g] + r
            KTp, KTpr, VAp, VApr, msk = KTh[:, :, hb, :], "KTh%d" % hb, VAh[:, hb, :, :], "VAh", mask2f
        else:
            pb_ = ob_ - d
            KTp, KTpr, VAp, VApr, msk = KTo[:, :, pb_, :], "KTo%d" % pb_, VAo[:, pb_, :, :], "VAo%d" % pb_, mask2

        def s1():
            if s == 0 and r == 0:
                load_wg(A, g)
            attn_p1(A, it, lambda k: xnT[:, k, ssl(start, 128, d)], "xnT", VAo[:, ob_, :, :], "VAo%d" % ob_)

        def s2():
            attn_p2(A, it, rope[:, RB[g] + d + ob_, :], "rope", kv_out=kv_out)

        def s3():
            attn_p3(A, it, KTo[:, :, ob_, :], "KTo%d" % ob_, QTo[:, :, ob_, :], "QTo%d" % ob_)

        def s4():
            attn_c1(it, QTo[:, :, ob_, :], "QTo%d" % ob_, KTp, KTpr, KTo[:, :, ob_, :], "KTo%d" % ob_, PT)

        def s5():
            attn_c2(it, VAp, VApr, VAo[:, ob_, :, :], "VAo%d" % ob_, msk, PT, acc[:, :, ssl(start, 128, d)], "acc", g == 0, 128)
        return (s1, s2, s3, s4, s5)

    def samp_seq(g, b, it, its):
        d = DILS[g]
        i = its % 2
        ck = I["ck%d" % g]
        kvs = O["kvs%d" % g]
        P.dma("sp", ckst[i][:, :], ck[b, ssl(0, 128, d), :], writes=["ckst%d" % i])
        P.op("act", lambda e: e.copy(VAs[i][:, :, 0:64], ckst[i][:, 256:512].rearrange("p (h d) -> p h d", d=64)), reads=["ckst%d" % i], writes=["VAs%d" % i])
        P.op("dve", lambda e: e.tensor_copy(ckb[i][:, :], ckst[i][:, 0:256]), reads=["ckst%d" % i], writes=["ckb%d" % i])
        for j in range(2):
            P.op("pe", lambda e, j=j: e.transpose(psT[:, 512 + j * 128:512 + (j + 1) * 128], ckb[i][:, j * 128:(j + 1) * 128], ident[:, :]),
                 reads=["ckb%d" % i, "ident"], writes=["psT"], signal=(j == 1))
        P.op("act", lambda e: e.copy(KTs[i][:, :, :], psT[:, 512:768].rearrange("p (j t) -> p j t", j=2)), reads=["psT"], writes=["KTs%d" % i])
        P.op("dve", lambda e: e.tensor_copy(xnSb[i][:, :, 0:1], xnSc[:, :, b:b + 1]), reads=["xnSc", "xnSb%d" % i], writes=["xnSb%d" % i])
        attn_p1(A, it, lambda k: xnSb[i][:, k, :], "xnSb%d" % i, VAn[i][:, :, :], "VAn%d" % i)
        attn_p2(A, it, rope_s[:, :], "rope_s", kv_out=kvs[b, WINS[g] - 1:WINS[g], :], kv_rows=1)
        attn_p3(A, it, KTn[i][:, :, :], "KTn%d" % i, QTn[i][:, :, :], "QTn%d" % i)
        attn_c1(it, QTn[i][:, :, :], "QTn%d" % i, KTs[i][:, :, :], "KTs%d" % i, KTn[i][:, :, :], "KTn%d" % i, PT)
        attn_c2(it, VAs[i][:, :, :], "VAs%d" % i, VAn[i][:, :, :], "VAn%d" % i, mask2, PT, accs[:, :, b:b + 1], "accs", g == 0, 1)

    it = 0
    its = 0
    for g in range(3):
        d = DILS[g]
        items = []
        for s in range(NT // (128 * d)):
            for r in range(d):
                items.append(own_item(g, s, r, it))
                it += 1
        run_pipeline(items)
        if g == 0:
            for g_ in range(3):
                W_ = WINS[g_]
                for b_ in range(4):
                    P.dma("sp", O["kvs%d" % g_][b_, 0:W_ - 1, :], I["ck%d" % g_][b_, 1:W_, :], writes=["kvscopy"])
            for b_ in range(4):
                P.dma("sp", O["convs"][b_, 0:2, :], I["sconv"][b_, 1:3, :], writes=["convscopy"])
        for b in range(4):
            samp_seq(g, b, it, its)
            it += 1
            its += 1
    phase_done(4.5)
    rd = [P.sb("rd%d" % i, [64, 512], F32) for i in range(2)]
    n = 0
    for h in range(4):
        for tt in range(4):
            cs = slice(tt * 512, (tt + 1) * 512)
            i = n % 2
            pz, pzr = (ps0, "ps0") if i == 0 else (ps1, "ps1")
            P.op("pe", lambda e, h=h, cs=cs, pz=pz: e.matmul(pz[0:64, :], ones[64:65, 0:64], acc[64:65, h, cs], start=True, stop=True),
                 reads=["ones", "acc"], writes=[pzr])
            P.op("dve", lambda e, pz=pz, i=i: e.reciprocal(rd[i][:, :], pz[0:64, :]), reads=[pzr], writes=["rd%d" % i])
            P.op("dve", lambda e, h=h, cs=cs, i=i: e.tensor_tensor(oaT[:, h, cs], acc[0:64, h, cs], rd[i][:, :], ALU.mult), reads=["acc", "rd%d" % i], writes=["oaT"])
            n += 1
    for h in range(4):
        P.op("pe", lambda e, h=h: e.matmul(ps0[0:64, 0:4], ones[64:65, 0:64], accs[64:65, h, :], start=True, stop=True), reads=["ones", "accs"], writes=["ps0"])
        P.op("dve", lambda e: e.reciprocal(rd[0][:, 0:4], ps0[0:64, 0:4]), reads=["ps0"], writes=["rd0"])
        P.op("dve", lambda e, h=h: e.tensor_tensor(oaTs[:, h, :], accs[0:64, h, :], rd[0][:, 0:4], ALU.mult), reads=["accs", "rd0"], writes=["oaTs"])
    P.pop()
    P.pop()
    phase_done(5)

    P.push()
    obT = P.sb("obT", [128, 8, NT], BF16)
    obTs = P.sb("obTs", [128, 8, 4], BF16)
    P.push()
    M = mlstm_scope(NT, True)
    P.op("dve", lambda e: e.tensor_copy(M["MP"][:, 0:1], MPh[:, :]), reads=["MPh"], writes=["MP"])
    mlstm_gates(M, lambda k, cs: xnT[:, k, cs], "xnT", "own")
    P.dma("sp", O["mp"][:, :], M["MP"][:, 16:17], reads=["MP"], writes=["mpout"])
    for h in range(4):
        load_wm(M, h)
        mlstm_head(M, h, xn_cols, "xnT", "own", obT, "obT")
    P.pop()
    phase_done(6)
    P.push()
    M = mlstm_scope(512, True)
    xnS = P.sb("xnS", [128, 8, 512], BF16)
    P.op("dve", lambda e: e.memset(xnS[:], 0.0), writes=["xnS"])
    P.op("dve", lambda e: e.tensor_copy(xnS[:, :, 0:512:128], xnSc[:, :, :]), reads=["xnSc", "xnS"], writes=["xnS"])
    P.dma("sp", M["MP"][:, 0:4], I["smT"][:, :], writes=["MP"])
    M["scT"] = P.sb("scT", [128, 4, 16, 3], F32)
    P.dma("sp", M["scT"][:], I["sconvT"][:, :, :, :], writes=["scT"])
    mlstm_gates(M, lambda k, cs: xnSc[:, k, :], "xnSc", "samp")
    P.dma("sp", O["ms"][:, :], M["MN"][:, :], reads=["MN"], writes=["msout"])
    for h in range(4):
        load_wm(M, h)
        mlstm_head(M, h, lambda k, s, n: xnS[:, k, s:s + n], "xnS", "samp", obTs, "obTs")
    P.pop()
    phase_done(7)

    P.push()
    wga = P.sb("wga", [128, 8, 2048], BF16)
    wpb = P.sb("wpb", [128, 8, D], BF16)
    wpa = P.sb("wpa", [64, 4, D], BF16)
    mixst = P.sb("mixst", [128, 8, 512], BF16)
    sga = [P.sb("sga%d" % i, [128, 512], F32) for i in range(2)]
    sgb = [P.sb("sgb%d" % i, [128, 512], F32) for i in range(2)]
    t1 = [P.sb("t1_%d" % i, [128, 512], F32) for i in range(2)]
    for j in range(8):
        load_w(wga[:, :, j * 256:(j + 1) * 256], I["w_in_b"][25 + j, :, :, :], "wga")
    load_w(wpb[:, :, :], I["w_pb"].rearrange("(k p) c -> p k c", p=128), "wpb")
    load_w(wpa[:, :, :], I["w_pa"].rearrange("(h p) c -> p h c", p=64), "wpa")
    n = 0
    for tt in range(5):
        if tt < 4:
            N = 512
            xs_ = lambda k, tt=tt: xnT[:, k, tt * 512:(tt + 1) * 512]
            oas = lambda h, tt=tt: oaT[:, h, tt * 512:(tt + 1) * 512]
            obs = lambda k, tt=tt: obT[:, k, tt * 512:(tt + 1) * 512]
            rr = ["xnT", "oaT", "obT"]
        else:
            N = 4
            xs_ = lambda k: xnSc[:, k, :]
            oas = lambda h: oaTs[:, h, :]
            obs = lambda k: obTs[:, k, :]
            rr = ["xnSc", "oaTs", "obTs"]
        for c in range(8):
            i = n % 2
            for k in range(8):
                P.op("pe", lambda e, k=k, c=c: e.matmul(ps0[:, 0:N], wga[:, k, c * 128:(c + 1) * 128], xs_(k), start=(k == 0), stop=(k == 7)),
                     reads=["wga"] + rr, writes=["ps0"], signal=(k == 7))
            for k in range(8):
                P.op("pe", lambda e, k=k, c=c: e.matmul(ps1[:, 0:N], wga[:, k, 1024 + c * 128:1024 + (c + 1) * 128], xs_(k), start=(k == 0), stop=(k == 7)),
                     reads=["wga"] + rr, writes=["ps1"], signal=(k == 7))
            for h in range(4):
                P.op("pe", lambda e, h=h, c=c: e.matmul(ps6[:, 0:N], wpa[:, h, c * 128:(c + 1) * 128], oas(h), start=(h == 0), stop=(h == 3)),
                     reads=["wpa"] + rr, writes=["ps6"], signal=(h == 3))
            for k in range(8):
                P.op("pe", lambda e, k=k, c=c: e.matmul(ps7[:, 0:N], wpb[:, k, c * 128:(c + 1) * 128], obs(k), start=(k == 0), stop=(k == 7)),
                     reads=["wpb"] + rr, writes=["ps7"], signal=(k == 7))
            P.op("act", lambda e, i=i: e.activation(sga[i][:, 0:N], ps0[:, 0:N], AF.Sigmoid), reads=["ps0"], writes=["sga%d" % i])
            P.op("act", lambda e, i=i: e.activation(sgb[i][:, 0:N], ps1[:, 0:N], AF.Sigmoid), reads=["ps1"], writes=["sgb%d" % i])
            P.op("dve", lambda e, i=i: e.tensor_tensor(t1[i][:, 0:N], sga[i][:, 0:N], ps6[:, 0:N], ALU.mult), reads=["sga%d" % i, "ps6"], writes=["t1_%d" % i])
            P.op("dve", lambda e, i=i: e.tensor_tensor(sgb[i][:, 0:N], sgb[i][:, 0:N], ps7[:, 0:N], ALU.mult), reads=["sgb%d" % i, "ps7"], writes=["sgb%d" % i])
            if tt < 4:
                dst, dres = mixst[:, c, :], "mixst"
            else:
                dst, dres = mixS[:, c, :], "mixS"
            P.op("dve", lambda e, i=i, dst=dst: e.tensor_tensor(dst, t1[i][:, 0:N], sgb[i][:, 0:N], ALU.add), reads=["t1_%d" % i, "sgb%d" % i], writes=[dres])
            n += 1
        if tt < 4:
            P.op("act", lambda e, tt=tt: e.copy(xnT[:, :, tt * 512:(tt + 1) * 512], mixst[:, :, :]), reads=["mixst"], writes=["xnT"])
    P.pop()
    P.pop()
    P.pop()
    phase_done(8)

    P.push()
    mixT = xnT
    wout = P.sb("wout", [128, 8, D], BF16)
    wpg = P.sb("wpg", [128, 8, D], BF16)
    wpp = P.sb("wpp", [128, 2, D], BF16)
    gffn = P.sb("gffn", [128, D], F32)
    gple = P.sb("gple", [128, D], F32)
    gfin = P.sb("gfin", [128, D], F32)
    h32 = [P.sb("h32_%d" % i, [128, D], F32) for i in range(4)]
    xfT = P.sb("xfT", [128, 8, 512], BF16)
    hidT = P.sb("hidT", [128, NJ, 512], BF16)
    wgu = [P.sb("wgu%d" % i, [128, 8, 256], BF16) for i in range(5)]
    wdn = [P.sb("wdn%d" % i, [128, D], BF16) for i in range(6)]
    sgu = [P.sb("sgu%d" % i, [128, 512], BF16) for i in range(2)]
    sga2 = [P.sb("sga2_%d" % i, [128, 512], F32) for i in range(2)]
    pst = [P.sb("pst%d" % i, [128, 256], F32) for i in range(2)]
    pbf = [P.sb("pbf%d" % i, [128, 256], BF16) for i in range(2)]
    ppT = P.sb("ppT", [128, 2, 512], BF16)
    st = make_stage(with_xt=False)
    ss2 = [P.sb("ss2_%d" % i, [128, 1], F32) for i in range(2)]
    rs2 = [P.sb("rs2_%d" % i, [128, 1], F32) for i in range(2)]
    load_w(wout[:, :, :], I["w_out"].rearrange("(k p) c -> p k c", p=128), "wout")
    load_w(wpg[:, :, :], I["w_pg"].rearrange("(k p) c -> p k c", p=128), "wpg")
    load_w(wpp[:, :, :], I["w_pp"].rearrange("(k p) c -> p k c", p=128), "wpp")
    P.dma("sp", gffn[:], I["g_ffn"][0, :].partition_broadcast(128), writes=["gffn"])
    P.dma("sp", gple[:], I["g_ple"][0, :].partition_broadcast(128), writes=["gple"])
    P.dma("sp", gfin[:], I["g_fin"][0, :].partition_broadcast(128), writes=["gfin"])
    wdn_v = I["w_down"].rearrange("(j p) c -> p j c", p=128)
    nit = 0
    for tt in range(5):
        samp = (tt == 4)
        nb = 1 if samp else 4
        npp = 4 if samp else 128
        ncol = 4 if samp else 512
        def res_item(tb, ni):
            hh = h32[tb]; hr = "h32_%d" % tb
            i = ni % 2
            t = "n%d" % i
            _, junk, ss, rs, xnb = st

            def r1():
                if samp:
                    P.dma("sp", hh[0:4, :], I["xs"][0:4, :], writes=[hr])
                    lhs = lambda k: mixS[:, k, :]
                    lres = "mixS"
                else:
                    r0 = NT + tt * 512 + tb * 128
                    P.dma("sp", hh[:, :], I["x2"][r0:r0 + 128, :], writes=[hr])
                    lhs = lambda k: mixT[:, k, tt * 512 + tb * 128: tt * 512 + (tb + 1) * 128]
                    lres = "xnT"
                for half in range(2):
                    pz, pzr = banks[(tb % 2) * 2 + half]
                    for k in range(8):
                        P.op("pe", lambda e, k=k, pz=pz, half=half: e.matmul(pz[0:npp, :], lhs(k), wout[:, k, half * 512:(half + 1) * 512], start=(k == 0), stop=(k == 7)),
                             reads=[lres, "wout"], writes=[pzr], signal=(k == 7))
                    P.op("dve", lambda e, pz=pz, half=half: e.tensor_tensor(hh[0:npp, half * 512:(half + 1) * 512], hh[0:npp, half * 512:(half + 1) * 512], pz[0:npp, :], ALU.add),
                         reads=[hr, pzr], writes=[hr])

            def r2():
                rms_rstd(hh[0:npp, :], junk[i][0:npp, :], ss[i][0:npp, :], rs[i][0:npp, :], [hr], t)
                P.op("dve", lambda e: e.scalar_tensor_tensor(xnb[i][0:npp, :], hh[0:npp, :], rs[i][0:npp, 0:1], gffn[0:npp, :], ALU.mult, ALU.mult),
                     reads=[hr, "rs" + t, "gffn"], writes=["xnb%d" % i])

            def r3():
                for k in range(8):
                    P.op("pe", lambda e, k=k: e.transpose(psT[:, k * 128:k * 128 + npp], xnb[i][0:npp, k * 128:(k + 1) * 128], ident[0:npp, 0:npp]),
                         reads=["xnb%d" % i, "ident"], writes=["psT"], signal=(k == 7))
                P.op("act", lambda e: e.copy(xfT[:, :, tb * 128:tb * 128 + npp], psT[:, :].rearrange("p (k t) -> p k t", k=8)[:, :, 0:npp]),
                     reads=["psT"], writes=["xfT"])
            return (r1, r2, r3)

        items = []
        for tb in range(nb):
            items.append(res_item(tb, nit))
            nit += 1
        run_pipeline(items)
        for j in range(NJ):
            ws = j % 5
            load_w(wgu[ws][:, :, :], I["wgu_b"][j, :, :, :], "wgu%d" % ws)
            i = j % 2
            (pga, pgar), (pua, puar) = (banks[0], banks[1]) if i == 0 else (banks[2], banks[3])
            for k in range(8):
                P.op("pe", lambda e, k=k, ws=ws, pga=pga: e.matmul(pga[:, 0:ncol], wgu[ws][:, k, 0:128], xfT[:, k, 0:ncol], start=(k == 0), stop=(k == 7)),
                     reads=["wgu%d" % ws, "xfT"], writes=[pgar], signal=(k == 7))
            for k in range(8):
                P.op("pe", lambda e, k=k, ws=ws, pua=pua: e.matmul(pua[:, 0:ncol], wgu[ws][:, k, 128:256], xfT[:, k, 0:ncol], start=(k == 0), stop=(k == 7)),
                     reads=["wgu%d" % ws, "xfT"], writes=[puar], signal=(k == 7))
            P.op("act", lambda e, i=i, pga=pga: e.activation(sgu[i][:, 0:ncol], pga[:, 0:ncol], AF.Silu), reads=[pgar], writes=["sgu%d" % i])
            P.op("dve", lambda e, i=i, j=j, pua=pua: e.tensor_tensor(hidT[:, j, 0:ncol], sgu[i][:, 0:ncol], pua[:, 0:ncol], ALU.mult), reads=["sgu%d" % i, puar], writes=["hidT"])
        outs = [(tb, half) for tb in range(nb) for half in range(2)]
        for j in range(NJ):
            ws = j % 6
            load_w(wdn[ws][:, :], wdn_v[:, j, :], "wdn%d" % ws)
            for oi, (tb, half) in enumerate(outs):
                pz, pzr = banks[oi]
                P.op("pe", lambda e, j=j, ws=ws, pz=pz, tb=tb, half=half: e.matmul(pz[0:npp, 0:512], hidT[:, j, tb * 128:tb * 128 + npp], wdn[ws][:, half * 512:(half + 1) * 512],
                                                                                 start=(j == 0), stop=(j == NJ - 1), skip_group_check=True),
                     reads=["hidT", "wdn%d" % ws], writes=[pzr], signal=(j == NJ - 1 or oi == len(outs) - 1))
        for oi, (tb, half) in enumerate(outs):
            pz, pzr = banks[oi]
            hh = h32[tb]; hr = "h32_%d" % tb
            P.op("dve", lambda e, pz=pz, half=half, hh=hh: e.tensor_tensor(hh[0:npp, half * 512:(half + 1) * 512], hh[0:npp, half * 512:(half + 1) * 512], pz[0:npp, 0:512], ALU.add),
                 reads=[hr, pzr], writes=[hr])
        def ple_item(tb, ni):
            hh = h32[tb]; hr = "h32_%d" % tb
            i = ni % 2
            t = "n%d" % i
            pi = ni % 2
            _, junk, ss, rs, xnb = st

            def p1():
                rms_rstd(hh[0:npp, :], junk[i][0:npp, :], ss[i][0:npp, :], rs[i][0:npp, :], [hr], t)
                P.op("dve", lambda e: e.scalar_tensor_tensor(xnb[i][0:npp, :], hh[0:npp, :], rs[i][0:npp, 0:1], gple[0:npp, :], ALU.mult, ALU.mult),
                     reads=[hr, "rs" + t, "gple"], writes=["xnb%d" % i])
                if samp:
                    P.dma("sp", pst[pi][0:4, :], I["psm"][0:4, :], writes=["pst%d" % pi])
                else:
                    r0 = tt * 512 + tb * 128
                    P.dma("sp", pst[pi][:, :], I["po"][r0:r0 + 128, :], writes=["pst%d" % pi])
                P.op("dve", lambda e: e.tensor_copy(pbf[pi][0:npp, :], pst[pi][0:npp, :]), reads=["pst%d" % pi], writes=["pbf%d" % pi])

            def p2():
                for k in range(8):
                    P.op("pe", lambda e, k=k: e.transpose(psT[:, k * 128:k * 128 + npp], xnb[i][0:npp, k * 128:(k + 1) * 128], ident[0:npp, 0:npp]),
                         reads=["xnb%d" % i, "ident"], writes=["psT"], signal=(k == 7))
                P.op("act", lambda e: e.copy(xfT[:, :, tb * 128:tb * 128 + npp], psT[:, :].rearrange("p (k t) -> p k t", k=8)[:, :, 0:npp]),
                     reads=["psT"], writes=["xfT"])
                for k in range(2):
                    P.op("pe", lambda e, k=k: e.transpose(psT[:, k * 128:k * 128 + npp], pbf[pi][0:npp, k * 128:(k + 1) * 128], ident[0:npp, 0:npp]),
                         reads=["pbf%d" % pi, "ident"], writes=["psT"], signal=(k == 1))
                P.op("act", lambda e: e.copy(ppT[:, :, tb * 128:tb * 128 + npp], psT[:, 0:256].rearrange("p (k t) -> p k t", k=2)[:, :, 0:npp]),
                     reads=["psT"], writes=["ppT"])

            def p3():
                for half in range(2):
                    pg, pgr = banks[half * 2]
                    pp, ppr = banks[half * 2 + 1]
                    for k in range(8):
                        P.op("pe", lambda e, k=k, pg=pg, half=half: e.matmul(pg[0:npp, :], xfT[:, k, tb * 128:tb * 128 + npp], wpg[:, k, half * 512:(half + 1) * 512], start=(k == 0), stop=(k == 7)),
                             reads=["xfT", "wpg"], writes=[pgr], signal=(k == 7))
                    for k in range(2):
                        P.op("pe", lambda e, k=k, pp=pp, half=half: e.matmul(pp[0:npp, :], ppT[:, k, tb * 128:tb * 128 + npp], wpp[:, k, half * 512:(half + 1) * 512], start=(k == 0), stop=(k == 1)),
                             reads=["ppT", "wpp"], writes=[ppr], signal=(k == 1))
                    si = half
                    P.op("act", lambda e, pg=pg, si=si: e.activation(sga2[si][0:npp, :], pg[0:npp, :], AF.Sigmoid), reads=[pgr], writes=["sga2_%d" % si])
                    P.op("dve", lambda e, pp=pp, si=si: e.tensor_tensor(sga2[si][0:npp, :], sga2[si][0:npp, :], pp[0:npp, :], ALU.mult), reads=["sga2_%d" % si, ppr], writes=["sga2_%d" % si])
                    P.op("dve", lambda e, si=si, half=half: e.tensor_tensor(hh[0:npp, half * 512:(half + 1) * 512], hh[0:npp, half * 512:(half + 1) * 512], sga2[si][0:npp, :], ALU.add),
                         reads=[hr, "sga2_%d" % si], writes=[hr])

            def p4():
                rms_rstd(hh[0:npp, :], junk[i][0:npp, :], ss2[i][0:npp, :], rs2[i][0:npp, :], [hr], "f%d" % i)
                P.op("dve", lambda e: e.scalar_tensor_tensor(hh[0:npp, :], hh[0:npp, :], rs2[i][0:npp, 0:1], gfin[0:npp, :], ALU.mult, ALU.mult),
                     reads=[hr, "rsf%d" % i, "gfin"], writes=[hr])
                if samp:
                    P.dma("sp", O["ys"][0:4, :], hh[0:4, :], reads=[hr], writes=["ysout"])
                else:
                    r0 = tt * 512 + tb * 128
                    P.dma("sp", O["y"][r0:r0 + 128, :], hh[:, :], reads=[hr], writes=["yout"])
            return (p1, p2, p3, p4)

        items = []
        for tb in range(nb):
            items.append(ple_item(tb, nit))
            nit += 1
        run_pipeline(items)
    P.pop()
    P.pop()
    P.finish()


_NC = None


def _rope_tab(pos):
    inv = np.power(np.float32(500000.0), -np.arange(8, dtype=np.float32) / np.float32(8)).astype(np.float32)
    ang = pos.astype(np.float32)[:, None] * inv[None, :]
    return np.concatenate([np.cos(ang), np.sin(ang)], axis=1).astype(np.float32)


def _consts(half):
    j = np.arange(128)
    c = {}
    c["ident"] = np.eye(128, dtype=np.float32)
    c["mprev"] = (j[:, None] >= j[None, :]).astype(np.float32)
    c["mcur"] = (j[:, None] <= j[None, :]).astype(np.float32)
    c["mbias"] = np.where(j[:, None] <= j[None, :], 0.0, NEGPAD).astype(np.float32)
    sel = np.zeros((4, 4, 128), np.float32)
    for h in range(4):
        sel[h, h, :] = 1.0
    c["sel4"] = sel.reshape(4, 512)
    own0 = half * NT
    rope = np.zeros((128, 69, 16), np.float32)
    for g, d in enumerate(DILS):
        span = 128 * d
        for r in range(d):
            pos = own0 - span + r + d * j
            rope[:, RB[g] + r, :] = _rope_tab(np.maximum(pos, 0))
        for s in range(NT // span):
            for r in range(d):
                pos = own0 + s * span + r + d * j
                rope[:, RB[g] + d + s * d + r, :] = _rope_tab(pos)
    c["rope"] = rope
    c["rope_s"] = _rope_tab(np.full((128,), PAST))
    c["flag"] = np.full((128, 1), float(half), np.float32)
    return c


def _block_w_in(w):
    offs = []
    for g in range(3):
        offs += [OFF_AQ + g * 256, OFF_AK + g * 256, OFF_AV + g * 256]
    for h in range(4):
        offs += [OFF_MQ + h * 256, OFF_MK + h * 256, OFF_MV + h * 256, OFF_MO + h * 256]
    offs += [OFF_GA + j * 256 for j in range(8)]
    out = np.empty((33, 128, 8, 256), np.float32)
    for i, o in enumerate(offs):
        out[i] = w[:, o:o + 256].reshape(8, 128, 256).transpose(1, 0, 2)
    return out


def _block_gu(wg, wu):
    out = np.empty((NJ, 128, 8, 256), np.float32)
    out[:, :, :, 0:128] = wg.reshape(8, 128, NJ, 128).transpose(2, 1, 0, 3)
    out[:, :, :, 128:256] = wu.reshape(8, 128, NJ, 128).transpose(2, 1, 0, 3)
    return out


def kernel(x_prompt, x_sample, cache_kv_w128, cache_kv_w512, cache_kv_w2048, state_conv, state_C, state_n, state_m,
           p_prompt, p_sample, norm_mix, w_in, conv_w, conv_b, b_igate, b_fgate, mh_norm, w_proj_a, w_proj_b, w_out,
           norm_ffn, w_gate, w_up, w_down, norm_ple, w_ple_gate, w_ple_proj, norm_final):
    global _NC
    f = lambda a: np.ascontiguousarray(np.asarray(a, dtype=np.float32))
    x_prompt, x_sample, p_prompt, p_sample = f(x_prompt), f(x_sample), f(p_prompt), f(p_sample)
    caches = [f(cache_kv_w128), f(cache_kv_w512), f(cache_kv_w2048)]
    state_conv, state_C, state_n, state_m = f(state_conv), f(state_C), f(state_n), f(state_m)
    shared = {
        "w_in_b": _block_w_in(f(w_in)[0]), "w_gates": f(f(w_in)[0][:, OFF_MI:OFF_MI + 8].reshape(8, 128, 8).transpose(1, 0, 2)),
        "wgu_b": _block_gu(f(w_gate)[0], f(w_up)[0]), "cw": f(f(conv_w)[0].reshape(4, 16, 128).transpose(2, 1, 0)),
        "cb": f(f(conv_b)[0].reshape(16, 128).T), "b_i": f(b_igate)[0].reshape(4, 1), "b_f": f(b_fgate)[0].reshape(4, 1),
        "mhn_col": f(f(mh_norm)[0].reshape(8, 128).T), "w_pa": f(w_proj_a)[0], "w_pb": f(w_proj_b)[0], "w_out": f(w_out)[0],
        "g_mix": f(norm_mix)[0].reshape(1, D), "g_ffn": f(norm_ffn)[0].reshape(1, D), "g_ple": f(norm_ple)[0].reshape(1, D),
        "g_fin": f(norm_final).reshape(1, D), "w_down": f(w_down)[0],
        "w_pg": f(w_ple_gate)[0], "w_pp": f(w_ple_proj)[0],
    }
    cons = [_consts(0), _consts(1)]
    in_maps = []
    for c in range(8):
        b, half = c // 2, c % 2
        own = x_prompt[b, half * NT:(half + 1) * NT]
        hist = x_prompt[b, 0:NT]
        m = dict(shared)
        m.update(cons[half])
        m["x2"] = f(np.concatenate([hist, own], axis=0))
        xs = np.zeros((128, D), np.float32); xs[0:4] = x_sample[4 * c:4 * c + 4, 0]
        m["xs"] = xs
        m["po"] = f(p_prompt[0, b, half * NT:(half + 1) * NT])
        psm = np.zeros((128, 256), np.float32); psm[0:4] = p_sample[0, 4 * c:4 * c + 4, 0]
        m["psm"] = psm
        for g in range(3):
            m["ck%d" % g] = f(caches[g][0, 4 * c:4 * c + 4].reshape(4, WINS[g], 512))
        sc = state_conv[0, 4 * c:4 * c + 4]
        m["sconv"] = f(sc)
        m["sconvT"] = f(sc.reshape(4, 3, 16, 128).transpose(3, 0, 2, 1))
        m["sC"] = f(state_C[0, 4 * c:4 * c + 4])
        m["sn"] = f(state_n[0, 4 * c:4 * c + 4])
        m["smT"] = f(state_m[0, 4 * c:4 * c + 4].T)
        in_maps.append(m)
    if _NC is None:
        _NC = build()
    res = run_bass_kernel_spmd(_NC, in_maps, core_ids=list(range(8))).results

    y_prompt = np.zeros((4, SEQ, D), np.float32)
    y_sample = np.zeros((32, 1, D), np.float32)
    kvp = [np.zeros((1, 4, WINS[g], 2, 4, 64), np.float32) for g in range(3)]
    kvs = [np.zeros((1, 32, WINS[g], 2, 4, 64), np.float32) for g in range(3)]
    conv_p = np.zeros((1, 4, 3, 2048), np.float32); conv_s = np.zeros((1, 32, 3, 2048), np.float32)
    C_p = np.zeros((1, 4, 4, 256, 256), np.float32); C_s = np.zeros((1, 32, 4, 256, 256), np.float32)
    n_p = np.zeros((1, 4, 4, 256), np.float32); n_s = np.zeros((1, 32, 4, 256), np.float32)
    m_p = np.zeros((1, 4, 4), np.float32); m_s = np.zeros((1, 32, 4), np.float32)
    for c in range(8):
        b, half = c // 2, c % 2
        r = res[c]
        y_prompt[b, half * NT:(half + 1) * NT] = r["y"]
        y_sample[4 * c:4 * c + 4, 0] = r["ys"]
        for g in range(3):
            kvs[g][0, 4 * c:4 * c + 4] = r["kvs%d" % g].reshape(4, WINS[g], 2, 4, 64)
        conv_s[0, 4 * c:4 * c + 4] = r["convs"]
        C_s[0, 4 * c:4 * c + 4] = r["Cs"]
        n_s[0, 4 * c:4 * c + 4] = r["ns"]
        m_s[0, 4 * c:4 * c + 4] = r["ms"].T
        if half == 1:
            for g in range(3):
                kvp[g][0, b] = r["kvp%d" % g].reshape(WINS[g], 2, 4, 64)
            conv_p[0, b] = r["convp"]
            C_p[0, b] = r["Cp"]
            n_p[0, b] = r["np"]
            m_p[0, b] = r["mp"][:, 0]
    return (y_prompt, y_sample, kvp[0], kvs[0], kvp[1], kvs[1], kvp[2], kvs[2], conv_p, conv_s, C_p, C_s, n_p, n_s, m_p, m_s)
```

```python
from contextlib import ExitStack
import numpy as np
import concourse.bass as bass
import concourse.mybir as mybir
from concourse.bass_utils import run_bass_kernel_spmd

F32 = mybir.dt.float32
BF16 = mybir.dt.bfloat16
AF = mybir.ActivationFunctionType
ALU = mybir.AluOpType

COMPUTE = ("pe", "act", "dve", "pool")
ROLL = 24000

D = 1024
SEQ = 4096
NT = 2048
HD = 64
DILS = (1, 4, 16)
WINS = (128, 512, 2048)
PAST = 16384
OFF_AQ, OFF_AK, OFF_AV = 0, 768, 1536
OFF_MQ, OFF_MK, OFF_MV, OFF_MO = 2304, 3328, 4352, 5376
OFF_MI, OFF_MF, OFF_GA, OFF_GB = 6400, 6404, 6408, 7432
IN_COLS = 8456
DFF = 2816
NJ = DFF // 128
EPS = 1e-6
NEGPAD = -30000.0
HB = (0, 1, 5)
RB = (0, 17, 37)


class Res:
    __slots__ = ("name", "w", "rd", "psum")

    def __init__(self, name, psum=False):
        self.name = name
        self.w = None
        self.rd = []
        self.psum = psum


class _Rec:
    def __init__(self):
        self.call = None

    def __getattr__(self, name):
        def f(*a, **k):
            self.call = (name, a, k)
            return None
        return f


class Prog:
    def __init__(self, nc, n_dma_sems=12):
        self.nc = nc
        self.scopes = [ExitStack()]
        self.streams = {e: [] for e in ("pe", "act", "dve", "pool", "sp")}
        self.sem = {}
        self.cnt = {}
        self.seen = {e: {} for e in self.streams}
        self.nsem = 0
        for e in COMPUTE:
            self._new_sem(e)
        self.dma_sems = {}
        self.dma_tot = {}
        self.dma_rr = {}
        for q in ("sp", "pool"):
            self.dma_sems[q] = [self._alloc_sem("d%s%d" % (q, i)) for i in range(n_dma_sems)]
            self.dma_tot[q] = [0] * n_dma_sems
            self.dma_rr[q] = 0
        self.res = {}
        self.nops = 0

    def _alloc_sem(self, name):
        self.nsem += 1
        return self.scopes[0].enter_context(self.nc.semaphore(name))

    def _new_sem(self, e):
        self.sem[e] = self._alloc_sem("c%s%d" % (e, self.nsem))
        self.cnt[e] = 0

    def sb(self, name, shape, dt):
        self.ntens = getattr(self, "ntens", 0) + 1
        return self.scopes[-1].enter_context(self.nc.sbuf_tensor("s%d_%s" % (self.ntens, name), list(shape), dt))

    def ps(self, name, shape, dt):
        return self.scopes[0].enter_context(self.nc.psum_tensor("p_" + name, list(shape), dt))

    def push(self):
        self.scopes.append(ExitStack())

    def pop(self):
        self.flush()
        self.scopes.pop().close()

    def R(self, name, psum=False):
        r = self.res.get(name)
        if r is None:
            r = Res(name, psum)
            self.res[name] = r
        return r

    def _need(self, e, tok, deps, same_raw):
        if tok is None:
            return
        we, sem, val = tok
        if we == e and e == "pe":
            return
        deps.append((sem, val))

    def _waits(self, e, reads, writes):
        deps = []
        for r in reads:
            self._need(e, r.w, deps, True)
            if r.psum:
                for t in r.rd:
                    self._need(e, t, deps, False)
        for w in writes:
            self._need(e, w.w, deps, w.psum)
            for t in w.rd:
                self._need(e, t, deps, False)
        seen = self.seen[e]
        best = {}
        for sem, val in deps:
            k = id(sem)
            if seen.get(k, 0) >= val:
                continue
            if k not in best or best[k][1] < val:
                best[k] = (sem, val)
        for k, (sem, val) in best.items():
            seen[k] = val
            self.streams[e].append(("wait", sem, val))

    def _commit(self, tok, reads, writes):
        for r in reads:
            r.rd = [t for t in r.rd if not (t[1] is tok[1])] + [tok]
        for w in writes:
            w.w = tok
            w.rd = []

    def _rl(self, xs):
        return [self.R(x) if isinstance(x, str) else x for x in xs]

    def op(self, e, fn, reads=(), writes=(), signal=True):
        reads = self._rl(reads)
        writes = self._rl(writes)
        self._waits(e, reads, writes)
        self.nops += 1
        rec = _Rec()
        fn(rec)
        fn = rec.call
        if signal:
            if self.cnt[e] >= ROLL:
                self._new_sem(e)
            self.cnt[e] += 1
            tok = (e, self.sem[e], self.cnt[e])
            self.streams[e].append(("op", fn, self.sem[e]))
        else:
            tok = (e, self.sem[e], self.cnt[e] + 1)
            self.streams[e].append(("op", fn, None))
        self._commit(tok, reads, writes)
        return tok

    def dma(self, q, out_ap, in_ap, reads=(), writes=(), **kw):
        reads = self._rl(reads)
        writes = self._rl(writes)
        self._waits(q, reads, writes)
        i = self.dma_rr[q]
        self.dma_rr[q] = (i + 1) % len(self.dma_sems[q])
        sem = self.dma_sems[q][i]
        prev = self.dma_tot[q][i]
        if prev > 0 and self.seen[q].get(id(sem), 0) < prev:
            self.seen[q][id(sem)] = prev
            self.streams[q].append(("wait", sem, prev))
        self.dma_tot[q][i] = prev + 16
        tok = ("dma", sem, prev + 16)
        self.streams[q].append(("dma", out_ap, in_ap, sem, kw))
        self._commit(tok, reads, writes)
        return tok

    def barrier(self):
        toks = []
        for e in COMPUTE:
            if self.cnt[e] > 0:
                toks.append((self.sem[e], self.cnt[e]))
        for q in self.dma_sems:
            for sem, tot in zip(self.dma_sems[q], self.dma_tot[q]):
                if tot > 0:
                    toks.append((sem, tot))
        for e in self.streams:
            for sem, val in toks:
                if e in COMPUTE and sem is self.sem[e]:
                    continue
                if self.seen[e].get(id(sem), 0) < val:
                    self.seen[e][id(sem)] = val
                    self.streams[e].append(("wait", sem, val))

    def flush(self):
        self.barrier()
        nc = self.nc
        streams = self.streams

        def replay(eng, items):
            for it in items:
                if it[0] == "wait":
                    eng.wait_ge(it[1], it[2])
                elif it[0] == "op":
                    ins = getattr(eng, it[1][0])(*it[1][1], **it[1][2])
                    if it[2] is not None:
                        ins.then_inc(it[2], 1)
                else:
                    _, o, i, sem, kw = it
                    eng.dma_start(out=o, in_=i, **kw).then_inc(sem, 16)

        with nc.Block() as block:
            @block.sync
            def _(sync):
                replay(sync, streams["sp"])

            @block.tensor
            def _(t):
                replay(t, streams["pe"])

            @block.scalar
            def _(s):
                replay(s, streams["act"])

            @block.vector
            def _(v):
                replay(v, streams["dve"])

            @block.gpsimd
            def _(g):
                replay(g, streams["pool"])
        for e in streams:
            streams[e] = []

    def finish(self):
        self.flush()
        while self.scopes:
            self.scopes.pop().close()


def ssl(start, n, step):
    return slice(start, start + (n - 1) * step + 1, step)


class _Stop(Exception):
    pass


STOP_AFTER = None


def build():
    nc = bass.Bass("TRN2", target_bir_lowering=False)
    state = {}
    try:
        _build_inner(nc, state)
    except _Stop:
        state["P"].finish()
    return nc


def _build_inner(nc, state):
    I, O = {}, {}

    def din(n, shape):
        I[n] = nc.dram_tensor(n, list(shape), F32, kind="ExternalInput").ap()

    def dout(n, shape):
        O[n] = nc.dram_tensor(n, list(shape), F32, kind="ExternalOutput").ap()

    din("x2", [2 * NT, D]); din("xs", [128, D]); din("po", [NT, 256]); din("psm", [128, 256])
    din("ck0", [4, 128, 512]); din("ck1", [4, 512, 512]); din("ck2", [4, 2048, 512])
    din("sconvT", [128, 4, 16, 3]); din("sconv", [4, 3, 2048])
    din("sC", [4, 4, 256, 256]); din("sn", [4, 4, 256]); din("smT", [4, 4])
    din("w_in_b", [33, 128, 8, 256]); din("w_gates", [128, 8, 8]); din("cw", [128, 16, 4]); din("cb", [128, 16])
    din("b_i", [4, 1]); din("b_f", [4, 1]); din("mhn_col", [128, 8])
    din("w_pa", [256, D]); din("w_pb", [D, D]); din("w_out", [D, D])
    din("g_mix", [1, D]); din("g_ffn", [1, D]); din("g_ple", [1, D]); din("g_fin", [1, D])
    din("wgu_b", [NJ, 128, 8, 256]); din("w_down", [DFF, D])
    din("w_pg", [D, D]); din("w_pp", [256, D])
    din("ident", [128, 128]); din("mprev", [128, 128]); din("mcur", [128, 128]); din("mbias", [128, 128])
    din("sel4", [4, 512]); din("rope", [128, 69, 16]); din("rope_s", [128, 16]); din("flag", [128, 1])

    dout("y", [NT, D]); dout("ys", [4, D])
    dout("kvp0", [128, 512]); dout("kvp1", [512, 512]); dout("kvp2", [2048, 512])
    dout("kvs0", [4, 128, 512]); dout("kvs1", [4, 512, 512]); dout("kvs2", [4, 2048, 512])
    dout("convp", [3, 2048]); dout("convs", [4, 3, 2048])
    dout("Cp", [4, 256, 256]); dout("np", [4, 256]); dout("mp", [4, 1])
    dout("Cs", [4, 4, 256, 256]); dout("ns", [4, 4, 256]); dout("ms", [4, 4])

    P = Prog(nc)
    state["P"] = P

    def phase_done(i):
        if STOP_AFTER is not None and i >= STOP_AFTER:
            raise _Stop()

    ps0 = P.ps("ps0", [128, 512], F32)
    ps1 = P.ps("ps1", [128, 512], F32)
    psA = P.ps("psA", [128, 1024], F32)
    ps4 = P.ps("ps4", [128, 512], F32)
    psTf = P.ps("psT", [128, 512], F32)
    psT = psTf[:, :].bitcast(BF16)
    ps6 = P.ps("ps6", [128, 512], F32)
    ps7 = P.ps("ps7", [128, 512], F32)
    for n in ("ps0", "ps1", "psA0", "psA1", "ps4", "psT", "ps6", "ps7"):
        P.R(n, True)
    banks = [(ps0, "ps0"), (ps1, "ps1"), (psA[:, 0:512], "psA0"), (psA[:, 512:1024], "psA1"),
             (ps4, "ps4"), (ps6, "ps6"), (ps7, "ps7"), (psTf, "psT")]

    identf = P.sb("identf", [128, 128], F32)
    ident = P.sb("ident", [128, 128], BF16)
    mask2 = P.sb("mask2", [128, 2, 128], BF16)
    mask2f = P.sb("mask2f", [128, 2, 128], BF16)
    mbias = P.sb("mbias", [128, 128], BF16)
    sel4 = P.sb("sel4", [4, 4, 128], F32)
    rope = P.sb("rope", [128, 69, 16], F32)
    rope_s = P.sb("rope_s", [128, 16], F32)
    flag = P.sb("flag", [128, 1], F32)
    ones = P.sb("ones", [128, 512], F32)
    C32 = P.sb("C32", [128, 4, 2, 257], F32)
    cbnd = P.sb("cbnd", [128, 4, 4, 3], F32)
    MPh = P.sb("MPh", [4, 1], F32)
    cw = P.sb("cw", [128, 16, 4], F32)
    cb = P.sb("cb", [128, 16], F32)
    b_i = P.sb("b_i", [4, 1], F32)
    nb_f = P.sb("nb_f", [4, 1], F32)
    wgt8 = P.sb("wgt8", [128, 8, 8], BF16)
    nln16 = P.sb("nln16", [4, 1], F32)

    P.push()
    mtmp = P.sb("mtmp", [128, 2, 128], F32)
    P.dma("sp", identf[:], I["ident"][:, :], writes=["identf"])
    P.dma("sp", mtmp[:, 0, :], I["mprev"][:, :], writes=["mtmp"])
    P.dma("sp", mtmp[:, 1, :], I["mcur"][:, :], writes=["mtmp"])
    P.dma("pool", mbias[:], I["mbias"][:, :], writes=["mbias"])
    P.dma("sp", sel4[:], I["sel4"].rearrange("k (h m) -> k h m", h=4), writes=["sel4"])
    P.dma("sp", rope[:], I["rope"][:, :, :], writes=["rope"])
    P.dma("sp", rope_s[:], I["rope_s"][:, :], writes=["rope_s"])
    P.dma("sp", flag[:], I["flag"][:, :], writes=["flag"])
    P.dma("sp", cw[:], I["cw"][:, :, :], writes=["cw"])
    P.dma("sp", cb[:], I["cb"][:, :], writes=["cb"])
    P.dma("sp", b_i[:], I["b_i"][:, :], writes=["b_i"])
    P.dma("sp", nb_f[:], I["b_f"][:, :], writes=["nb_f"])
    P.dma("pool", wgt8[:], I["w_gates"][:, :, :], writes=["wgt8"])
    P.op("dve", lambda e: e.tensor_copy(ident[:], identf[:]), reads=["identf"], writes=["ident"])
    P.op("dve", lambda e: e.tensor_copy(mask2[:], mtmp[:]), reads=["mtmp"], writes=["mask2"])
    P.op("dve", lambda e: e.tensor_copy(mask2f[:, 1, :], mtmp[:, 1, :]), reads=["mtmp"], writes=["mask2f"])
    P.op("dve", lambda e: e.tensor_scalar(mask2f[:, 0, :], mtmp[:, 0, :], flag[:, 0:1], None, ALU.mult),
         reads=["mtmp", "flag", "mask2f"], writes=["mask2f"])
    P.op("dve", lambda e: e.memset(ones[:], 1.0), writes=["ones"])
    P.op("dve", lambda e: e.memset(nln16[:], -2.7725887222), writes=["nln16"])
    P.op("dve", lambda e: e.tensor_scalar(nb_f[:], nb_f[:], -1.0, None, ALU.mult), reads=["nb_f"], writes=["nb_f"])
    P.op("dve", lambda e: e.memset(C32[:], 0.0), writes=["C32"])
    P.op("dve", lambda e: e.memset(cbnd[:], 0.0), writes=["cbnd"])
    P.pop()


    def load_w(dst, src_view, res):
        P.dma("pool", dst, src_view, writes=[res])

    def rms_rstd(src_ap, junk, ss, rs, srcres, tag):
        P.op("act", lambda e: e.activation(junk, src_ap, AF.Square, accum_out=ss), reads=srcres, writes=["junk" + tag, "ss" + tag])
        P.op("dve", lambda e: e.tensor_scalar(rs, ss, 1.0 / D, EPS, ALU.mult, ALU.add), reads=["ss" + tag], writes=["rs" + tag])
        P.op("act", lambda e: e.activation(rs, rs, AF.Sqrt), reads=["rs" + tag], writes=["rs" + tag])
        P.op("dve", lambda e: e.reciprocal(rs, rs), reads=["rs" + tag], writes=["rs" + tag])

    def run_pipeline(items):
        ns = max(len(t) for t in items)
        for n_ in range(len(items) + ns - 1):
            for k in range(ns):
                idx = n_ - k
                if 0 <= idx < len(items) and k < len(items[idx]):
                    items[idx][k]()

    def make_stage(with_xt=True):
        xt = [P.sb("xt%d" % i, [128, D], F32) for i in range(2)] if with_xt else None
        junk = [P.sb("junk%d" % i, [128, D], BF16) for i in range(2)]
        ss = [P.sb("ss%d" % i, [128, 1], F32) for i in range(2)]
        rs = [P.sb("rs%d" % i, [128, 1], F32) for i in range(2)]
        xnb = [P.sb("xnb%d" % i, [128, D], BF16) for i in range(2)]
        return xt, junk, ss, rs, xnb

    def norm_T(st, i, src, srcres, np_, gbc, gres, dstT, dst_res):
        _, junk, ss, rs, xnb = st
        t = "n%d" % i
        rms_rstd(src, junk[i][0:np_, :], ss[i][0:np_, :], rs[i][0:np_, :], [srcres], t)
        P.op("dve", lambda e: e.scalar_tensor_tensor(xnb[i][0:np_, :], src, rs[i][0:np_, 0:1], gbc[0:np_, :], ALU.mult, ALU.mult),
             reads=[srcres, "rs" + t, gres], writes=["xnb%d" % i])
        for k in range(8):
            P.op("pe", lambda e, k=k: e.transpose(psT[:, k * 128:k * 128 + np_], xnb[i][0:np_, k * 128:(k + 1) * 128], ident[0:np_, 0:np_]),
                 reads=["xnb%d" % i, "ident"], writes=["psT"], signal=(k == 7))
        P.op("act", lambda e: e.copy(dstT, psT[:, :].rearrange("p (k t) -> p k t", k=8)[:, :, 0:np_]),
             reads=["psT"], writes=[dst_res])

    P.push()
    xnT = P.sb("xnT", [128, 8, NT], BF16)
    xnSc = P.sb("xnSc", [128, 8, 4], BF16)
    mixS = P.sb("mixS", [128, 8, 4], BF16)
    P.push()
    oaT = P.sb("oaT", [64, 4, NT], BF16)
    oaTs = P.sb("oaTs", [64, 4, 4], BF16)
    P.push()
    KTh = P.sb("KTh", [128, 2, 21, 128], BF16)
    VAh = P.sb("VAh", [128, 21, 4, 65], BF16)
    P.op("dve", lambda e: e.memset(VAh[:], 1.0), writes=["VAh"])

    def phase_norm(row0, with_samples):
        P.push()
        gmix = P.sb("gmix", [128, D], F32)
        P.dma("sp", gmix[:], I["g_mix"][0, :].partition_broadcast(128), writes=["gmix"])
        st = make_stage()
        def nitem(blk):
            i = blk % 2
            _, junk, ss, rs, xnb = st
            t = "n%d" % i

            def n1():
                P.dma("sp", st[0][i][:, :], I["x2"][row0 + blk * 128:row0 + (blk + 1) * 128, :], writes=["xt%d" % i])
                rms_rstd(st[0][i][:, :], junk[i][:, :], ss[i][:, :], rs[i][:, :], ["xt%d" % i], t)
                P.op("dve", lambda e: e.scalar_tensor_tensor(xnb[i][:, :], st[0][i][:, :], rs[i][:, 0:1], gmix[:, :], ALU.mult, ALU.mult),
                     reads=["xt%d" % i, "rs" + t, "gmix"], writes=["xnb%d" % i])

            def n2():
                for k in range(8):
                    P.op("pe", lambda e, k=k: e.transpose(psT[:, k * 128:(k + 1) * 128], xnb[i][:, k * 128:(k + 1) * 128], ident[:, :]),
                         reads=["xnb%d" % i, "ident"], writes=["psT"], signal=(k == 7))
                P.op("act", lambda e: e.copy(xnT[:, :, blk * 128:(blk + 1) * 128], psT[:, :].rearrange("p (k t) -> p k t", k=8)),
                     reads=["psT"], writes=["xnT"])
            return (n1, n2)
        run_pipeline([nitem(blk) for blk in range(16)])
        if with_samples:
            P.dma("sp", st[0][0][0:4, :], I["xs"][0:4, :], writes=["xt0"])
            norm_T(st, 0, st[0][0][0:4, :], "xt0", 4, gmix, "gmix", xnSc[:, :, :], "xnSc")
        P.pop()

    phase_norm(0, True)
    def attn_scope():
        A = {}
        A["wg"] = P.sb("wg", [128, 8, 768], BF16)
        A["qkv32"] = [P.sb("qkv32_%d" % i, [128, 768], F32) for i in range(3)]
        A["qkb"] = [P.sb("qkb%d" % i, [128, 512], BF16) for i in range(2)]
        A["rt"] = [P.sb("rt%d" % i, [128, 4, 8, 8], F32) for i in range(2)]
        return A

    def load_wg(A, g):
        for j, off in enumerate((OFF_AQ, OFF_AK, OFF_AV)):
            load_w(A["wg"][:, :, j * 256:(j + 1) * 256], I["w_in_b"][g * 3 + j, :, :, :], "wg")

    def attn_p1(A, it, lhs_fn, lhs_res, VA_dst, VA_res):
        i3 = it % 3
        i = it % 2
        wg = A["wg"]
        qkv32 = A["qkv32"][i3]; qres = "qkv32_%d" % i3
        pqk, pqkr = (ps0, "ps0") if i == 0 else (ps6, "ps6")
        pvv, pvr = (ps1, "ps1") if i == 0 else (ps7, "ps7")
        for k in range(8):
            P.op("pe", lambda e, k=k: e.matmul(pqk[:, :], lhs_fn(k), wg[:, k, 0:512], start=(k == 0), stop=(k == 7)),
                 reads=[lhs_res, "wg"], writes=[pqkr], signal=(k == 7))
        for k in range(8):
            P.op("pe", lambda e, k=k: e.matmul(pvv[:, 0:256], lhs_fn(k), wg[:, k, 512:768], start=(k == 0), stop=(k == 7)),
                 reads=[lhs_res, "wg"], writes=[pvr], signal=(k == 7))
        P.op("act", lambda e: e.copy(qkv32[:, 0:512], pqk[:, :]), reads=[pqkr], writes=[qres])
        P.op("act", lambda e: e.copy(qkv32[:, 512:768], pvv[:, 0:256]), reads=[pvr], writes=[qres])
        P.op("act", lambda e: e.copy(VA_dst[:, :, 0:64], pvv[:, 0:256].rearrange("p (h d) -> p h d", d=64)), reads=[pvr], writes=[VA_res])

    def attn_p2(A, it, ropetab, roperes, kv_out=None, kv_rows=128):
        i3 = it % 3
        i = it % 2
        qkv32 = A["qkv32"][i3]; qres = "qkv32_%d" % i3
        qkb = A["qkb"][i]; bres = "qkb%d" % i
        rt = A["rt"][i]; rres = "rt%d" % i
        cosb = ropetab[:, 0:8].unsqueeze(1).broadcast_to([128, 8, 8])
        sinb = ropetab[:, 8:16].unsqueeze(1).broadcast_to([128, 8, 8])
        q3 = qkv32[:, 0:512].rearrange("p (h d) -> p h d", d=64)
        x1 = q3[:, :, 0:8]
        x2 = q3[:, :, 8:16]
        P.op("dve", lambda e: e.tensor_tensor(rt[:, 0], x1, cosb, ALU.mult), reads=[qres, roperes], writes=[rres])
        P.op("dve", lambda e: e.tensor_tensor(rt[:, 1], x2, sinb, ALU.mult), reads=[qres, roperes], writes=[rres])
        P.op("dve", lambda e: e.tensor_tensor(rt[:, 2], x2, cosb, ALU.mult), reads=[qres, roperes], writes=[rres])
        P.op("dve", lambda e: e.tensor_tensor(rt[:, 3], x1, sinb, ALU.mult), reads=[qres, roperes], writes=[rres])
        P.op("dve", lambda e: e.tensor_tensor(x1, rt[:, 0], rt[:, 1], ALU.subtract), reads=[rres, qres], writes=[qres])
        P.op("dve", lambda e: e.tensor_tensor(x2, rt[:, 2], rt[:, 3], ALU.add), reads=[rres, qres], writes=[qres])
        P.op("dve", lambda e: e.tensor_copy(qkb[:, :], qkv32[:, 0:512]), reads=[qres], writes=[bres])
        if kv_out is not None:
            P.dma("sp", kv_out, qkv32[0:kv_rows, 256:768], reads=[qres], writes=["kvout"])

    def attn_p3(A, it, KT_dst, KT_res, QT_dst=None, QT_res=None):
        i = it % 2
        qkb = A["qkb"][i]; bres = "qkb%d" % i
        for j in range(4):
            P.op("pe", lambda e, j=j: e.transpose(psT[:, j * 128:(j + 1) * 128], qkb[:, j * 128:(j + 1) * 128], ident[:, :]),
                 reads=[bres, "ident"], writes=["psT"], signal=(j == 3))
        pT3 = psT[:, 0:512].rearrange("p (j t) -> p j t", j=4)
        if QT_dst is not None:
            P.op("act", lambda e: e.copy(QT_dst, pT3[:, 0:2, :]), reads=["psT"], writes=[QT_res])
        P.op("act", lambda e: e.copy(KT_dst, pT3[:, 2:4, :]), reads=["psT"], writes=[KT_res])

    def attn_c1(it, QT, QT_res, KTp, KTp_res, KTc, KTc_res, PT):
        i = it % 2
        pt = PT[i]; pres = "PT%d" % i
        pS = psA[:, :].rearrange("p (a c b q) -> p a c b q", a=2, c=2, b=2)
        for h in range(4):
            hp, po = h // 2, (h % 2) * 64
            bank = "psA%d" % (h % 2)
            P.op("pe", lambda e, h=h, hp=hp, po=po: e.matmul(pS[:, h % 2, h // 2, 0, :], KTp[po:po + 64, hp, :], QT[po:po + 64, hp, :], start=True, stop=True),
                 reads=[KTp_res, QT_res], writes=[bank], signal=False)
            P.op("pe", lambda e, h=h, hp=hp, po=po: e.matmul(pS[:, h % 2, h // 2, 1, :], KTc[po:po + 64, hp, :], QT[po:po + 64, hp, :], start=True, stop=True),
                 reads=[KTc_res, QT_res], writes=[bank], signal=(h >= 2))
        P.op("act", lambda e: e.activation(pt[:, 0, :, :, :], pS[:, 0, :, :, :], AF.Exp, scale=0.125), reads=["psA0"], writes=[pres])
        P.op("act", lambda e: e.activation(pt[:, 1, :, :, :], pS[:, 1, :, :, :], AF.Exp, scale=0.125), reads=["psA1"], writes=[pres])

    def attn_c2(it, VAp, VAp_res, VAc, VAc_res, msk, PT, acc_dst, acc_res, first, ncol):
        i = it % 2
        pt = PT[i]; pres = "PT%d" % i
        mb = msk[:, :, :].unsqueeze(1).broadcast_to([128, 4, 2, 128])
        pt4 = pt[:, :, :, :, :].rearrange("p a c b q -> p (a c) b q")
        P.op("dve", lambda e: e.tensor_tensor(pt4, pt4, mb, ALU.mult), reads=[pres, "mask2", "mask2f"], writes=[pres])
        pO = ps4[0:65, :].rearrange("p (h q) -> p h q", h=4)
        for h in range(4):
            P.op("pe", lambda e, h=h: e.matmul(pO[:, h, :], VAp[:, h, :], pt[:, h % 2, h // 2, 0, :], start=True, stop=False),
                 reads=[VAp_res, pres], writes=["ps4"], signal=False)
            P.op("pe", lambda e, h=h: e.matmul(pO[:, h, :], VAc[:, h, :], pt[:, h % 2, h // 2, 1, :], start=False, stop=True),
                 reads=[VAc_res, pres], writes=["ps4"], signal=(h == 3))
        if first:
            P.op("act", lambda e: e.copy(acc_dst, pO[:, :, 0:ncol]), reads=["ps4"], writes=[acc_res])
        else:
            P.op("dve", lambda e: e.tensor_tensor(acc_dst, acc_dst, pO[:, :, 0:ncol], ALU.add), reads=["ps4", acc_res], writes=[acc_res])

    P.push()
    A = attn_scope()

    def hist_item(g, r, it):
        d = DILS[g]
        hb = HB[g] + r
        start = NT - 128 * d + r

        def s1():
            if r == 0:
                load_wg(A, g)
            attn_p1(A, it, lambda k: xnT[:, k, ssl(start, 128, d)], "xnT", VAh[:, hb, :, :], "VAh")

        def s2():
            attn_p2(A, it, rope[:, RB[g] + r, :], "rope")

        def s3():
            attn_p3(A, it, KTh[:, :, hb, :], "KTh%d" % hb)
        return (s1, s2, s3)

    items = []
    it = 0
    for g in range(3):
        for r in range(DILS[g]):
            items.append(hist_item(g, r, it))
            it += 1
    run_pipeline(items)
    P.pop()
    phase_done(2)


    def mlstm_scope(nt, full):
        nch = nt // 128
        M = {"nt": nt, "nch": nch, "full": full}
        M["wm"] = P.sb("wm", [128, 8, 1024], BF16)
        for n in ("IG", "LF", "Bc", "bb", "Aa", "Mr", "WK", "tmpr"):
            M[n] = P.sb(n, [4, 512], F32)
        M["MP"] = P.sb("MP", [4, nch + 1], F32)
        M["MN"] = P.sb("MN", [4, nch], F32)
        M["DEC"] = P.sb("DEC", [4, nch], F32)
        M["COL"] = P.sb("COL", [128, nch, 3, 4], F32)
        M["DECB"] = P.sb("DECB", [128, 4, nch], F32)
        nloc = min(nt, 1024)
        M["nloc"] = nloc
        M["kT"] = P.sb("kT", [128, 2, nloc], BF16)
        nfc = 4 if full else 2
        M["xpre"] = P.sb("xpre", [128, nfc, 515], F32)
        M["bnd"] = P.sb("bnd", [128, nfc, 3], F32)
        M["cacc"] = [P.sb("cacc%d" % i, [128, 512], F32) for i in range(2)]
        M["ktmp"] = P.sb("ktmp", [128, 512], F32)
        M["Vaug"] = P.sb("Vaug", [128, nloc // 128, 257], BF16)
        M["kwa"] = P.sb("kwa", [128, 8, 256], BF16)
        if full:
            M["EM"] = P.sb("EM", [4, 512], F32)
            M["NEGM"] = P.sb("NEGM", [4, nt], F32)
            M["WI"] = P.sb("WI", [4, nt], F32)
            M["Cbf"] = [P.sb("Cbf%d" % i, [128, 2, 257], BF16) for i in range(2)]
            M["wTa"] = P.sb("wTa", [128, 8, 128], BF16)
            M["qpTa"] = P.sb("qpTa", [128, 8, 2, 128], BF16)
            if nt == 512:
                M["Csb"] = [P.sb("Csb%d" % i, [128, 2, 257], F32) for i in range(2)]
            M["qT"] = P.sb("qT", [128, 2, nloc], BF16)
            M["G"] = P.sb("G", [128, nloc // 128, 256], BF16)
            M["mhn"] = P.sb("mhncol", [128, 8], F32)
            M["Dt"] = [P.sb("Dt%d" % i, [128, 128], F32) for i in range(2)]
            M["sm"] = [P.sb("sm%d" % i, [128, 8], F32) for i in range(2)]
            M["jk"] = [P.sb("jk%d" % i, [128, 256], BF16) for i in range(2)]
            M["ob"] = [P.sb("ob%d" % i, [128, 256], BF16) for i in range(2)]
            P.dma("sp", M["mhn"][:], I["mhn_col"][:, :], writes=["mhncol"])
        P.op("dve", lambda e: e.memset(M["Vaug"][:], 1.0), writes=["Vaug"])
        return M

    def mlstm_gates(M, Xf, Xres, mode):
        nt, nch, full = M["nt"], M["nch"], M["full"]
        IG, LF, Bc, bb, Aa, Mr, WK, tmpr = M["IG"], M["LF"], M["Bc"], M["bb"], M["Aa"], M["Mr"], M["WK"], M["tmpr"]
        MP, MN, DEC = M["MP"], M["MN"], M["DEC"]
        samp = (mode == "samp")
        B3 = Bc[:, :].rearrange("p (c t) -> p c t", t=128)
        b3 = bb[:, :].rearrange("p (c t) -> p c t", t=128)
        A3 = Aa[:, :].rearrange("p (c t) -> p c t", t=128)
        M3 = Mr[:, :].rearrange("p (c t) -> p c t", t=128)
        t3 = tmpr[:, :].rearrange("p (c t) -> p c t", t=128)
        for tt in range(nt // 512):
            c0 = tt * 4
            cs = slice(tt * 512, (tt + 1) * 512)
            ncol = 4 if samp else 512
            for k in range(8):
                P.op("pe", lambda e, k=k: e.matmul(ps0[0:4, 0:ncol], wgt8[:, k, 0:4], Xf(k, cs), start=(k == 0), stop=(k == 7)),
                     reads=["wgt8", Xres], writes=["ps0"], signal=(k == 7))
            for k in range(8):
                P.op("pe", lambda e, k=k: e.matmul(ps1[0:4, 0:ncol], wgt8[:, k, 4:8], Xf(k, cs), start=(k == 0), stop=(k == 7)),
                     reads=["wgt8", Xres], writes=["ps1"], signal=(k == 7))
            if samp:
                P.op("dve", lambda e: e.memset(IG[:], NEGPAD), writes=["IG"])
                P.op("dve", lambda e: e.memset(LF[:], 0.0), writes=["LF"])
                P.op("act", lambda e: e.activation(IG[:, 0:512:128], ps0[0:4, 0:4], AF.Identity, bias=b_i[:, 0:1]), reads=["ps0", "b_i", "IG"], writes=["IG"])
                P.op("act", lambda e: e.activation(tmpr[:, 0:4], ps1[0:4, 0:4], AF.Exp, bias=nb_f[:, 0:1], scale=-1.0), reads=["ps1", "nb_f"], writes=["tmpr"])
                P.op("act", lambda e: e.activation(tmpr[:, 0:4], tmpr[:, 0:4], AF.Ln, bias=1.0), reads=["tmpr"], writes=["tmpr"])
                P.op("dve", lambda e: e.tensor_scalar(LF[:, 0:512:128], tmpr[:, 0:4], -1.0, None, ALU.mult), reads=["tmpr", "LF"], writes=["LF"])
            else:
                P.op("act", lambda e: e.activation(IG[:, :], ps0[0:4, :], AF.Identity, bias=b_i[:, 0:1]), reads=["ps0", "b_i"], writes=["IG"])
                P.op("act", lambda e: e.activation(tmpr[:, :], ps1[0:4, :], AF.Exp, bias=nb_f[:, 0:1], scale=-1.0), reads=["ps1", "nb_f"], writes=["tmpr"])
                P.op("act", lambda e: e.activation(tmpr[:, :], tmpr[:, :], AF.Ln, bias=1.0), reads=["tmpr"], writes=["tmpr"])
                P.op("dve", lambda e: e.tensor_scalar(LF[:, :], tmpr[:, :], -1.0, None, ALU.mult), reads=["tmpr"], writes=["LF"])
            P.op("dve", lambda e: e.tensor_tensor_scan(Bc[:, :], ones[0:4, 0:512], LF[:, :], 0.0, ALU.mult, ALU.add),
                 reads=["LF", "ones"], writes=["Bc"])
            P.op("dve", lambda e: e.tensor_copy(b3[:, 0, :], B3[:, 0, :]), reads=["Bc"], writes=["bb"])
            P.op("dve", lambda e: e.tensor_tensor(b3[:, 1:4, :], B3[:, 1:4, :], B3[:, 0:3, 127:128].broadcast_to([4, 3, 128]), ALU.subtract),
                 reads=["Bc", "bb"], writes=["bb"])
            P.op("dve", lambda e: e.tensor_tensor(Aa[:, :], IG[:, :], bb[:, :], ALU.subtract), reads=["IG", "bb"], writes=["Aa"])
            for cl in range(4):
                c = c0 + cl
                P.op("dve", lambda e, c=c, cl=cl: e.tensor_tensor_scan(M3[:, cl, :], A3[:, cl, :], A3[:, cl, :], MP[:, c:c + 1], ALU.max, ALU.max),
                     reads=["Aa", "MP", "Mr"], writes=["Mr"])
                dst = MN[:, c:c + 1] if samp else MP[:, c + 1:c + 2]
                P.op("dve", lambda e, cl=cl, dst=dst: e.tensor_tensor(dst, b3[:, cl, 127:128], M3[:, cl, 127:128], ALU.add),
                     reads=["bb", "Mr", "MP", "MN"], writes=["MN" if samp else "MP"])
            if not samp:
                P.op("dve", lambda e, c0=c0: e.tensor_copy(MN[:, c0:c0 + 4], MP[:, c0 + 1:c0 + 5]), reads=["MP", "MN"], writes=["MN"])
            mpb = MP[:, c0:c0 + 4].unsqueeze(2).broadcast_to([4, 4, 128])
            mlb = M3[:, :, 127:128].broadcast_to([4, 4, 128])
            if full:
                P.op("dve", lambda e, cs=cs: e.tensor_scalar(M["NEGM"][:, cs], Mr[:, :], -1.0, None, ALU.mult), reads=["Mr", "NEGM"], writes=["NEGM"])
                P.op("dve", lambda e: e.tensor_tensor(t3, mpb, M3, ALU.subtract), reads=["MP", "Mr", "tmpr"], writes=["tmpr"])
                P.op("act", lambda e, cs=cs: e.activation(M["WI"][:, cs], tmpr[:, :], AF.Exp), reads=["tmpr", "WI"], writes=["WI"])
                P.op("dve", lambda e: e.tensor_tensor(tmpr[:, :], bb[:, :], Mr[:, :], ALU.add), reads=["bb", "Mr", "tmpr"], writes=["tmpr"])
                P.op("act", lambda e: e.activation(M["EM"][:, :], tmpr[:, :], AF.Exp, scale=-2.0), reads=["tmpr"], writes=["EM"])
            P.op("dve", lambda e: e.tensor_tensor(t3, A3, mlb, ALU.subtract), reads=["Aa", "Mr", "tmpr"], writes=["tmpr"])
            P.op("act", lambda e: e.activation(WK[:, :], tmpr[:, :], AF.Exp, bias=nln16[:, 0:1]), reads=["tmpr", "nln16"], writes=["WK"])
            P.op("dve", lambda e, c0=c0: e.tensor_tensor(DEC[:, c0:c0 + 4], MP[:, c0:c0 + 4], M3[:, :, 127], ALU.subtract), reads=["MP", "Mr", "DEC"], writes=["DEC"])
            P.op("act", lambda e, c0=c0: e.activation(DEC[:, c0:c0 + 4], DEC[:, c0:c0 + 4], AF.Exp), reads=["DEC"], writes=["DEC"])
            pc = ps0[:, 0:48].rearrange("p (c j h) -> p c j h", j=3, h=4)
            srcs = (Aa, WK, M["EM"]) if full else (Aa, WK)
            nj = len(srcs)
            n = 0
            for cl in range(4):
                for j in range(nj):
                    n += 1
                    P.op("pe", lambda e, cl=cl, j=j: e.transpose(pc[:, cl, j, :], srcs[j][0:4, cl * 128:(cl + 1) * 128], identf[0:4, 0:4]),
                         reads=["Aa", "WK", "EM", "identf"], writes=["ps0"], signal=(n == 4 * nj))
            P.op("act", lambda e, c0=c0: e.copy(M["COL"][:, c0:c0 + 4, 0:nj, :], pc[:, :, 0:nj, :]), reads=["ps0"], writes=["COL"])
            P.op("dve", lambda e, c0=c0: e.tensor_scalar(M["COL"][:, c0:c0 + 4, 0, :], M["COL"][:, c0:c0 + 4, 0, :], -2.7725887222, None, ALU.add), reads=["COL"], writes=["COL"])
            pd = ps1[:, 0:16].rearrange("p (h c) -> p h c", h=4)
            for h in range(4):
                P.op("pe", lambda e, h=h, c0=c0: e.matmul(pd[:, h, :], sel4[0:4, h, :], DEC[0:4, c0:c0 + 4], start=True, stop=True),
                     reads=["sel4", "DEC"], writes=["ps1"], signal=(h == 3))
            P.op("act", lambda e, c0=c0: e.copy(M["DECB"][:, :, c0:c0 + 4], pd), reads=["ps1"], writes=["DECB"])

    def mlstm_head(M, h, Xcols, Xres, mode, obT_dst=None, obT_res=None):
        nt, nch, full, nloc = M["nt"], M["nch"], M["full"], M["nloc"]
        samp = mode == "samp"
        wm = M["wm"]
        kT = M["kT"]; xpre = M["xpre"]; bnd = M["bnd"]; Vaug = M["Vaug"]; COL = M["COL"]; DECB = M["DECB"]
        nfc = 4 if full else 2
        W = 128 if samp else 512
        C3 = C32[:, h, :, :]
        C3 = C32[:, h, :, :]
        if mode == "hist":
            for fc in (0, 1):
                for k in range(8):
                    P.op("pe", lambda e, k=k, fc=fc: e.matmul(ps0[:, 0:128], wm[:, k, fc * 128:(fc + 1) * 128], Xcols(k, nt - 128, 128), start=(k == 0), stop=(k == 7)),
                         reads=["wm_q", Xres], writes=["ps0"], signal=(k == 7))
                P.op("act", lambda e, fc=fc: e.copy(cbnd[:, h, fc, :], ps0[:, 125:128]), reads=["ps0"], writes=["cbnd"])
        else:
            cst = M["ktmp"]
            npr = 4 if samp else 3
            lsel = (lambda k: xnSc[:, k, :]) if samp else (lambda k: Xcols(k, nt - 3, 3))
            for k in range(8):
                P.op("pe", lambda e, k=k: e.matmul(ps0[0:npr, :], lsel(k), wm[:, k, 0:512], start=(k == 0), stop=(k == 7)),
                     reads=["wm_q", "wm_k", Xres], writes=["ps0"], signal=(k == 7))
            P.op("act", lambda e: e.copy(cst[0:npr, :], ps0[0:npr, :]), reads=["ps0"], writes=["ktmp"])
            if samp:
                P.dma("sp", O["convs"][:, 2, h * 256:(h + 1) * 256], cst[0:4, 0:256], reads=["ktmp"], writes=["convsout"])
                P.dma("sp", O["convs"][:, 2, 1024 + h * 256:1024 + (h + 1) * 256], cst[0:4, 256:512], reads=["ktmp"], writes=["convsout"])
            else:
                P.dma("sp", O["convp"][:, h * 256:(h + 1) * 256], cst[0:3, 0:256], reads=["ktmp"], writes=["convpout"])
                P.dma("sp", O["convp"][:, 1024 + h * 256:1024 + (h + 1) * 256], cst[0:3, 256:512], reads=["ktmp"], writes=["convpout"])
        for hf in range(nt // nloc):
            def b_chunks(cls):
                for cl in cls:
                    c = hf * (nloc // 128) + cl
                    pz, pzr = (ps0, "ps0") if cl % 2 == 0 else (ps1, "ps1")
                    ncols = 512 if full else 256
                    for k in range(8):
                        P.op("pe", lambda e, k=k, c=c, pz=pz: e.matmul(pz[:, 0:ncols], Xcols(k, c * 128, 128), wm[:, k, 512:512 + ncols], start=(k == 0), stop=(k == 7)),
                             reads=["wm_v", "wm_o", Xres], writes=[pzr], signal=(k == 7))
                    P.op("act", lambda e, cl=cl, pz=pz: e.copy(Vaug[:, cl, 0:256], pz[:, 0:256]), reads=[pzr], writes=["Vaug"])
                    if full:
                        P.op("act", lambda e, pz=pz, cl=cl: e.activation(M["G"][:, cl, :], pz[:, 256:512], AF.Sigmoid), reads=[pzr], writes=["G"])

            for tl in range(nloc // W):
                tt = hf * (nloc // W) + tl
                t0 = tt * W
                l0 = tl * W
                for fi in range(nfc):
                    fc = fi + (4 - nfc)
                    chunk = (fc // 2) * 8 + h * 2 + fc % 2
                    pz, pzr = (ps0, "ps0") if fi % 2 == 0 else (ps1, "ps1")
                    xp = xpre[:, fi, :]
                    xr = "xpre%d" % fi
                    wres = "wm_q" if fc < 2 else "wm_k"
                    if samp:
                        P.op("dve", lambda e, xp=xp, chunk=chunk, tt=tt: e.tensor_copy(xp[:, 0:3], M["scT"][:, tt, chunk, :]), reads=["scT"], writes=[xr])
                    elif tt > 0:
                        P.op("dve", lambda e, fi=fi, xp=xp: e.tensor_copy(bnd[:, fi, :], xp[:, 512:515]), reads=[xr], writes=["bnd%d" % fi])
                        P.op("dve", lambda e, fi=fi, xp=xp: e.tensor_copy(xp[:, 0:3], bnd[:, fi, :]), reads=["bnd%d" % fi], writes=[xr])
                    elif mode == "hist":
                        P.op("dve", lambda e, xp=xp: e.memset(xp[:, 0:3], 0.0), writes=[xr])
                    else:
                        P.op("dve", lambda e, xp=xp, fc=fc: e.tensor_scalar(xp[:, 0:3], cbnd[:, h, fc, :], flag[:, 0:1], None, ALU.mult),
                             reads=["cbnd", "flag"], writes=[xr])
                    for k in range(8):
                        P.op("pe", lambda e, k=k, pz=pz, fc=fc, t0=t0: e.matmul(pz[:, 0:W], wm[:, k, fc * 128:(fc + 1) * 128], Xcols(k, t0, W), start=(k == 0), stop=(k == 7)),
                             reads=[wres, Xres], writes=[pzr], signal=(k == 7))
                    P.op("act", lambda e, pz=pz, xp=xp: e.copy(xp[:, 3:3 + W], pz[:, 0:W]), reads=[pzr], writes=[xr])
                b_chunks(range(tl * (W // 128), (tl + 1) * (W // 128)))
                for fi in range(nfc):
                    fc = fi + (4 - nfc)
                    chunk = (fc // 2) * 8 + h * 2 + fc % 2
                    xp = xpre[:, fi, :]
                    xr = "xpre%d" % fi
                    ca = M["cacc"][fi % 2]; car = "cacc%d" % (fi % 2)
                    P.op("dve", lambda e, xp=xp, chunk=chunk, ca=ca: e.tensor_scalar(ca[:, 0:W], xp[:, 3:3 + W], cw[:, chunk, 3:4], cb[:, chunk:chunk + 1], ALU.mult, ALU.add),
                         reads=[xr, "cw", "cb"], writes=[car])
                    for j in (2, 1, 0):
                        P.op("dve", lambda e, xp=xp, chunk=chunk, j=j, ca=ca: e.scalar_tensor_tensor(ca[:, 0:W], xp[:, j:j + W], cw[:, chunk, j:j + 1], ca[:, 0:W], ALU.mult, ALU.add),
                             reads=[xr, "cw", car], writes=[car])
                    if fc < 2:
                        P.op("act", lambda e, fc=fc, l0=l0, ca=ca: e.activation(M["qT"][:, fc, l0:l0 + W], ca[:, 0:W], AF.Silu), reads=[car], writes=["qT"])
                    else:
                        P.op("act", lambda e, ca=ca, fc=fc, l0=l0: e.activation(kT[:, fc - 2, l0:l0 + W], ca[:, 0:W], AF.Silu), reads=[car], writes=["kT"])
                    if mode == "hist" and t0 + W == nt:
                        P.op("dve", lambda e, xp=xp, fc=fc: e.tensor_copy(cbnd[:, h, fc, :], xp[:, 512:515]), reads=[xr], writes=["cbnd"])
            nlc = nloc // 128
            for cl in range(nlc):
                c = hf * nlc + cl
                ls = slice(cl * 128, (cl + 1) * 128)
                gs = slice(c * 128, (c + 1) * 128)
                par = cl % 2
                if full:
                    qT = M["qT"]
                    pS, pSr = (psA[:, 0:128], "psA0") if par == 0 else (ps6[:, 0:128], "ps6")
                    pB, pBr = (psA[:, 512:640], "psA1") if par == 0 else (ps7[:, 0:128], "ps7")
                    pW, pWr = (ps0[:, 0:128], "ps0") if par == 0 else (ps1[:, 0:128], "ps1")
                    Dt = M["Dt"][par]
                    for k2 in range(2):
                        P.op("pe", lambda e, k2=k2: e.matmul(pS, kT[:, k2, ls], qT[:, k2, ls], start=(k2 == 0), stop=(k2 == 1)),
                             reads=["kT", "qT"], writes=[pSr], signal=(k2 == 1))
                    P.op("pe", lambda e: e.matmul(pB, sel4[0:4, h, :], M["NEGM"][0:4, gs], start=True, stop=False),
                         reads=["sel4", "NEGM"], writes=[pBr], signal=False)
                    P.op("pe", lambda e: e.matmul(pB, ident[:, :], mbias[:, :], start=False, stop=True),
                         reads=["ident", "mbias"], writes=[pBr])
                    P.op("pe", lambda e: e.matmul(pW, sel4[0:4, h, :], M["WI"][0:4, gs], start=True, stop=True),
                         reads=["sel4", "WI"], writes=[pWr])
                    P.op("act", lambda e, c=c: e.activation(Dt[:, :], pB, AF.Exp, bias=COL[:, c, 0, h:h + 1]), reads=[pBr, "COL"], writes=["Dt%d" % par])
                    P.op("dve", lambda e, cl=cl: e.tensor_tensor(M["wTa"][:, cl, :], Dt[:, :], pS, ALU.mult), reads=["Dt%d" % par, pSr], writes=["wT%d" % cl])
                    for k2 in range(2):
                        P.op("dve", lambda e, k2=k2, cl=cl: e.tensor_tensor(M["qpTa"][:, cl, k2, :], qT[:, k2, ls], pW, ALU.mult), reads=["qT", pWr], writes=["qpT%d" % cl])
                for k2 in range(2):
                    P.op("pe", lambda e, k2=k2: e.transpose(psT[:, par * 256 + k2 * 128:par * 256 + (k2 + 1) * 128], kT[:, k2, ls], ident[:, :]),
                         reads=["kT", "ident"], writes=["psT"], signal=(k2 == 1))
                P.op("act", lambda e, c=c, cl=cl: e.activation(M["kwa"][:, cl, :], psT[:, par * 256:(par + 1) * 256], AF.Copy, scale=COL[:, c, 1, h:h + 1]),
                     reads=["psT", "COL"], writes=["kw%d" % cl])
            def stageA(cl):
                c = hf * nlc + cl
                i = cl % 2
                if samp:
                    C3 = M["Csb"][i][:, :, :]
                    C3r = "Csb%d" % i
                    P.dma("sp", C3[:, :, 0:256], I["sC"][c, h, :, :].rearrange("(k p) v -> p k v", p=128), writes=[C3r])
                    P.dma("sp", C3[:, :, 256:257], I["sn"][c, h, :].rearrange("(k p o) -> p k o", p=128, o=1), writes=[C3r], allow_slow_non_contiguous=True)
                else:
                    C3 = C32[:, h, :, :]
                    C3r = "C32"
                pCs = ((ps6, "ps6"), (ps7, "ps7")) if i == 0 else ((ps0, "ps0"), (ps1, "ps1"))
                for k2, (pC, pCr) in enumerate(pCs):
                    P.op("pe", lambda e, k2=k2, pC=pC: e.matmul(pC[:, 0:257], M["kwa"][:, cl, k2 * 128:(k2 + 1) * 128], Vaug[:, cl, :], start=True, stop=True),
                         reads=["kw%d" % cl, "Vaug"], writes=[pCr])
                if full:
                    Cbf = M["Cbf"][i]; Cbr = "Cbf%d" % i
                    pN, pNr = (ps4[:, 0:257], "ps4") if i == 0 else (psA[:, 0:257], "psA0")
                    P.op("act", lambda e: e.copy(Cbf[:, :, :], C3), reads=[C3r], writes=[Cbr])
                    P.op("pe", lambda e: e.matmul(pN, M["wTa"][:, cl, :], Vaug[:, cl, :], start=True, stop=False), reads=["wT%d" % cl, "Vaug"], writes=[pNr], signal=False)
                    P.op("pe", lambda e: e.matmul(pN, M["qpTa"][:, cl, 0, :], Cbf[:, 0, :], start=False, stop=False), reads=["qpT%d" % cl, Cbr], writes=[pNr], signal=False)
                    P.op("pe", lambda e: e.matmul(pN, M["qpTa"][:, cl, 1, :], Cbf[:, 1, :], start=False, stop=True), reads=["qpT%d" % cl, Cbr], writes=[pNr])
                for k2, (pC, pCr) in enumerate(pCs):
                    P.op("dve", lambda e, k2=k2, pC=pC: e.scalar_tensor_tensor(C3[:, k2, :], C3[:, k2, :], DECB[:, h, c:c + 1], pC[:, 0:257], ALU.mult, ALU.add),
                         reads=[C3r, "DECB", pCr], writes=[C3r])
                if samp:
                    P.dma("sp", O["Cs"][c, h, :, :].rearrange("(k p) v -> p k v", p=128), C3[:, :, 0:256], reads=[C3r], writes=["Csout"])
                    P.dma("sp", O["ns"][c, h, :].rearrange("(k p o) -> p k o", p=128, o=1), C3[:, :, 256:257], reads=[C3r], writes=["nsout"], allow_slow_non_contiguous=True)

            def stageB(cl):
                c = hf * nlc + cl
                i = cl % 2
                sm = M["sm"][i]; jk = M["jk"][i]; ob = M["ob"][i]
                smr = "sm%d" % i
                pN, pNr = (ps4[:, 0:257], "ps4") if i == 0 else (psA[:, 0:257], "psA0")
                P.op("act", lambda e: e.activation(jk[:, :], pN[:, 0:256], AF.Square, scale=1.0 / 16.0, accum_out=sm[:, 0:1]),
                     reads=[pNr], writes=["jk%d" % i, smr])
                P.op("dve", lambda e: e.scalar_tensor_tensor(sm[:, 1:2], pN[:, 256:257], pN[:, 256:257], COL[:, c, 2, h:h + 1], ALU.mult, ALU.max),
                     reads=[pNr, "COL", smr], writes=[smr])
                P.op("dve", lambda e: e.scalar_tensor_tensor(sm[:, 2:3], sm[:, 1:2], EPS, sm[:, 0:1], ALU.mult, ALU.add), reads=[smr], writes=[smr])
                P.op("act", lambda e: e.activation(sm[:, 2:3], sm[:, 2:3], AF.Sqrt), reads=[smr], writes=[smr])
                P.op("dve", lambda e: e.reciprocal(sm[:, 3:4], sm[:, 2:3]), reads=[smr], writes=[smr])
                P.op("dve", lambda e: e.scalar_tensor_tensor(ob[:, :], pN[:, 0:256], sm[:, 3:4], M["G"][:, cl, :], ALU.mult, ALU.mult),
                     reads=[pNr, smr, "G"], writes=["ob%d" % i])

            def stageC(cl):
                c = hf * nlc + cl
                i = cl % 2
                ob = M["ob"][i]
                gs = slice(c * 128, (c + 1) * 128)
                for j in range(2):
                    P.op("pe", lambda e, j=j: e.transpose(psT[:, 512 + i * 256 + j * 128:512 + i * 256 + (j + 1) * 128], ob[:, j * 128:(j + 1) * 128], ident[:, :]),
                         reads=["ob%d" % i, "ident"], writes=["psT"], signal=(j == 1))
                pob = psT[:, 512 + i * 256:512 + (i + 1) * 256].rearrange("p (j t) -> p j t", j=2)
                for j in range(2):
                    mcol = M["mhn"][:, h * 2 + j:h * 2 + j + 1]
                    if samp:
                        P.op("act", lambda e, j=j, mcol=mcol: e.activation(obT_dst[:, h * 2 + j, c:c + 1], pob[:, j, 0:1], AF.Copy, scale=mcol), reads=["psT", "mhncol"], writes=[obT_res])
                    else:
                        P.op("act", lambda e, j=j, mcol=mcol: e.activation(obT_dst[:, h * 2 + j, gs], pob[:, j, :], AF.Copy, scale=mcol), reads=["psT", "mhncol"], writes=[obT_res])

            for step in range(nlc + 2):
                if step < nlc:
                    stageA(step)
                if full and 0 <= step - 1 < nlc:
                    stageB(step - 1)
                if full and 0 <= step - 2 < nlc:
                    stageC(step - 2)
        if mode == "hist":
            P.op("dve", lambda e: e.tensor_scalar(C3, C3, flag[:, 0:1], None, ALU.mult), reads=["C32", "flag"], writes=["C32"])
        elif mode == "own":
            P.dma("sp", O["Cp"][h, :, :].rearrange("(k p) v -> p k v", p=128), C3[:, :, 0:256], reads=["C32"], writes=["Cpout"])
            P.dma("sp", O["np"][h, :].rearrange("(k p o) -> p k o", p=128, o=1), C3[:, :, 256:257], reads=["C32"], writes=["npout"], allow_slow_non_contiguous=True)

    def load_wm(M, h):
        for j, off in enumerate((OFF_MQ, OFF_MK, OFF_MV, OFF_MO)):
            load_w(M["wm"][:, :, j * 256:(j + 1) * 256], I["w_in_b"][9 + h * 4 + j, :, :, :], ("wm_q", "wm_k", "wm_v", "wm_o")[j])

    xn_cols = lambda k, s, n: xnT[:, k, s:s + n]

    P.push()
    M = mlstm_scope(NT, False)
    P.op("dve", lambda e: e.memset(M["MP"][:, 0:1], 0.0), writes=["MP"])
    mlstm_gates(M, lambda k, cs: xnT[:, k, cs], "xnT", "hist")
    P.op("dve", lambda e: e.tensor_scalar(MPh[:, :], M["MP"][:, 16:17], flag[0:4, 0:1], None, ALU.mult), reads=["MP", "flag"], writes=["MPh"])
    for h in range(4):
        for g_ in range(3):
            W_ = WINS[g_]
            P.dma("sp", O["kvs%d" % g_][h, 0:W_ - 1, :], I["ck%d" % g_][h, 1:W_, :], writes=["kvscopy"])
        P.dma("sp", O["convs"][h, 0:2, :], I["sconv"][h, 1:3, :], writes=["convscopy"])
        load_wm(M, h)
        mlstm_head(M, h, xn_cols, "xnT", "hist")
    P.pop()
    phase_done(3)

    phase_norm(NT, False)
    phase_done(4)

    P.push()
    A = attn_scope()
    KTo = P.sb("KTo", [128, 2, 16, 128], BF16)
    QTo = P.sb("QTo", [128, 2, 16, 128], BF16)
    VAo = P.sb("VAo", [128, 16, 4, 65], BF16)
    acc = P.sb("acc", [65, 4, NT], F32)
    accs = P.sb("accs", [65, 4, 4], F32)
    PT = [P.sb("PT%d" % i, [128, 2, 2, 2, 128], BF16) for i in range(2)]
    ckst = [P.sb("ckst%d" % i, [128, 512], F32) for i in range(2)]
    ckb = [P.sb("ckb%d" % i, [128, 256], BF16) for i in range(2)]
    KTs = [P.sb("KTs%d" % i, [128, 2, 128], BF16) for i in range(2)]
    VAs = [P.sb("VAs%d" % i, [128, 4, 65], BF16) for i in range(2)]
    KTn = [P.sb("KTn%d" % i, [128, 2, 128], BF16) for i in range(2)]
    QTn = [P.sb("QTn%d" % i, [128, 2, 128], BF16) for i in range(2)]
    VAn = [P.sb("VAn%d" % i, [128, 4, 65], BF16) for i in range(2)]
    xnSb = [P.sb("xnSb%d" % i, [128, 8, 128], BF16) for i in range(2)]
    P.op("dve", lambda e: e.memset(VAo[:], 1.0), writes=["VAo%d" % k for k in range(16)])
    for i in range(2):
        P.op("dve", lambda e, i=i: e.memset(VAs[i][:], 1.0), writes=["VAs%d" % i])
        P.op("dve", lambda e, i=i: e.memset(VAn[i][:], 1.0), writes=["VAn%d" % i])
        P.op("dve", lambda e, i=i: e.memset(xnSb[i][:], 0.0), writes=["xnSb%d" % i])
    def own_item(g, s, r, it):
        d = DILS[g]
        span = 128 * d
        nsp = NT // span
        ob_ = s * d + r
        start = s * span + r
        kv_out = O["kvp%d" % g][ssl(r, 128, d), :] if s == nsp - 1 else None
        if s == 0:
            hb = HB[g] + r
            KTp, KTpr, VAp, VApr, msk = KTh[:, :, hb, :], "KTh%d" % hb, VAh[:, hb, :, :], "VAh", mask2f
        else:
            pb_ = ob_ - d
            KTp, KTpr, VAp, VApr, msk = KTo[:, :, pb_, :], "KTo%d" % pb_, VAo[:, pb_, :, :], "VAo%d" % pb_, mask2

        def s1():
            if s == 0 and r == 0:
                load_wg(A, g)
            attn_p1(A, it, lambda k: xnT[:, k, ssl(start, 128, d)], "xnT", VAo[:, ob_, :, :], "VAo%d" % ob_)

        def s2():
            attn_p2(A, it, rope[:, RB[g] + d + ob_, :], "rope", kv_out=kv_out)

        def s3():
            attn_p3(A, it, KTo[:, :, ob_, :], "KTo%d" % ob_, QTo[:, :, ob_, :], "QTo%d" % ob_)

        def s4():
            attn_c1(it, QTo[:, :, ob_, :], "QTo%d" % ob_, KTp, KTpr, KTo[:, :, ob_, :], "KTo%d" % ob_, PT)

        def s5():
            attn_c2(it, VAp, VApr, VAo[:, ob_, :, :], "VAo%d" % ob_, msk, PT, acc[:, :, ssl(start, 128, d)], "acc", g == 0, 128)
        return (s1, s2, s3, s4, s5)

    def samp_seq(g, b, it, its):
        d = DILS[g]
        i = its % 2
        ck = I["ck%d" % g]
        kvs = O["kvs%d" % g]
        P.dma("sp", ckst[i][:, :], ck[b, ssl(0, 128, d), :], writes=["ckst%d" % i])
        P.op("act", lambda e: e.copy(VAs[i][:, :, 0:64], ckst[i][:, 256:512].rearrange("p (h d) -> p h d", d=64)), reads=["ckst%d" % i], writes=["VAs%d" % i])
        P.op("dve", lambda e: e.tensor_copy(ckb[i][:, :], ckst[i][:, 0:256]), reads=["ckst%d" % i], writes=["ckb%d" % i])
        for j in range(2):
            P.op("pe", lambda e, j=j: e.transpose(psT[:, 512 + j * 128:512 + (j + 1) * 128], ckb[i][:, j * 128:(j + 1) * 128], ident[:, :]),
                 reads=["ckb%d" % i, "ident"], writes=["psT"], signal=(j == 1))
        P.op("act", lambda e: e.copy(KTs[i][:, :, :], psT[:, 512:768].rearrange("p (j t) -> p j t", j=2)), reads=["psT"], writes=["KTs%d" % i])
        P.op("dve", lambda e: e.tensor_copy(xnSb[i][:, :, 0:1], xnSc[:, :, b:b + 1]), reads=["xnSc", "xnSb%d" % i], writes=["xnSb%d" % i])
        attn_p1(A, it, lambda k: xnSb[i][:, k, :], "xnSb%d" % i, VAn[i][:, :, :], "VAn%d" % i)
        attn_p2(A, it, rope_s[:, :], "rope_s", kv_out=kvs[b, WINS[g] - 1:WINS[g], :], kv_rows=1)
        attn_p3(A, it, KTn[i][:, :, :], "KTn%d" % i, QTn[i][:, :, :], "QTn%d" % i)
        attn_c1(it, QTn[i][:, :, :], "QTn%d" % i, KTs[i][:, :, :], "KTs%d" % i, KTn[i][:, :, :], "KTn%d" % i, PT)
        attn_c2(it, VAs[i][:, :, :], "VAs%d" % i, VAn[i][:, :, :], "VAn%d" % i, mask2, PT, accs[:, :, b:b + 1], "accs", g == 0, 1)

    it = 0
    its = 0
    for g in range(3):
        d = DILS[g]
        items = []
        for s in range(NT // (128 * d)):
            for r in range(d):
                items.append(own_item(g, s, r, it))
                it += 1
        run_pipeline(items)
        for b in range(4):
            samp_seq(g, b, it, its)
            it += 1
            its += 1
    phase_done(4.5)
    rd = [P.sb("rd%d" % i, [64, 512], F32) for i in range(2)]
    n = 0
    for h in range(4):
        for tt in range(4):
            cs = slice(tt * 512, (tt + 1) * 512)
            i = n % 2
            pz, pzr = (ps0, "ps0") if i == 0 else (ps1, "ps1")
            P.op("pe", lambda e, h=h, cs=cs, pz=pz: e.matmul(pz[0:64, :], ones[64:65, 0:64], acc[64:65, h, cs], start=True, stop=True),
                 reads=["ones", "acc"], writes=[pzr])
            P.op("dve", lambda e, pz=pz, i=i: e.reciprocal(rd[i][:, :], pz[0:64, :]), reads=[pzr], writes=["rd%d" % i])
            P.op("dve", lambda e, h=h, cs=cs, i=i: e.tensor_tensor(oaT[:, h, cs], acc[0:64, h, cs], rd[i][:, :], ALU.mult), reads=["acc", "rd%d" % i], writes=["oaT"])
            n += 1
    for h in range(4):
        P.op("pe", lambda e, h=h: e.matmul(ps0[0:64, 0:4], ones[64:65, 0:64], accs[64:65, h, :], start=True, stop=True), reads=["ones", "accs"], writes=["ps0"])
        P.op("dve", lambda e: e.reciprocal(rd[0][:, 0:4], ps0[0:64, 0:4]), reads=["ps0"], writes=["rd0"])
        P.op("dve", lambda e, h=h: e.tensor_tensor(oaTs[:, h, :], accs[0:64, h, :], rd[0][:, 0:4], ALU.mult), reads=["accs", "rd0"], writes=["oaTs"])
    P.pop()
    P.pop()
    phase_done(5)

    P.push()
    obT = P.sb("obT", [128, 8, NT], BF16)
    obTs = P.sb("obTs", [128, 8, 4], BF16)
    P.push()
    M = mlstm_scope(NT, True)
    P.op("dve", lambda e: e.tensor_copy(M["MP"][:, 0:1], MPh[:, :]), reads=["MPh"], writes=["MP"])
    mlstm_gates(M, lambda k, cs: xnT[:, k, cs], "xnT", "own")
    P.dma("sp", O["mp"][:, :], M["MP"][:, 16:17], reads=["MP"], writes=["mpout"])
    for h in range(4):
        load_wm(M, h)
        mlstm_head(M, h, xn_cols, "xnT", "own", obT, "obT")
    P.pop()
    phase_done(6)
    P.push()
    M = mlstm_scope(512, True)
    xnS = P.sb("xnS", [128, 8, 512], BF16)
    P.op("dve", lambda e: e.memset(xnS[:], 0.0), writes=["xnS"])
    P.op("dve", lambda e: e.tensor_copy(xnS[:, :, 0:512:128], xnSc[:, :, :]), reads=["xnSc", "xnS"], writes=["xnS"])
    P.dma("sp", M["MP"][:, 0:4], I["smT"][:, :], writes=["MP"])
    M["scT"] = P.sb("scT", [128, 4, 16, 3], F32)
    P.dma("sp", M["scT"][:], I["sconvT"][:, :, :, :], writes=["scT"])
    mlstm_gates(M, lambda k, cs: xnSc[:, k, :], "xnSc", "samp")
    P.dma("sp", O["ms"][:, :], M["MN"][:, :], reads=["MN"], writes=["msout"])
    for h in range(4):
        load_wm(M, h)
        mlstm_head(M, h, lambda k, s, n: xnS[:, k, s:s + n], "xnS", "samp", obTs, "obTs")
    P.pop()
    phase_done(7)

    P.push()
    wga = P.sb("wga", [128, 8, 2048], BF16)
    wpb = P.sb("wpb", [128, 8, D], BF16)
    wpa = P.sb("wpa", [64, 4, D], BF16)
    mixst = P.sb("mixst", [128, 8, 512], BF16)
    sga = [P.sb("sga%d" % i, [128, 512], F32) for i in range(2)]
    sgb = [P.sb("sgb%d" % i, [128, 512], F32) for i in range(2)]
    t1 = [P.sb("t1_%d" % i, [128, 512], F32) for i in range(2)]
    for j in range(8):
        load_w(wga[:, :, j * 256:(j + 1) * 256], I["w_in_b"][25 + j, :, :, :], "wga")
    load_w(wpb[:, :, :], I["w_pb"].rearrange("(k p) c -> p k c", p=128), "wpb")
    load_w(wpa[:, :, :], I["w_pa"].rearrange("(h p) c -> p h c", p=64), "wpa")
    n = 0
    for tt in range(5):
        if tt < 4:
            N = 512
            xs_ = lambda k, tt=tt: xnT[:, k, tt * 512:(tt + 1) * 512]
            oas = lambda h, tt=tt: oaT[:, h, tt * 512:(tt + 1) * 512]
            obs = lambda k, tt=tt: obT[:, k, tt * 512:(tt + 1) * 512]
            rr = ["xnT", "oaT", "obT"]
        else:
            N = 4
            xs_ = lambda k: xnSc[:, k, :]
            oas = lambda h: oaTs[:, h, :]
            obs = lambda k: obTs[:, k, :]
            rr = ["xnSc", "oaTs", "obTs"]
        for c in range(8):
            i = n % 2
            for k in range(8):
                P.op("pe", lambda e, k=k, c=c: e.matmul(ps0[:, 0:N], wga[:, k, c * 128:(c + 1) * 128], xs_(k), start=(k == 0), stop=(k == 7)),
                     reads=["wga"] + rr, writes=["ps0"], signal=(k == 7))
            for k in range(8):
                P.op("pe", lambda e, k=k, c=c: e.matmul(ps1[:, 0:N], wga[:, k, 1024 + c * 128:1024 + (c + 1) * 128], xs_(k), start=(k == 0), stop=(k == 7)),
                     reads=["wga"] + rr, writes=["ps1"], signal=(k == 7))
            for h in range(4):
                P.op("pe", lambda e, h=h, c=c: e.matmul(ps6[:, 0:N], wpa[:, h, c * 128:(c + 1) * 128], oas(h), start=(h == 0), stop=(h == 3)),
                     reads=["wpa"] + rr, writes=["ps6"], signal=(h == 3))
            for k in range(8):
                P.op("pe", lambda e, k=k, c=c: e.matmul(ps7[:, 0:N], wpb[:, k, c * 128:(c + 1) * 128], obs(k), start=(k == 0), stop=(k == 7)),
                     reads=["wpb"] + rr, writes=["ps7"], signal=(k == 7))
            P.op("act", lambda e, i=i: e.activation(sga[i][:, 0:N], ps0[:, 0:N], AF.Sigmoid), reads=["ps0"], writes=["sga%d" % i])
            P.op("act", lambda e, i=i: e.activation(sgb[i][:, 0:N], ps1[:, 0:N], AF.Sigmoid), reads=["ps1"], writes=["sgb%d" % i])
            P.op("dve", lambda e, i=i: e.tensor_tensor(t1[i][:, 0:N], sga[i][:, 0:N], ps6[:, 0:N], ALU.mult), reads=["sga%d" % i, "ps6"], writes=["t1_%d" % i])
            P.op("dve", lambda e, i=i: e.tensor_tensor(sgb[i][:, 0:N], sgb[i][:, 0:N], ps7[:, 0:N], ALU.mult), reads=["sgb%d" % i, "ps7"], writes=["sgb%d" % i])
            if tt < 4:
                dst, dres = mixst[:, c, :], "mixst"
            else:
                dst, dres = mixS[:, c, :], "mixS"
            P.op("dve", lambda e, i=i, dst=dst: e.tensor_tensor(dst, t1[i][:, 0:N], sgb[i][:, 0:N], ALU.add), reads=["t1_%d" % i, "sgb%d" % i], writes=[dres])
            n += 1
        if tt < 4:
            P.op("act", lambda e, tt=tt: e.copy(xnT[:, :, tt * 512:(tt + 1) * 512], mixst[:, :, :]), reads=["mixst"], writes=["xnT"])
    P.pop()
    P.pop()
    P.pop()
    phase_done(8)

    P.push()
    mixT = xnT
    wout = P.sb("wout", [128, 8, D], BF16)
    wpg = P.sb("wpg", [128, 8, D], BF16)
    wpp = P.sb("wpp", [128, 2, D], BF16)
    gffn = P.sb("gffn", [128, D], F32)
    gple = P.sb("gple", [128, D], F32)
    gfin = P.sb("gfin", [128, D], F32)
    h32 = [P.sb("h32_%d" % i, [128, D], F32) for i in range(4)]
    xfT = P.sb("xfT", [128, 8, 512], BF16)
    hidT = P.sb("hidT", [128, NJ, 512], BF16)
    wgu = [P.sb("wgu%d" % i, [128, 8, 256], BF16) for i in range(5)]
    wdn = [P.sb("wdn%d" % i, [128, D], BF16) for i in range(6)]
    sgu = [P.sb("sgu%d" % i, [128, 512], BF16) for i in range(2)]
    sga2 = [P.sb("sga2_%d" % i, [128, 512], F32) for i in range(2)]
    pst = [P.sb("pst%d" % i, [128, 256], F32) for i in range(2)]
    pbf = [P.sb("pbf%d" % i, [128, 256], BF16) for i in range(2)]
    ppT = P.sb("ppT", [128, 2, 512], BF16)
    st = make_stage(with_xt=False)
    ss2 = [P.sb("ss2_%d" % i, [128, 1], F32) for i in range(2)]
    rs2 = [P.sb("rs2_%d" % i, [128, 1], F32) for i in range(2)]
    load_w(wout[:, :, :], I["w_out"].rearrange("(k p) c -> p k c", p=128), "wout")
    load_w(wpg[:, :, :], I["w_pg"].rearrange("(k p) c -> p k c", p=128), "wpg")
    load_w(wpp[:, :, :], I["w_pp"].rearrange("(k p) c -> p k c", p=128), "wpp")
    P.dma("sp", gffn[:], I["g_ffn"][0, :].partition_broadcast(128), writes=["gffn"])
    P.dma("sp", gple[:], I["g_ple"][0, :].partition_broadcast(128), writes=["gple"])
    P.dma("sp", gfin[:], I["g_fin"][0, :].partition_broadcast(128), writes=["gfin"])
    wdn_v = I["w_down"].rearrange("(j p) c -> p j c", p=128)
    nit = 0
    for tt in range(5):
        samp = (tt == 4)
        nb = 1 if samp else 4
        npp = 4 if samp else 128
        ncol = 4 if samp else 512
        def res_item(tb, ni):
            hh = h32[tb]; hr = "h32_%d" % tb
            i = ni % 2
            t = "n%d" % i
            _, junk, ss, rs, xnb = st

            def r1():
                if samp:
                    P.dma("sp", hh[0:4, :], I["xs"][0:4, :], writes=[hr])
                    lhs = lambda k: mixS[:, k, :]
                    lres = "mixS"
                else:
                    r0 = NT + tt * 512 + tb * 128
                    P.dma("sp", hh[:, :], I["x2"][r0:r0 + 128, :], writes=[hr])
                    lhs = lambda k: mixT[:, k, tt * 512 + tb * 128: tt * 512 + (tb + 1) * 128]
                    lres = "xnT"
                for half in range(2):
                    pz, pzr = banks[(tb % 2) * 2 + half]
                    for k in range(8):
                        P.op("pe", lambda e, k=k, pz=pz, half=half: e.matmul(pz[0:npp, :], lhs(k), wout[:, k, half * 512:(half + 1) * 512], start=(k == 0), stop=(k == 7)),
                             reads=[lres, "wout"], writes=[pzr], signal=(k == 7))
                    P.op("dve", lambda e, pz=pz, half=half: e.tensor_tensor(hh[0:npp, half * 512:(half + 1) * 512], hh[0:npp, half * 512:(half + 1) * 512], pz[0:npp, :], ALU.add),
                         reads=[hr, pzr], writes=[hr])

            def r2():
                rms_rstd(hh[0:npp, :], junk[i][0:npp, :], ss[i][0:npp, :], rs[i][0:npp, :], [hr], t)
                P.op("dve", lambda e: e.scalar_tensor_tensor(xnb[i][0:npp, :], hh[0:npp, :], rs[i][0:npp, 0:1], gffn[0:npp, :], ALU.mult, ALU.mult),
                     reads=[hr, "rs" + t, "gffn"], writes=["xnb%d" % i])

            def r3():
                for k in range(8):
                    P.op("pe", lambda e, k=k: e.transpose(psT[:, k * 128:k * 128 + npp], xnb[i][0:npp, k * 128:(k + 1) * 128], ident[0:npp, 0:npp]),
                         reads=["xnb%d" % i, "ident"], writes=["psT"], signal=(k == 7))
                P.op("act", lambda e: e.copy(xfT[:, :, tb * 128:tb * 128 + npp], psT[:, :].rearrange("p (k t) -> p k t", k=8)[:, :, 0:npp]),
                     reads=["psT"], writes=["xfT"])
            return (r1, r2, r3)

        items = []
        for tb in range(nb):
            items.append(res_item(tb, nit))
            nit += 1
        run_pipeline(items)
        for j in range(NJ):
            ws = j % 5
            load_w(wgu[ws][:, :, :], I["wgu_b"][j, :, :, :], "wgu%d" % ws)
            i = j % 2
            (pga, pgar), (pua, puar) = (banks[0], banks[1]) if i == 0 else (banks[2], banks[3])
            for k in range(8):
                P.op("pe", lambda e, k=k, ws=ws, pga=pga: e.matmul(pga[:, 0:ncol], wgu[ws][:, k, 0:128], xfT[:, k, 0:ncol], start=(k == 0), stop=(k == 7)),
                     reads=["wgu%d" % ws, "xfT"], writes=[pgar], signal=(k == 7))
            for k in range(8):
                P.op("pe", lambda e, k=k, ws=ws, pua=pua: e.matmul(pua[:, 0:ncol], wgu[ws][:, k, 128:256], xfT[:, k, 0:ncol], start=(k == 0), stop=(k == 7)),
                     reads=["wgu%d" % ws, "xfT"], writes=[puar], signal=(k == 7))
            P.op("act", lambda e, i=i, pga=pga: e.activation(sgu[i][:, 0:ncol], pga[:, 0:ncol], AF.Silu), reads=[pgar], writes=["sgu%d" % i])
            P.op("dve", lambda e, i=i, j=j, pua=pua: e.tensor_tensor(hidT[:, j, 0:ncol], sgu[i][:, 0:ncol], pua[:, 0:ncol], ALU.mult), reads=["sgu%d" % i, puar], writes=["hidT"])
        outs = [(tb, half) for tb in range(nb) for half in range(2)]
        for j in range(NJ):
            ws = j % 6
            load_w(wdn[ws][:, :], wdn_v[:, j, :], "wdn%d" % ws)
            for oi, (tb, half) in enumerate(outs):
                pz, pzr = banks[oi]
                P.op("pe", lambda e, j=j, ws=ws, pz=pz, tb=tb, half=half: e.matmul(pz[0:npp, 0:512], hidT[:, j, tb * 128:tb * 128 + npp], wdn[ws][:, half * 512:(half + 1) * 512],
                                                                                 start=(j == 0), stop=(j == NJ - 1), skip_group_check=True),
                     reads=["hidT", "wdn%d" % ws], writes=[pzr], signal=(j == NJ - 1 or oi == len(outs) - 1))
        for oi, (tb, half) in enumerate(outs):
            pz, pzr = banks[oi]
            hh = h32[tb]; hr = "h32_%d" % tb
            P.op("dve", lambda e, pz=pz, half=half, hh=hh: e.tensor_tensor(hh[0:npp, half * 512:(half + 1) * 512], hh[0:npp, half * 512:(half + 1) * 512], pz[0:npp, 0:512], ALU.add),
                 reads=[hr, pzr], writes=[hr])
        def ple_item(tb, ni):
            hh = h32[tb]; hr = "h32_%d" % tb
            i = ni % 2
            t = "n%d" % i
            pi = ni % 2
            _, junk, ss, rs, xnb = st

            def p1():
                rms_rstd(hh[0:npp, :], junk[i][0:npp, :], ss[i][0:npp, :], rs[i][0:npp, :], [hr], t)
                P.op("dve", lambda e: e.scalar_tensor_tensor(xnb[i][0:npp, :], hh[0:npp, :], rs[i][0:npp, 0:1], gple[0:npp, :], ALU.mult, ALU.mult),
                     reads=[hr, "rs" + t, "gple"], writes=["xnb%d" % i])
                if samp:
                    P.dma("sp", pst[pi][0:4, :], I["psm"][0:4, :], writes=["pst%d" % pi])
                else:
                    r0 = tt * 512 + tb * 128
                    P.dma("sp", pst[pi][:, :], I["po"][r0:r0 + 128, :], writes=["pst%d" % pi])
                P.op("dve", lambda e: e.tensor_copy(pbf[pi][0:npp, :], pst[pi][0:npp, :]), reads=["pst%d" % pi], writes=["pbf%d" % pi])

            def p2():
                for k in range(8):
                    P.op("pe", lambda e, k=k: e.transpose(psT[:, k * 128:k * 128 + npp], xnb[i][0:npp, k * 128:(k + 1) * 128], ident[0:npp, 0:npp]),
                         reads=["xnb%d" % i, "ident"], writes=["psT"], signal=(k == 7))
                P.op("act", lambda e: e.copy(xfT[:, :, tb * 128:tb * 128 + npp], psT[:, :].rearrange("p (k t) -> p k t", k=8)[:, :, 0:npp]),
                     reads=["psT"], writes=["xfT"])
                for k in range(2):
                    P.op("pe", lambda e, k=k: e.transpose(psT[:, k * 128:k * 128 + npp], pbf[pi][0:npp, k * 128:(k + 1) * 128], ident[0:npp, 0:npp]),
                         reads=["pbf%d" % pi, "ident"], writes=["psT"], signal=(k == 1))
                P.op("act", lambda e: e.copy(ppT[:, :, tb * 128:tb * 128 + npp], psT[:, 0:256].rearrange("p (k t) -> p k t", k=2)[:, :, 0:npp]),
                     reads=["psT"], writes=["ppT"])

            def p3():
                for half in range(2):
                    pg, pgr = banks[half * 2]
                    pp, ppr = banks[half * 2 + 1]
                    for k in range(8):
                        P.op("pe", lambda e, k=k, pg=pg, half=half: e.matmul(pg[0:npp, :], xfT[:, k, tb * 128:tb * 128 + npp], wpg[:, k, half * 512:(half + 1) * 512], start=(k == 0), stop=(k == 7)),
                             reads=["xfT", "wpg"], writes=[pgr], signal=(k == 7))
                    for k in range(2):
                        P.op("pe", lambda e, k=k, pp=pp, half=half: e.matmul(pp[0:npp, :], ppT[:, k, tb * 128:tb * 128 + npp], wpp[:, k, half * 512:(half + 1) * 512], start=(k == 0), stop=(k == 1)),
                             reads=["ppT", "wpp"], writes=[ppr], signal=(k == 1))
                    si = half
                    P.op("act", lambda e, pg=pg, si=si: e.activation(sga2[si][0:npp, :], pg[0:npp, :], AF.Sigmoid), reads=[pgr], writes=["sga2_%d" % si])
                    P.op("dve", lambda e, pp=pp, si=si: e.tensor_tensor(sga2[si][0:npp, :], sga2[si][0:npp, :], pp[0:npp, :], ALU.mult), reads=["sga2_%d" % si, ppr], writes=["sga2_%d" % si])
                    P.op("dve", lambda e, si=si, half=half: e.tensor_tensor(hh[0:npp, half * 512:(half + 1) * 512], hh[0:npp, half * 512:(half + 1) * 512], sga2[si][0:npp, :], ALU.add),
                         reads=[hr, "sga2_%d" % si], writes=[hr])

            def p4():
                rms_rstd(hh[0:npp, :], junk[i][0:npp, :], ss2[i][0:npp, :], rs2[i][0:npp, :], [hr], "f%d" % i)
                P.op("dve", lambda e: e.scalar_tensor_tensor(hh[0:npp, :], hh[0:npp, :], rs2[i][0:npp, 0:1], gfin[0:npp, :], ALU.mult, ALU.mult),
                     reads=[hr, "rsf%d" % i, "gfin"], writes=[hr])
                if samp:
                    P.dma("sp", O["ys"][0:4, :], hh[0:4, :], reads=[hr], writes=["ysout"])
                else:
                    r0 = tt * 512 + tb * 128
                    P.dma("sp", O["y"][r0:r0 + 128, :], hh[:, :], reads=[hr], writes=["yout"])
            return (p1, p2, p3, p4)

        items = []
        for tb in range(nb):
            items.append(ple_item(tb, nit))
            nit += 1
        run_pipeline(items)
    P.pop()
    P.pop()
    P.finish()


_NC = None


def _rope_tab(pos):
    inv = np.power(np.float32(500000.0), -np.arange(8, dtype=np.float32) / np.float32(8)).astype(np.float32)
    ang = pos.astype(np.float32)[:, None] * inv[None, :]
    return np.concatenate([np.cos(ang), np.sin(ang)], axis=1).astype(np.float32)


def _consts(half):
    j = np.arange(128)
    c = {}
    c["ident"] = np.eye(128, dtype=np.float32)
    c["mprev"] = (j[:, None] >= j[None, :]).astype(np.float32)
    c["mcur"] = (j[:, None] <= j[None, :]).astype(np.float32)
    c["mbias"] = np.where(j[:, None] <= j[None, :], 0.0, NEGPAD).astype(np.float32)
    sel = np.zeros((4, 4, 128), np.float32)
    for h in range(4):
        sel[h, h, :] = 1.0
    c["sel4"] = sel.reshape(4, 512)
    own0 = half * NT
    rope = np.zeros((128, 69, 16), np.float32)
    for g, d in enumerate(DILS):
        span = 128 * d
        for r in range(d):
            pos = own0 - span + r + d * j
            rope[:, RB[g] + r, :] = _rope_tab(np.maximum(pos, 0))
        for s in range(NT // span):
            for r in range(d):
                pos = own0 + s * span + r + d * j
                rope[:, RB[g] + d + s * d + r, :] = _rope_tab(pos)
    c["rope"] = rope
    c["rope_s"] = _rope_tab(np.full((128,), PAST))
    c["flag"] = np.full((128, 1), float(half), np.float32)
    return c


def _block_w_in(w):
    offs = []
    for g in range(3):
        offs += [OFF_AQ + g * 256, OFF_AK + g * 256, OFF_AV + g * 256]
    for h in range(4):
        offs += [OFF_MQ + h * 256, OFF_MK + h * 256, OFF_MV + h * 256, OFF_MO + h * 256]
    offs += [OFF_GA + j * 256 for j in range(8)]
    out = np.empty((33, 128, 8, 256), np.float32)
    for i, o in enumerate(offs):
        out[i] = w[:, o:o + 256].reshape(8, 128, 256).transpose(1, 0, 2)
    return out


def _block_gu(wg, wu):
    out = np.empty((NJ, 128, 8, 256), np.float32)
    out[:, :, :, 0:128] = wg.reshape(8, 128, NJ, 128).transpose(2, 1, 0, 3)
    out[:, :, :, 128:256] = wu.reshape(8, 128, NJ, 128).transpose(2, 1, 0, 3)
    return out


def kernel(x_prompt, x_sample, cache_kv_w128, cache_kv_w512, cache_kv_w2048, state_conv, state_C, state_n, state_m,
           p_prompt, p_sample, norm_mix, w_in, conv_w, conv_b, b_igate, b_fgate, mh_norm, w_proj_a, w_proj_b, w_out,
           norm_ffn, w_gate, w_up, w_down, norm_ple, w_ple_gate, w_ple_proj, norm_final):
    global _NC
    f = lambda a: np.ascontiguousarray(np.asarray(a, dtype=np.float32))
    x_prompt, x_sample, p_prompt, p_sample = f(x_prompt), f(x_sample), f(p_prompt), f(p_sample)
    caches = [f(cache_kv_w128), f(cache_kv_w512), f(cache_kv_w2048)]
    state_conv, state_C, state_n, state_m = f(state_conv), f(state_C), f(state_n), f(state_m)
    shared = {
        "w_in_b": _block_w_in(f(w_in)[0]), "w_gates": f(f(w_in)[0][:, OFF_MI:OFF_MI + 8].reshape(8, 128, 8).transpose(1, 0, 2)),
        "wgu_b": _block_gu(f(w_gate)[0], f(w_up)[0]), "cw": f(f(conv_w)[0].reshape(4, 16, 128).transpose(2, 1, 0)),
        "cb": f(f(conv_b)[0].reshape(16, 128).T), "b_i": f(b_igate)[0].reshape(4, 1), "b_f": f(b_fgate)[0].reshape(4, 1),
        "mhn_col": f(f(mh_norm)[0].reshape(8, 128).T), "w_pa": f(w_proj_a)[0], "w_pb": f(w_proj_b)[0], "w_out": f(w_out)[0],
        "g_mix": f(norm_mix)[0].reshape(1, D), "g_ffn": f(norm_ffn)[0].reshape(1, D), "g_ple": f(norm_ple)[0].reshape(1, D),
        "g_fin": f(norm_final).reshape(1, D), "w_down": f(w_down)[0],
        "w_pg": f(w_ple_gate)[0], "w_pp": f(w_ple_proj)[0],
    }
    cons = [_consts(0), _consts(1)]
    in_maps = []
    for c in range(8):
        b, half = c // 2, c % 2
        own = x_prompt[b, half * NT:(half + 1) * NT]
        hist = x_prompt[b, 0:NT]
        m = dict(shared)
        m.update(cons[half])
        m["x2"] = f(np.concatenate([hist, own], axis=0))
        xs = np.zeros((128, D), np.float32); xs[0:4] = x_sample[4 * c:4 * c + 4, 0]
        m["xs"] = xs
        m["po"] = f(p_prompt[0, b, half * NT:(half + 1) * NT])
        psm = np.zeros((128, 256), np.float32); psm[0:4] = p_sample[0, 4 * c:4 * c + 4, 0]
        m["psm"] = psm
        for g in range(3):
            m["ck%d" % g] = f(caches[g][0, 4 * c:4 * c + 4].reshape(4, WINS[g], 512))
        sc = state_conv[0, 4 * c:4 * c + 4]
        m["sconv"] = f(sc)
        m["sconvT"] = f(sc.reshape(4, 3, 16, 128).transpose(3, 0, 2, 1))
        m["sC"] = f(state_C[0, 4 * c:4 * c + 4])
        m["sn"] = f(state_n[0, 4 * c:4 * c + 4])
        m["smT"] = f(state_m[0, 4 * c:4 * c + 4].T)
        in_maps.append(m)
    if _NC is None:
        _NC = build()
    res = run_bass_kernel_spmd(_NC, in_maps, core_ids=list(range(8))).results

    y_prompt = np.zeros((4, SEQ, D), np.float32)
    y_sample = np.zeros((32, 1, D), np.float32)
    kvp = [np.zeros((1, 4, WINS[g], 2, 4, 64), np.float32) for g in range(3)]
    kvs = [np.zeros((1, 32, WINS[g], 2, 4, 64), np.float32) for g in range(3)]
    conv_p = np.zeros((1, 4, 3, 2048), np.float32); conv_s = np.zeros((1, 32, 3, 2048), np.float32)
    C_p = np.zeros((1, 4, 4, 256, 256), np.float32); C_s = np.zeros((1, 32, 4, 256, 256), np.float32)
    n_p = np.zeros((1, 4, 4, 256), np.float32); n_s = np.zeros((1, 32, 4, 256), np.float32)
    m_p = np.zeros((1, 4, 4), np.float32); m_s = np.zeros((1, 32, 4), np.float32)
    for c in range(8):
        b, half = c // 2, c % 2
        r = res[c]
        y_prompt[b, half * NT:(half + 1) * NT] = r["y"]
        y_sample[4 * c:4 * c + 4, 0] = r["ys"]
        for g in range(3):
            kvs[g][0, 4 * c:4 * c + 4] = r["kvs%d" % g].reshape(4, WINS[g], 2, 4, 64)
        conv_s[0, 4 * c:4 * c + 4] = r["convs"]
        C_s[0, 4 * c:4 * c + 4] = r["Cs"]
        n_s[0, 4 * c:4 * c + 4] = r["ns"]
        m_s[0, 4 * c:4 * c + 4] = r["ms"].T
        if half == 1:
            for g in range(3):
                kvp[g][0, b] = r["kvp%d" % g].reshape(WINS[g], 2, 4, 64)
            conv_p[0, b] = r["convp"]
            C_p[0, b] = r["Cp"]
            n_p[0, b] = r["np"]
            m_p[0, b] = r["mp"][:, 0]
    return (y_prompt, y_sample, kvp[0], kvs[0], kvp[1], kvs[1], kvp[2], kvs[2], conv_p, conv_s, C_p, C_s, n_p, n_s, m_p, m_s)
```

```python
from contextlib import ExitStack
import numpy as np
import concourse.bass as bass
import concourse.mybir as mybir
from concourse.bass_utils import run_bass_kernel_spmd

F32 = mybir.dt.float32
BF16 = mybir.dt.bfloat16
AF = mybir.ActivationFunctionType
ALU = mybir.AluOpType

COMPUTE = ("pe", "act", "dve", "pool")
ROLL = 24000

D = 1024
SEQ = 4096
NT = 2048
HD = 64
DILS = (1, 4, 16)
WINS = (128, 512, 2048)
PAST = 16384
OFF_AQ, OFF_AK, OFF_AV = 0, 768, 1536
OFF_MQ, OFF_MK, OFF_MV, OFF_MO = 2304, 3328, 4352, 5376
OFF_MI, OFF_MF, OFF_GA, OFF_GB = 6400, 6404, 6408, 7432
IN_COLS = 8456
DFF = 2816
NJ = DFF // 128
EPS = 1e-6
NEGPAD = -30000.0
HB = (0, 1, 5)
RB = (0, 17, 37)


class Res:
    __slots__ = ("name", "w", "rd", "psum")

    def __init__(self, name, psum=False):
        self.name = name
        self.w = None
        self.rd = []
        self.psum = psum


class _Rec:
    def __init__(self):
        self.call = None

    def __getattr__(self, name):
        def f(*a, **k):
            self.call = (name, a, k)
            return None
        return f


class Prog:
    def __init__(self, nc, n_dma_sems=12):
        self.nc = nc
        self.scopes = [ExitStack()]
        self.streams = {e: [] for e in ("pe", "act", "dve", "pool", "sp")}
        self.sem = {}
        self.cnt = {}
        self.seen = {e: {} for e in self.streams}
        self.nsem = 0
        for e in COMPUTE:
            self._new_sem(e)
        self.dma_sems = {}
        self.dma_tot = {}
        self.dma_rr = {}
        for q in ("sp", "pool"):
            self.dma_sems[q] = [self._alloc_sem("d%s%d" % (q, i)) for i in range(n_dma_sems)]
            self.dma_tot[q] = [0] * n_dma_sems
            self.dma_rr[q] = 0
        self.res = {}
        self.nops = 0

    def _alloc_sem(self, name):
        self.nsem += 1
        return self.scopes[0].enter_context(self.nc.semaphore(name))

    def _new_sem(self, e):
        self.sem[e] = self._alloc_sem("c%s%d" % (e, self.nsem))
        self.cnt[e] = 0

    def sb(self, name, shape, dt):
        self.ntens = getattr(self, "ntens", 0) + 1
        return self.scopes[-1].enter_context(self.nc.sbuf_tensor("s%d_%s" % (self.ntens, name), list(shape), dt))

    def ps(self, name, shape, dt):
        return self.scopes[0].enter_context(self.nc.psum_tensor("p_" + name, list(shape), dt))

    def push(self):
        self.scopes.append(ExitStack())

    def pop(self):
        self.flush()
        self.scopes.pop().close()

    def R(self, name, psum=False):
        r = self.res.get(name)
        if r is None:
            r = Res(name, psum)
            self.res[name] = r
        return r

    def _need(self, e, tok, deps, same_raw):
        if tok is None:
            return
        we, sem, val = tok
        if we == e and e == "pe":
            return
        deps.append((sem, val))

    def _waits(self, e, reads, writes):
        deps = []
        for r in reads:
            self._need(e, r.w, deps, True)
            if r.psum:
                for t in r.rd:
                    self._need(e, t, deps, False)
        for w in writes:
            self._need(e, w.w, deps, w.psum)
            for t in w.rd:
                self._need(e, t, deps, False)
        seen = self.seen[e]
        best = {}
        for sem, val in deps:
            k = id(sem)
            if seen.get(k, 0) >= val:
                continue
            if k not in best or best[k][1] < val:
                best[k] = (sem, val)
        for k, (sem, val) in best.items():
            seen[k] = val
            self.streams[e].append(("wait", sem, val))

    def _commit(self, tok, reads, writes):
        for r in reads:
            r.rd = [t for t in r.rd if not (t[1] is tok[1])] + [tok]
        for w in writes:
            w.w = tok
            w.rd = []

    def _rl(self, xs):
        return [self.R(x) if isinstance(x, str) else x for x in xs]

    def op(self, e, fn, reads=(), writes=(), signal=True):
        reads = self._rl(reads)
        writes = self._rl(writes)
        self._waits(e, reads, writes)
        self.nops += 1
        rec = _Rec()
        fn(rec)
        fn = rec.call
        if signal:
            if self.cnt[e] >= ROLL:
                self._new_sem(e)
            self.cnt[e] += 1
            tok = (e, self.sem[e], self.cnt[e])
            self.streams[e].append(("op", fn, self.sem[e]))
        else:
            tok = (e, self.sem[e], self.cnt[e] + 1)
            self.streams[e].append(("op", fn, None))
        self._commit(tok, reads, writes)
        return tok

    def dma(self, q, out_ap, in_ap, reads=(), writes=(), **kw):
        reads = self._rl(reads)
        writes = self._rl(writes)
        self._waits(q, reads, writes)
        i = self.dma_rr[q]
        self.dma_rr[q] = (i + 1) % len(self.dma_sems[q])
        sem = self.dma_sems[q][i]
        prev = self.dma_tot[q][i]
        if prev > 0 and self.seen[q].get(id(sem), 0) < prev:
            self.seen[q][id(sem)] = prev
            self.streams[q].append(("wait", sem, prev))
        self.dma_tot[q][i] = prev + 16
        tok = ("dma", sem, prev + 16)
        self.streams[q].append(("dma", out_ap, in_ap, sem, kw))
        self._commit(tok, reads, writes)
        return tok

    def barrier(self):
        toks = []
        for e in COMPUTE:
            if self.cnt[e] > 0:
                toks.append((self.sem[e], self.cnt[e]))
        for q in self.dma_sems:
            for sem, tot in zip(self.dma_sems[q], self.dma_tot[q]):
                if tot > 0:
                    toks.append((sem, tot))
        for e in self.streams:
            for sem, val in toks:
                if e in COMPUTE and sem is self.sem[e]:
                    continue
                if self.seen[e].get(id(sem), 0) < val:
                    self.seen[e][id(sem)] = val
                    self.streams[e].append(("wait", sem, val))

    def flush(self):
        self.barrier()
        nc = self.nc
        streams = self.streams

        def replay(eng, items):
            for it in items:
                if it[0] == "wait":
                    eng.wait_ge(it[1], it[2])
                elif it[0] == "op":
                    ins = getattr(eng, it[1][0])(*it[1][1], **it[1][2])
                    if it[2] is not None:
                        ins.then_inc(it[2], 1)
                else:
                    _, o, i, sem, kw = it
                    eng.dma_start(out=o, in_=i, **kw).then_inc(sem, 16)

        with nc.Block() as block:
            @block.sync
            def _(sync):
                replay(sync, streams["sp"])

            @block.tensor
            def _(t):
                replay(t, streams["pe"])

            @block.scalar
            def _(s):
                replay(s, streams["act"])

            @block.vector
            def _(v):
                replay(v, streams["dve"])

            @block.gpsimd
            def _(g):
                replay(g, streams["pool"])
        for e in streams:
            streams[e] = []

    def finish(self):
        self.flush()
        while self.scopes:
            self.scopes.pop().close()


def ssl(start, n, step):
    return slice(start, start + (n - 1) * step + 1, step)


class _Stop(Exception):
    pass


STOP_AFTER = None


def build():
    nc = bass.Bass("TRN2", target_bir_lowering=False)
    state = {}
    try:
        _build_inner(nc, state)
    except _Stop:
        state["P"].finish()
    return nc


def _build_inner(nc, state):
    I, O = {}, {}

    def din(n, shape):
        I[n] = nc.dram_tensor(n, list(shape), F32, kind="ExternalInput").ap()

    def dout(n, shape):
        O[n] = nc.dram_tensor(n, list(shape), F32, kind="ExternalOutput").ap()

    din("x2", [2 * NT, D]); din("xs", [128, D]); din("po", [NT, 256]); din("psm", [128, 256])
    din("ck0", [4, 128, 512]); din("ck1", [4, 512, 512]); din("ck2", [4, 2048, 512])
    din("sconvT", [128, 4, 16, 3]); din("sconv", [4, 3, 2048])
    din("sC", [4, 4, 256, 256]); din("sn", [4, 4, 256]); din("smT", [4, 4])
    din("w_in_b", [33, 128, 8, 256]); din("w_gates", [128, 8, 8]); din("cw", [128, 16, 4]); din("cb", [128, 16])
    din("b_i", [4, 1]); din("b_f", [4, 1]); din("mhn_col", [128, 8])
    din("w_pa", [256, D]); din("w_pb", [D, D]); din("w_out", [D, D])
    din("g_mix", [1, D]); din("g_ffn", [1, D]); din("g_ple", [1, D]); din("g_fin", [1, D])
    din("wgu_b", [NJ, 128, 8, 256]); din("w_down", [DFF, D])
    din("w_pg", [D, D]); din("w_pp", [256, D])
    din("ident", [128, 128]); din("mprev", [128, 128]); din("mcur", [128, 128]); din("mbias", [128, 128])
    din("sel4", [4, 512]); din("rope", [128, 69, 16]); din("rope_s", [128, 16]); din("flag", [128, 1])

    dout("y", [NT, D]); dout("ys", [4, D])
    dout("kvp0", [128, 512]); dout("kvp1", [512, 512]); dout("kvp2", [2048, 512])
    dout("kvs0", [4, 128, 512]); dout("kvs1", [4, 512, 512]); dout("kvs2", [4, 2048, 512])
    dout("convp", [3, 2048]); dout("convs", [4, 3, 2048])
    dout("Cp", [4, 256, 256]); dout("np", [4, 256]); dout("mp", [4, 1])
    dout("Cs", [4, 4, 256, 256]); dout("ns", [4, 4, 256]); dout("ms", [4, 4])

    P = Prog(nc)
    state["P"] = P

    def phase_done(i):
        if STOP_AFTER is not None and i >= STOP_AFTER:
            raise _Stop()

    ps0 = P.ps("ps0", [128, 512], F32)
    ps1 = P.ps("ps1", [128, 512], F32)
    psA = P.ps("psA", [128, 1024], F32)
    ps4 = P.ps("ps4", [128, 512], F32)
    psTf = P.ps("psT", [128, 512], F32)
    psT = psTf[:, :].bitcast(BF16)
    ps6 = P.ps("ps6", [128, 512], F32)
    ps7 = P.ps("ps7", [128, 512], F32)
    for n in ("ps0", "ps1", "psA0", "psA1", "ps4", "psT", "ps6", "ps7"):
        P.R(n, True)
    banks = [(ps0, "ps0"), (ps1, "ps1"), (psA[:, 0:512], "psA0"), (psA[:, 512:1024], "psA1"),
             (ps4, "ps4"), (ps6, "ps6"), (ps7, "ps7"), (psTf, "psT")]

    identf = P.sb("identf", [128, 128], F32)
    ident = P.sb("ident", [128, 128], BF16)
    mask2 = P.sb("mask2", [128, 2, 128], BF16)
    mask2f = P.sb("mask2f", [128, 2, 128], BF16)
    mbias = P.sb("mbias", [128, 128], BF16)
    sel4 = P.sb("sel4", [4, 4, 128], F32)
    rope = P.sb("rope", [128, 69, 16], F32)
    rope_s = P.sb("rope_s", [128, 16], F32)
    flag = P.sb("flag", [128, 1], F32)
    ones = P.sb("ones", [128, 512], F32)
    C32 = P.sb("C32", [128, 4, 2, 257], F32)
    cbnd = P.sb("cbnd", [128, 4, 4, 3], F32)
    MPh = P.sb("MPh", [4, 1], F32)
    cw = P.sb("cw", [128, 16, 4], F32)
    cb = P.sb("cb", [128, 16], F32)
    b_i = P.sb("b_i", [4, 1], F32)
    nb_f = P.sb("nb_f", [4, 1], F32)
    wgt8 = P.sb("wgt8", [128, 8, 8], BF16)
    nln16 = P.sb("nln16", [4, 1], F32)

    P.push()
    mtmp = P.sb("mtmp", [128, 2, 128], F32)
    P.dma("sp", identf[:], I["ident"][:, :], writes=["identf"])
    P.dma("sp", mtmp[:, 0, :], I["mprev"][:, :], writes=["mtmp"])
    P.dma("sp", mtmp[:, 1, :], I["mcur"][:, :], writes=["mtmp"])
    P.dma("pool", mbias[:], I["mbias"][:, :], writes=["mbias"])
    P.dma("sp", sel4[:], I["sel4"].rearrange("k (h m) -> k h m", h=4), writes=["sel4"])
    P.dma("sp", rope[:], I["rope"][:, :, :], writes=["rope"])
    P.dma("sp", rope_s[:], I["rope_s"][:, :], writes=["rope_s"])
    P.dma("sp", flag[:], I["flag"][:, :], writes=["flag"])
    P.dma("sp", cw[:], I["cw"][:, :, :], writes=["cw"])
    P.dma("sp", cb[:], I["cb"][:, :], writes=["cb"])
    P.dma("sp", b_i[:], I["b_i"][:, :], writes=["b_i"])
    P.dma("sp", nb_f[:], I["b_f"][:, :], writes=["nb_f"])
    P.dma("pool", wgt8[:], I["w_gates"][:, :, :], writes=["wgt8"])
    P.op("dve", lambda e: e.tensor_copy(ident[:], identf[:]), reads=["identf"], writes=["ident"])
    P.op("dve", lambda e: e.tensor_copy(mask2[:], mtmp[:]), reads=["mtmp"], writes=["mask2"])
    P.op("dve", lambda e: e.tensor_copy(mask2f[:, 1, :], mtmp[:, 1, :]), reads=["mtmp"], writes=["mask2f"])
    P.op("dve", lambda e: e.tensor_scalar(mask2f[:, 0, :], mtmp[:, 0, :], flag[:, 0:1], None, ALU.mult),
         reads=["mtmp", "flag", "mask2f"], writes=["mask2f"])
    P.op("dve", lambda e: e.memset(ones[:], 1.0), writes=["ones"])
    P.op("dve", lambda e: e.memset(nln16[:], -2.7725887222), writes=["nln16"])
    P.op("dve", lambda e: e.tensor_scalar(nb_f[:], nb_f[:], -1.0, None, ALU.mult), reads=["nb_f"], writes=["nb_f"])
    P.op("dve", lambda e: e.memset(C32[:], 0.0), writes=["C32"])
    P.op("dve", lambda e: e.memset(cbnd[:], 0.0), writes=["cbnd"])
    P.pop()


    def load_w(dst, src_view, res):
        P.dma("pool", dst, src_view, writes=[res])

    def rms_rstd(src_ap, junk, ss, rs, srcres, tag):
        P.op("act", lambda e: e.activation(junk, src_ap, AF.Square, accum_out=ss), reads=srcres, writes=["junk" + tag, "ss" + tag])
        P.op("dve", lambda e: e.tensor_scalar(rs, ss, 1.0 / D, EPS, ALU.mult, ALU.add), reads=["ss" + tag], writes=["rs" + tag])
        P.op("act", lambda e: e.activation(rs, rs, AF.Sqrt), reads=["rs" + tag], writes=["rs" + tag])
        P.op("dve", lambda e: e.reciprocal(rs, rs), reads=["rs" + tag], writes=["rs" + tag])

    def run_pipeline(items):
        ns = max(len(t) for t in items)
        for n_ in range(len(items) + ns - 1):
            for k in range(ns):
                idx = n_ - k
                if 0 <= idx < len(items) and k < len(items[idx]):
                    items[idx][k]()

    def make_stage(with_xt=True):
        xt = [P.sb("xt%d" % i, [128, D], F32) for i in range(2)] if with_xt else None
        junk = [P.sb("junk%d" % i, [128, D], BF16) for i in range(2)]
        ss = [P.sb("ss%d" % i, [128, 1], F32) for i in range(2)]
        rs = [P.sb("rs%d" % i, [128, 1], F32) for i in range(2)]
        xnb = [P.sb("xnb%d" % i, [128, D], BF16) for i in range(2)]
        return xt, junk, ss, rs, xnb

    def norm_T(st, i, src, srcres, np_, gbc, gres, dstT, dst_res):
        _, junk, ss, rs, xnb = st
        t = "n%d" % i
        rms_rstd(src, junk[i][0:np_, :], ss[i][0:np_, :], rs[i][0:np_, :], [srcres], t)
        P.op("dve", lambda e: e.scalar_tensor_tensor(xnb[i][0:np_, :], src, rs[i][0:np_, 0:1], gbc[0:np_, :], ALU.mult, ALU.mult),
             reads=[srcres, "rs" + t, gres], writes=["xnb%d" % i])
        for k in range(8):
            P.op("pe", lambda e, k=k: e.transpose(psT[:, k * 128:k * 128 + np_], xnb[i][0:np_, k * 128:(k + 1) * 128], ident[0:np_, 0:np_]),
                 reads=["xnb%d" % i, "ident"], writes=["psT"], signal=(k == 7))
        P.op("act", lambda e: e.copy(dstT, psT[:, :].rearrange("p (k t) -> p k t", k=8)[:, :, 0:np_]),
             reads=["psT"], writes=[dst_res])

    P.push()
    xnT = P.sb("xnT", [128, 8, NT], BF16)
    xnSc = P.sb("xnSc", [128, 8, 4], BF16)
    mixS = P.sb("mixS", [128, 8, 4], BF16)
    P.push()
    oaT = P.sb("oaT", [64, 4, NT], BF16)
    oaTs = P.sb("oaTs", [64, 4, 4], BF16)
    P.push()
    KTh = P.sb("KTh", [128, 2, 21, 128], BF16)
    VAh = P.sb("VAh", [128, 21, 4, 65], BF16)
    P.op("dve", lambda e: e.memset(VAh[:], 1.0), writes=["VAh"])

    def phase_norm(row0, with_samples):
        P.push()
        gmix = P.sb("gmix", [128, D], F32)
        P.dma("sp", gmix[:], I["g_mix"][0, :].partition_broadcast(128), writes=["gmix"])
        st = make_stage()
        def nitem(blk):
            i = blk % 2
            _, junk, ss, rs, xnb = st
            t = "n%d" % i

            def n1():
                P.dma("sp", st[0][i][:, :], I["x2"][row0 + blk * 128:row0 + (blk + 1) * 128, :], writes=["xt%d" % i])
                rms_rstd(st[0][i][:, :], junk[i][:, :], ss[i][:, :], rs[i][:, :], ["xt%d" % i], t)
                P.op("dve", lambda e: e.scalar_tensor_tensor(xnb[i][:, :], st[0][i][:, :], rs[i][:, 0:1], gmix[:, :], ALU.mult, ALU.mult),
                     reads=["xt%d" % i, "rs" + t, "gmix"], writes=["xnb%d" % i])

            def n2():
                for k in range(8):
                    P.op("pe", lambda e, k=k: e.transpose(psT[:, k * 128:(k + 1) * 128], xnb[i][:, k * 128:(k + 1) * 128], ident[:, :]),
                         reads=["xnb%d" % i, "ident"], writes=["psT"], signal=(k == 7))
                P.op("act", lambda e: e.copy(xnT[:, :, blk * 128:(blk + 1) * 128], psT[:, :].rearrange("p (k t) -> p k t", k=8)),
                     reads=["psT"], writes=["xnT"])
            return (n1, n2)
        run_pipeline([nitem(blk) for blk in range(16)])
        if with_samples:
            P.dma("sp", st[0][0][0:4, :], I["xs"][0:4, :], writes=["xt0"])
            norm_T(st, 0, st[0][0][0:4, :], "xt0", 4, gmix, "gmix", xnSc[:, :, :], "xnSc")
        P.pop()

    phase_norm(0, True)
    def attn_scope():
        A = {}
        A["wg"] = P.sb("wg", [128, 8, 768], BF16)
        A["qkv32"] = [P.sb("qkv32_%d" % i, [128, 768], F32) for i in range(3)]
        A["qkb"] = [P.sb("qkb%d" % i, [128, 512], BF16) for i in range(2)]
        A["rt"] = [P.sb("rt%d" % i, [128, 4, 8, 8], F32) for i in range(2)]
        return A

    def load_wg(A, g):
        for j, off in enumerate((OFF_AQ, OFF_AK, OFF_AV)):
            load_w(A["wg"][:, :, j * 256:(j + 1) * 256], I["w_in_b"][g * 3 + j, :, :, :], "wg")

    def attn_p1(A, it, lhs_fn, lhs_res, VA_dst, VA_res):
        i3 = it % 3
        i = it % 2
        wg = A["wg"]
        qkv32 = A["qkv32"][i3]; qres = "qkv32_%d" % i3
        pqk, pqkr = (ps0, "ps0") if i == 0 else (ps6, "ps6")
        pvv, pvr = (ps1, "ps1") if i == 0 else (ps7, "ps7")
        for k in range(8):
            P.op("pe", lambda e, k=k: e.matmul(pqk[:, :], lhs_fn(k), wg[:, k, 0:512], start=(k == 0), stop=(k == 7)),
                 reads=[lhs_res, "wg"], writes=[pqkr], signal=(k == 7))
        for k in range(8):
            P.op("pe", lambda e, k=k: e.matmul(pvv[:, 0:256], lhs_fn(k), wg[:, k, 512:768], start=(k == 0), stop=(k == 7)),
                 reads=[lhs_res, "wg"], writes=[pvr], signal=(k == 7))
        P.op("act", lambda e: e.copy(qkv32[:, 0:512], pqk[:, :]), reads=[pqkr], writes=[qres])
        P.op("act", lambda e: e.copy(qkv32[:, 512:768], pvv[:, 0:256]), reads=[pvr], writes=[qres])
        P.op("act", lambda e: e.copy(VA_dst[:, :, 0:64], pvv[:, 0:256].rearrange("p (h d) -> p h d", d=64)), reads=[pvr], writes=[VA_res])

    def attn_p2(A, it, ropetab, roperes, kv_out=None, kv_rows=128):
        i3 = it % 3
        i = it % 2
        qkv32 = A["qkv32"][i3]; qres = "qkv32_%d" % i3
        qkb = A["qkb"][i]; bres = "qkb%d" % i
        rt = A["rt"][i]; rres = "rt%d" % i
        cosb = ropetab[:, 0:8].unsqueeze(1).broadcast_to([128, 8, 8])
        sinb = ropetab[:, 8:16].unsqueeze(1).broadcast_to([128, 8, 8])
        q3 = qkv32[:, 0:512].rearrange("p (h d) -> p h d", d=64)
        x1 = q3[:, :, 0:8]
        x2 = q3[:, :, 8:16]
        P.op("dve", lambda e: e.tensor_tensor(rt[:, 0], x1, cosb, ALU.mult), reads=[qres, roperes], writes=[rres])
        P.op("dve", lambda e: e.tensor_tensor(rt[:, 1], x2, sinb, ALU.mult), reads=[qres, roperes], writes=[rres])
        P.op("dve", lambda e: e.tensor_tensor(rt[:, 2], x2, cosb, ALU.mult), reads=[qres, roperes], writes=[rres])
        P.op("dve", lambda e: e.tensor_tensor(rt[:, 3], x1, sinb, ALU.mult), reads=[qres, roperes], writes=[rres])
        P.op("dve", lambda e: e.tensor_tensor(x1, rt[:, 0], rt[:, 1], ALU.subtract), reads=[rres, qres], writes=[qres])
        P.op("dve", lambda e: e.tensor_tensor(x2, rt[:, 2], rt[:, 3], ALU.add), reads=[rres, qres], writes=[qres])
        P.op("dve", lambda e: e.tensor_copy(qkb[:, :], qkv32[:, 0:512]), reads=[qres], writes=[bres])
        if kv_out is not None:
            P.dma("sp", kv_out, qkv32[0:kv_rows, 256:768], reads=[qres], writes=["kvout"])

    def attn_p3(A, it, KT_dst, KT_res, QT_dst=None, QT_res=None):
        i = it % 2
        qkb = A["qkb"][i]; bres = "qkb%d" % i
        for j in range(4):
            P.op("pe", lambda e, j=j: e.transpose(psT[:, j * 128:(j + 1) * 128], qkb[:, j * 128:(j + 1) * 128], ident[:, :]),
                 reads=[bres, "ident"], writes=["psT"], signal=(j == 3))
        pT3 = psT[:, 0:512].rearrange("p (j t) -> p j t", j=4)
        if QT_dst is not None:
            P.op("act", lambda e: e.copy(QT_dst, pT3[:, 0:2, :]), reads=["psT"], writes=[QT_res])
        P.op("act", lambda e: e.copy(KT_dst, pT3[:, 2:4, :]), reads=["psT"], writes=[KT_res])

    def attn_c1(it, QT, QT_res, KTp, KTp_res, KTc, KTc_res, PT):
        i = it % 2
        pt = PT[i]; pres = "PT%d" % i
        pS = psA[:, :].rearrange("p (a c b q) -> p a c b q", a=2, c=2, b=2)
        for h in range(4):
            hp, po = h // 2, (h % 2) * 64
            bank = "psA%d" % (h % 2)
            P.op("pe", lambda e, h=h, hp=hp, po=po: e.matmul(pS[:, h % 2, h // 2, 0, :], KTp[po:po + 64, hp, :], QT[po:po + 64, hp, :], start=True, stop=True),
                 reads=[KTp_res, QT_res], writes=[bank], signal=False)
            P.op("pe", lambda e, h=h, hp=hp, po=po: e.matmul(pS[:, h % 2, h // 2, 1, :], KTc[po:po + 64, hp, :], QT[po:po + 64, hp, :], start=True, stop=True),
                 reads=[KTc_res, QT_res], writes=[bank], signal=(h >= 2))
        P.op("act", lambda e: e.activation(pt[:, 0, :, :, :], pS[:, 0, :, :, :], AF.Exp, scale=0.125), reads=["psA0"], writes=[pres])
        P.op("act", lambda e: e.activation(pt[:, 1, :, :, :], pS[:, 1, :, :, :], AF.Exp, scale=0.125), reads=["psA1"], writes=[pres])

    def attn_c2(it, VAp, VAp_res, VAc, VAc_res, msk, PT, acc_dst, acc_res, first, ncol):
        i = it % 2
        pt = PT[i]; pres = "PT%d" % i
        mb = msk[:, :, :].unsqueeze(1).broadcast_to([128, 4, 2, 128])
        pt4 = pt[:, :, :, :, :].rearrange("p a c b q -> p (a c) b q")
        P.op("dve", lambda e: e.tensor_tensor(pt4, pt4, mb, ALU.mult), reads=[pres, "mask2", "mask2f"], writes=[pres])
        pO = ps4[0:65, :].rearrange("p (h q) -> p h q", h=4)
        for h in range(4):
            P.op("pe", lambda e, h=h: e.matmul(pO[:, h, :], VAp[:, h, :], pt[:, h % 2, h // 2, 0, :], start=True, stop=False),
                 reads=[VAp_res, pres], writes=["ps4"], signal=False)
            P.op("pe", lambda e, h=h: e.matmul(pO[:, h, :], VAc[:, h, :], pt[:, h % 2, h // 2, 1, :], start=False, stop=True),
                 reads=[VAc_res, pres], writes=["ps4"], signal=(h == 3))
        if first:
            P.op("act", lambda e: e.copy(acc_dst, pO[:, :, 0:ncol]), reads=["ps4"], writes=[acc_res])
        else:
            P.op("dve", lambda e: e.tensor_tensor(acc_dst, acc_dst, pO[:, :, 0:ncol], ALU.add), reads=["ps4", acc_res], writes=[acc_res])

    P.push()
    A = attn_scope()

    def hist_item(g, r, it):
        d = DILS[g]
        hb = HB[g] + r
        start = NT - 128 * d + r

        def s1():
            if r == 0:
                load_wg(A, g)
            attn_p1(A, it, lambda k: xnT[:, k, ssl(start, 128, d)], "xnT", VAh[:, hb, :, :], "VAh")

        def s2():
            attn_p2(A, it, rope[:, RB[g] + r, :], "rope")

        def s3():
            attn_p3(A, it, KTh[:, :, hb, :], "KTh%d" % hb)
        return (s1, s2, s3)

    items = []
    it = 0
    for g in range(3):
        for r in range(DILS[g]):
            items.append(hist_item(g, r, it))
            it += 1
    run_pipeline(items)
    P.pop()
    phase_done(2)


    def mlstm_scope(nt, full):
        nch = nt // 128
        M = {"nt": nt, "nch": nch, "full": full}
        M["wm"] = P.sb("wm", [128, 8, 1024], BF16)
        for n in ("IG", "LF", "Bc", "bb", "Aa", "Mr", "WK", "tmpr"):
            M[n] = P.sb(n, [4, 512], F32)
        M["MP"] = P.sb("MP", [4, nch + 1], F32)
        M["MN"] = P.sb("MN", [4, nch], F32)
        M["DEC"] = P.sb("DEC", [4, nch], F32)
        M["COL"] = P.sb("COL", [128, nch, 3, 4], F32)
        M["DECB"] = P.sb("DECB", [128, 4, nch], F32)
        nloc = min(nt, 1024)
        M["nloc"] = nloc
        M["kT"] = P.sb("kT", [128, 2, nloc], BF16)
        nfc = 4 if full else 2
        M["xpre"] = P.sb("xpre", [128, nfc, 515], F32)
        M["bnd"] = P.sb("bnd", [128, nfc, 3], F32)
        M["cacc"] = [P.sb("cacc%d" % i, [128, 512], F32) for i in range(2)]
        M["ktmp"] = P.sb("ktmp", [128, 512], F32)
        M["Vaug"] = P.sb("Vaug", [128, nloc // 128, 257], BF16)
        M["kwa"] = P.sb("kwa", [128, 8, 256], BF16)
        if full:
            M["EM"] = P.sb("EM", [4, 512], F32)
            M["NEGM"] = P.sb("NEGM", [4, nt], F32)
            M["WI"] = P.sb("WI", [4, nt], F32)
            M["Cbf"] = [P.sb("Cbf%d" % i, [128, 2, 257], BF16) for i in range(2)]
            M["wTa"] = P.sb("wTa", [128, 8, 128], BF16)
            M["qpTa"] = P.sb("qpTa", [128, 8, 2, 128], BF16)
            if nt == 512:
                M["Csb"] = [P.sb("Csb%d" % i, [128, 2, 257], F32) for i in range(2)]
            M["qT"] = P.sb("qT", [128, 2, nloc], BF16)
            M["G"] = P.sb("G", [128, nloc // 128, 256], BF16)
            M["mhn"] = P.sb("mhncol", [128, 8], F32)
            M["Dt"] = [P.sb("Dt%d" % i, [128, 128], F32) for i in range(2)]
            M["sm"] = [P.sb("sm%d" % i, [128, 8], F32) for i in range(2)]
            M["jk"] = [P.sb("jk%d" % i, [128, 256], BF16) for i in range(2)]
            M["ob"] = [P.sb("ob%d" % i, [128, 256], BF16) for i in range(2)]
            P.dma("sp", M["mhn"][:], I["mhn_col"][:, :], writes=["mhncol"])
        P.op("dve", lambda e: e.memset(M["Vaug"][:], 1.0), writes=["Vaug"])
        return M

    def mlstm_gates(M, Xf, Xres, mode):
        nt, nch, full = M["nt"], M["nch"], M["full"]
        IG, LF, Bc, bb, Aa, Mr, WK, tmpr = M["IG"], M["LF"], M["Bc"], M["bb"], M["Aa"], M["Mr"], M["WK"], M["tmpr"]
        MP, MN, DEC = M["MP"], M["MN"], M["DEC"]
        samp = (mode == "samp")
        B3 = Bc[:, :].rearrange("p (c t) -> p c t", t=128)
        b3 = bb[:, :].rearrange("p (c t) -> p c t", t=128)
        A3 = Aa[:, :].rearrange("p (c t) -> p c t", t=128)
        M3 = Mr[:, :].rearrange("p (c t) -> p c t", t=128)
        t3 = tmpr[:, :].rearrange("p (c t) -> p c t", t=128)
        for tt in range(nt // 512):
            c0 = tt * 4
            cs = slice(tt * 512, (tt + 1) * 512)
            ncol = 4 if samp else 512
            for k in range(8):
                P.op("pe", lambda e, k=k: e.matmul(ps0[0:4, 0:ncol], wgt8[:, k, 0:4], Xf(k, cs), start=(k == 0), stop=(k == 7)),
                     reads=["wgt8", Xres], writes=["ps0"], signal=(k == 7))
            for k in range(8):
                P.op("pe", lambda e, k=k: e.matmul(ps1[0:4, 0:ncol], wgt8[:, k, 4:8], Xf(k, cs), start=(k == 0), stop=(k == 7)),
                     reads=["wgt8", Xres], writes=["ps1"], signal=(k == 7))
            if samp:
                P.op("dve", lambda e: e.memset(IG[:], NEGPAD), writes=["IG"])
                P.op("dve", lambda e: e.memset(LF[:], 0.0), writes=["LF"])
                P.op("act", lambda e: e.activation(IG[:, 0:512:128], ps0[0:4, 0:4], AF.Identity, bias=b_i[:, 0:1]), reads=["ps0", "b_i", "IG"], writes=["IG"])
                P.op("act", lambda e: e.activation(tmpr[:, 0:4], ps1[0:4, 0:4], AF.Exp, bias=nb_f[:, 0:1], scale=-1.0), reads=["ps1", "nb_f"], writes=["tmpr"])
                P.op("act", lambda e: e.activation(tmpr[:, 0:4], tmpr[:, 0:4], AF.Ln, bias=1.0), reads=["tmpr"], writes=["tmpr"])
                P.op("dve", lambda e: e.tensor_scalar(LF[:, 0:512:128], tmpr[:, 0:4], -1.0, None, ALU.mult), reads=["tmpr", "LF"], writes=["LF"])
            else:
                P.op("act", lambda e: e.activation(IG[:, :], ps0[0:4, :], AF.Identity, bias=b_i[:, 0:1]), reads=["ps0", "b_i"], writes=["IG"])
                P.op("act", lambda e: e.activation(tmpr[:, :], ps1[0:4, :], AF.Exp, bias=nb_f[:, 0:1], scale=-1.0), reads=["ps1", "nb_f"], writes=["tmpr"])
                P.op("act", lambda e: e.activation(tmpr[:, :], tmpr[:, :], AF.Ln, bias=1.0), reads=["tmpr"], writes=["tmpr"])
                P.op("dve", lambda e: e.tensor_scalar(LF[:, :], tmpr[:, :], -1.0, None, ALU.mult), reads=["tmpr"], writes=["LF"])
            P.op("dve", lambda e: e.tensor_tensor_scan(Bc[:, :], ones[0:4, 0:512], LF[:, :], 0.0, ALU.mult, ALU.add),
                 reads=["LF", "ones"], writes=["Bc"])
            P.op("dve", lambda e: e.tensor_copy(b3[:, 0, :], B3[:, 0, :]), reads=["Bc"], writes=["bb"])
            P.op("dve", lambda e: e.tensor_tensor(b3[:, 1:4, :], B3[:, 1:4, :], B3[:, 0:3, 127:128].broadcast_to([4, 3, 128]), ALU.subtract),
                 reads=["Bc", "bb"], writes=["bb"])
            P.op("dve", lambda e: e.tensor_tensor(Aa[:, :], IG[:, :], bb[:, :], ALU.subtract), reads=["IG", "bb"], writes=["Aa"])
            for cl in range(4):
                c = c0 + cl
                P.op("dve", lambda e, c=c, cl=cl: e.tensor_tensor_scan(M3[:, cl, :], A3[:, cl, :], A3[:, cl, :], MP[:, c:c + 1], ALU.max, ALU.max),
                     reads=["Aa", "MP", "Mr"], writes=["Mr"])
                dst = MN[:, c:c + 1] if samp else MP[:, c + 1:c + 2]
                P.op("dve", lambda e, cl=cl, dst=dst: e.tensor_tensor(dst, b3[:, cl, 127:128], M3[:, cl, 127:128], ALU.add),
                     reads=["bb", "Mr", "MP", "MN"], writes=["MN" if samp else "MP"])
            if not samp:
                P.op("dve", lambda e, c0=c0: e.tensor_copy(MN[:, c0:c0 + 4], MP[:, c0 + 1:c0 + 5]), reads=["MP", "MN"], writes=["MN"])
            mpb = MP[:, c0:c0 + 4].unsqueeze(2).broadcast_to([4, 4, 128])
            mlb = M3[:, :, 127:128].broadcast_to([4, 4, 128])
            if full:
                P.op("dve", lambda e, cs=cs: e.tensor_scalar(M["NEGM"][:, cs], Mr[:, :], -1.0, None, ALU.mult), reads=["Mr", "NEGM"], writes=["NEGM"])
                P.op("dve", lambda e: e.tensor_tensor(t3, mpb, M3, ALU.subtract), reads=["MP", "Mr", "tmpr"], writes=["tmpr"])
                P.op("act", lambda e, cs=cs: e.activation(M["WI"][:, cs], tmpr[:, :], AF.Exp), reads=["tmpr", "WI"], writes=["WI"])
                P.op("dve", lambda e: e.tensor_tensor(tmpr[:, :], bb[:, :], Mr[:, :], ALU.add), reads=["bb", "Mr", "tmpr"], writes=["tmpr"])
                P.op("act", lambda e: e.activation(M["EM"][:, :], tmpr[:, :], AF.Exp, scale=-2.0), reads=["tmpr"], writes=["EM"])
            P.op("dve", lambda e: e.tensor_tensor(t3, A3, mlb, ALU.subtract), reads=["Aa", "Mr", "tmpr"], writes=["tmpr"])
            P.op("act", lambda e: e.activation(WK[:, :], tmpr[:, :], AF.Exp, bias=nln16[:, 0:1]), reads=["tmpr", "nln16"], writes=["WK"])
            P.op("dve", lambda e, c0=c0: e.tensor_tensor(DEC[:, c0:c0 + 4], MP[:, c0:c0 + 4], M3[:, :, 127], ALU.subtract), reads=["MP", "Mr", "DEC"], writes=["DEC"])
            P.op("act", lambda e, c0=c0: e.activation(DEC[:, c0:c0 + 4], DEC[:, c0:c0 + 4], AF.Exp), reads=["DEC"], writes=["DEC"])
            pc = ps0[:, 0:48].rearrange("p (c j h) -> p c j h", j=3, h=4)
            srcs = (Aa, WK, M["EM"]) if full else (Aa, WK)
            nj = len(srcs)
            n = 0
            for cl in range(4):
                for j in range(nj):
                    n += 1
                    P.op("pe", lambda e, cl=cl, j=j: e.transpose(pc[:, cl, j, :], srcs[j][0:4, cl * 128:(cl + 1) * 128], identf[0:4, 0:4]),
                         reads=["Aa", "WK", "EM", "identf"], writes=["ps0"], signal=(n == 4 * nj))
            P.op("act", lambda e, c0=c0: e.copy(M["COL"][:, c0:c0 + 4, 0:nj, :], pc[:, :, 0:nj, :]), reads=["ps0"], writes=["COL"])
            P.op("dve", lambda e, c0=c0: e.tensor_scalar(M["COL"][:, c0:c0 + 4, 0, :], M["COL"][:, c0:c0 + 4, 0, :], -2.7725887222, None, ALU.add), reads=["COL"], writes=["COL"])
            pd = ps1[:, 0:16].rearrange("p (h c) -> p h c", h=4)
            for h in range(4):
                P.op("pe", lambda e, h=h, c0=c0: e.matmul(pd[:, h, :], sel4[0:4, h, :], DEC[0:4, c0:c0 + 4], start=True, stop=True),
                     reads=["sel4", "DEC"], writes=["ps1"], signal=(h == 3))
            P.op("act", lambda e, c0=c0: e.copy(M["DECB"][:, :, c0:c0 + 4], pd), reads=["ps1"], writes=["DECB"])

    def mlstm_head(M, h, Xcols, Xres, mode, obT_dst=None, obT_res=None):
        nt, nch, full, nloc = M["nt"], M["nch"], M["full"], M["nloc"]
        samp = mode == "samp"
        wm = M["wm"]
        kT = M["kT"]; xpre = M["xpre"]; bnd = M["bnd"]; Vaug = M["Vaug"]; COL = M["COL"]; DECB = M["DECB"]
        nfc = 4 if full else 2
        W = 128 if samp else 512
        C3 = C32[:, h, :, :]
        C3 = C32[:, h, :, :]
        if mode == "hist":
            for fc in (0, 1):
                for k in range(8):
                    P.op("pe", lambda e, k=k, fc=fc: e.matmul(ps0[:, 0:128], wm[:, k, fc * 128:(fc + 1) * 128], Xcols(k, nt - 128, 128), start=(k == 0), stop=(k == 7)),
                         reads=["wm_q", Xres], writes=["ps0"], signal=(k == 7))
                P.op("act", lambda e, fc=fc: e.copy(cbnd[:, h, fc, :], ps0[:, 125:128]), reads=["ps0"], writes=["cbnd"])
        else:
            cst = M["ktmp"]
            npr = 4 if samp else 3
            lsel = (lambda k: xnSc[:, k, :]) if samp else (lambda k: Xcols(k, nt - 3, 3))
            for k in range(8):
                P.op("pe", lambda e, k=k: e.matmul(ps0[0:npr, :], lsel(k), wm[:, k, 0:512], start=(k == 0), stop=(k == 7)),
                     reads=["wm_q", "wm_k", Xres], writes=["ps0"], signal=(k == 7))
            P.op("act", lambda e: e.copy(cst[0:npr, :], ps0[0:npr, :]), reads=["ps0"], writes=["ktmp"])
            if samp:
                P.dma("sp", O["convs"][:, 2, h * 256:(h + 1) * 256], cst[0:4, 0:256], reads=["ktmp"], writes=["convsout"])
                P.dma("sp", O["convs"][:, 2, 1024 + h * 256:1024 + (h + 1) * 256], cst[0:4, 256:512], reads=["ktmp"], writes=["convsout"])
            else:
                P.dma("sp", O["convp"][:, h * 256:(h + 1) * 256], cst[0:3, 0:256], reads=["ktmp"], writes=["convpout"])
                P.dma("sp", O["convp"][:, 1024 + h * 256:1024 + (h + 1) * 256], cst[0:3, 256:512], reads=["ktmp"], writes=["convpout"])
        for hf in range(nt // nloc):
            def b_chunks(cls):
                for cl in cls:
                    c = hf * (nloc // 128) + cl
                    pz, pzr = (ps0, "ps0") if cl % 2 == 0 else (ps1, "ps1")
                    ncols = 512 if full else 256
                    for k in range(8):
                        P.op("pe", lambda e, k=k, c=c, pz=pz: e.matmul(pz[:, 0:ncols], Xcols(k, c * 128, 128), wm[:, k, 512:512 + ncols], start=(k == 0), stop=(k == 7)),
                             reads=["wm_v", "wm_o", Xres], writes=[pzr], signal=(k == 7))
                    P.op("act", lambda e, cl=cl, pz=pz: e.copy(Vaug[:, cl, 0:256], pz[:, 0:256]), reads=[pzr], writes=["Vaug"])
                    if full:
                        P.op("act", lambda e, pz=pz, cl=cl: e.activation(M["G"][:, cl, :], pz[:, 256:512], AF.Sigmoid), reads=[pzr], writes=["G"])

            for tl in range(nloc // W):
                tt = hf * (nloc // W) + tl
                t0 = tt * W
                l0 = tl * W
                for fi in range(nfc):
                    fc = fi + (4 - nfc)
                    chunk = (fc // 2) * 8 + h * 2 + fc % 2
                    pz, pzr = (ps0, "ps0") if fi % 2 == 0 else (ps1, "ps1")
                    xp = xpre[:, fi, :]
                    xr = "xpre%d" % fi
                    wres = "wm_q" if fc < 2 else "wm_k"
                    if samp:
                        P.op("dve", lambda e, xp=xp, chunk=chunk, tt=tt: e.tensor_copy(xp[:, 0:3], M["scT"][:, tt, chunk, :]), reads=["scT"], writes=[xr])
                    elif tt > 0:
                        P.op("dve", lambda e, fi=fi, xp=xp: e.tensor_copy(bnd[:, fi, :], xp[:, 512:515]), reads=[xr], writes=["bnd%d" % fi])
                        P.op("dve", lambda e, fi=fi, xp=xp: e.tensor_copy(xp[:, 0:3], bnd[:, fi, :]), reads=["bnd%d" % fi], writes=[xr])
                    elif mode == "hist":
                        P.op("dve", lambda e, xp=xp: e.memset(xp[:, 0:3], 0.0), writes=[xr])
                    else:
                        P.op("dve", lambda e, xp=xp, fc=fc: e.tensor_scalar(xp[:, 0:3], cbnd[:, h, fc, :], flag[:, 0:1], None, ALU.mult),
                             reads=["cbnd", "flag"], writes=[xr])
                    for k in range(8):
                        P.op("pe", lambda e, k=k, pz=pz, fc=fc, t0=t0: e.matmul(pz[:, 0:W], wm[:, k, fc * 128:(fc + 1) * 128], Xcols(k, t0, W), start=(k == 0), stop=(k == 7)),
                             reads=[wres, Xres], writes=[pzr], signal=(k == 7))
                    P.op("act", lambda e, pz=pz, xp=xp: e.copy(xp[:, 3:3 + W], pz[:, 0:W]), reads=[pzr], writes=[xr])
                b_chunks(range(tl * (W // 128), (tl + 1) * (W // 128)))
                for fi in range(nfc):
                    fc = fi + (4 - nfc)
                    chunk = (fc // 2) * 8 + h * 2 + fc % 2
                    xp = xpre[:, fi, :]
                    xr = "xpre%d" % fi
                    ca = M["cacc"][fi % 2]; car = "cacc%d" % (fi % 2)
                    P.op("dve", lambda e, xp=xp, chunk=chunk, ca=ca: e.tensor_scalar(ca[:, 0:W], xp[:, 3:3 + W], cw[:, chunk, 3:4], cb[:, chunk:chunk + 1], ALU.mult, ALU.add),
                         reads=[xr, "cw", "cb"], writes=[car])
                    for j in (2, 1, 0):
                        P.op("dve", lambda e, xp=xp, chunk=chunk, j=j, ca=ca: e.scalar_tensor_tensor(ca[:, 0:W], xp[:, j:j + W], cw[:, chunk, j:j + 1], ca[:, 0:W], ALU.mult, ALU.add),
                             reads=[xr, "cw", car], writes=[car])
                    if fc < 2:
                        P.op("act", lambda e, fc=fc, l0=l0, ca=ca: e.activation(M["qT"][:, fc, l0:l0 + W], ca[:, 0:W], AF.Silu), reads=[car], writes=["qT"])
                    else:
                        P.op("act", lambda e, ca=ca, fc=fc, l0=l0: e.activation(kT[:, fc - 2, l0:l0 + W], ca[:, 0:W], AF.Silu), reads=[car], writes=["kT"])
                    if mode == "hist" and t0 + W == nt:
                        P.op("dve", lambda e, xp=xp, fc=fc: e.tensor_copy(cbnd[:, h, fc, :], xp[:, 512:515]), reads=[xr], writes=["cbnd"])
            nlc = nloc // 128
            for cl in range(nlc):
                c = hf * nlc + cl
                ls = slice(cl * 128, (cl + 1) * 128)
                gs = slice(c * 128, (c + 1) * 128)
                par = cl % 2
                if full:
                    qT = M["qT"]
                    pS, pSr = (psA[:, 0:128], "psA0") if par == 0 else (ps6[:, 0:128], "ps6")
                    pB, pBr = (psA[:, 512:640], "psA1") if par == 0 else (ps7[:, 0:128], "ps7")
                    pW, pWr = (ps0[:, 0:128], "ps0") if par == 0 else (ps1[:, 0:128], "ps1")
                    Dt = M["Dt"][par]
                    for k2 in range(2):
                        P.op("pe", lambda e, k2=k2: e.matmul(pS, kT[:, k2, ls], qT[:, k2, ls], start=(k2 == 0), stop=(k2 == 1)),
                             reads=["kT", "qT"], writes=[pSr], signal=(k2 == 1))
                    P.op("pe", lambda e: e.matmul(pB, sel4[0:4, h, :], M["NEGM"][0:4, gs], start=True, stop=False),
                         reads=["sel4", "NEGM"], writes=[pBr], signal=False)
                    P.op("pe", lambda e: e.matmul(pB, ident[:, :], mbias[:, :], start=False, stop=True),
                         reads=["ident", "mbias"], writes=[pBr])
                    P.op("pe", lambda e: e.matmul(pW, sel4[0:4, h, :], M["WI"][0:4, gs], start=True, stop=True),
                         reads=["sel4", "WI"], writes=[pWr])
                    P.op("act", lambda e, c=c: e.activation(Dt[:, :], pB, AF.Exp, bias=COL[:, c, 0, h:h + 1]), reads=[pBr, "COL"], writes=["Dt%d" % par])
                    P.op("dve", lambda e, cl=cl: e.tensor_tensor(M["wTa"][:, cl, :], Dt[:, :], pS, ALU.mult), reads=["Dt%d" % par, pSr], writes=["wT%d" % cl])
                    for k2 in range(2):
                        P.op("dve", lambda e, k2=k2, cl=cl: e.tensor_tensor(M["qpTa"][:, cl, k2, :], qT[:, k2, ls], pW, ALU.mult), reads=["qT", pWr], writes=["qpT%d" % cl])
                for k2 in range(2):
                    P.op("pe", lambda e, k2=k2: e.transpose(psT[:, par * 256 + k2 * 128:par * 256 + (k2 + 1) * 128], kT[:, k2, ls], ident[:, :]),
                         reads=["kT", "ident"], writes=["psT"], signal=(k2 == 1))
                P.op("act", lambda e, c=c, cl=cl: e.activation(M["kwa"][:, cl, :], psT[:, par * 256:(par + 1) * 256], AF.Copy, scale=COL[:, c, 1, h:h + 1]),
                     reads=["psT", "COL"], writes=["kw%d" % cl])
            def stageA(cl):
                c = hf * nlc + cl
                i = cl % 2
                if samp:
                    C3 = M["Csb"][i][:, :, :]
                    C3r = "Csb%d" % i
                    P.dma("sp", C3[:, :, 0:256], I["sC"][c, h, :, :].rearrange("(k p) v -> p k v", p=128), writes=[C3r])
                    P.dma("sp", C3[:, :, 256:257], I["sn"][c, h, :].rearrange("(k p o) -> p k o", p=128, o=1), writes=[C3r], allow_slow_non_contiguous=True)
                else:
                    C3 = C32[:, h, :, :]
                    C3r = "C32"
                pCs = ((ps6, "ps6"), (ps7, "ps7")) if i == 0 else ((ps0, "ps0"), (ps1, "ps1"))
                for k2, (pC, pCr) in enumerate(pCs):
                    P.op("pe", lambda e, k2=k2, pC=pC: e.matmul(pC[:, 0:257], M["kwa"][:, cl, k2 * 128:(k2 + 1) * 128], Vaug[:, cl, :], start=True, stop=True),
                         reads=["kw%d" % cl, "Vaug"], writes=[pCr])
                if full:
                    Cbf = M["Cbf"][i]; Cbr = "Cbf%d" % i
                    pN, pNr = (ps4[:, 0:257], "ps4") if i == 0 else (psA[:, 0:257], "psA0")
                    P.op("act", lambda e: e.copy(Cbf[:, :, :], C3), reads=[C3r], writes=[Cbr])
                    P.op("pe", lambda e: e.matmul(pN, M["wTa"][:, cl, :], Vaug[:, cl, :], start=True, stop=False), reads=["wT%d" % cl, "Vaug"], writes=[pNr], signal=False)
                    P.op("pe", lambda e: e.matmul(pN, M["qpTa"][:, cl, 0, :], Cbf[:, 0, :], start=False, stop=False), reads=["qpT%d" % cl, Cbr], writes=[pNr], signal=False)
                    P.op("pe", lambda e: e.matmul(pN, M["qpTa"][:, cl, 1, :], Cbf[:, 1, :], start=False, stop=True), reads=["qpT%d" % cl, Cbr], writes=[pNr])
                for k2, (pC, pCr) in enumerate(pCs):
                    P.op("dve", lambda e, k2=k2, pC=pC: e.scalar_tensor_tensor(C3[:, k2, :], C3[:, k2, :], DECB[:, h, c:c + 1], pC[:, 0:257], ALU.mult, ALU.add),
                         reads=[C3r, "DECB", pCr], writes=[C3r])
                if samp:
                    P.dma("sp", O["Cs"][c, h, :, :].rearrange("(k p) v -> p k v", p=128), C3[:, :, 0:256], reads=[C3r], writes=["Csout"])
                    P.dma("sp", O["ns"][c, h, :].rearrange("(k p o) -> p k o", p=128, o=1), C3[:, :, 256:257], reads=[C3r], writes=["nsout"], allow_slow_non_contiguous=True)

            def stageB(cl):
                c = hf * nlc + cl
                i = cl % 2
                sm = M["sm"][i]; jk = M["jk"][i]; ob = M["ob"][i]
                smr = "sm%d" % i
                pN, pNr = (ps4[:, 0:257], "ps4") if i == 0 else (psA[:, 0:257], "psA0")
                P.op("act", lambda e: e.activation(jk[:, :], pN[:, 0:256], AF.Square, scale=1.0 / 16.0, accum_out=sm[:, 0:1]),
                     reads=[pNr], writes=["jk%d" % i, smr])
                P.op("dve", lambda e: e.scalar_tensor_tensor(sm[:, 1:2], pN[:, 256:257], pN[:, 256:257], COL[:, c, 2, h:h + 1], ALU.mult, ALU.max),
                     reads=[pNr, "COL", smr], writes=[smr])
                P.op("dve", lambda e: e.scalar_tensor_tensor(sm[:, 2:3], sm[:, 1:2], EPS, sm[:, 0:1], ALU.mult, ALU.add), reads=[smr], writes=[smr])
                P.op("act", lambda e: e.activation(sm[:, 2:3], sm[:, 2:3], AF.Sqrt), reads=[smr], writes=[smr])
                P.op("dve", lambda e: e.reciprocal(sm[:, 3:4], sm[:, 2:3]), reads=[smr], writes=[smr])
                P.op("dve", lambda e: e.scalar_tensor_tensor(ob[:, :], pN[:, 0:256], sm[:, 3:4], M["G"][:, cl, :], ALU.mult, ALU.mult),
                     reads=[pNr, smr, "G"], writes=["ob%d" % i])

            def stageC(cl):
                c = hf * nlc + cl
                i = cl % 2
                ob = M["ob"][i]
                gs = slice(c * 128, (c + 1) * 128)
                for j in range(2):
                    P.op("pe", lambda e, j=j: e.transpose(psT[:, 512 + i * 256 + j * 128:512 + i * 256 + (j + 1) * 128], ob[:, j * 128:(j + 1) * 128], ident[:, :]),
                         reads=["ob%d" % i, "ident"], writes=["psT"], signal=(j == 1))
                pob = psT[:, 512 + i * 256:512 + (i + 1) * 256].rearrange("p (j t) -> p j t", j=2)
                for j in range(2):
                    mcol = M["mhn"][:, h * 2 + j:h * 2 + j + 1]
                    if samp:
                        P.op("act", lambda e, j=j, mcol=mcol: e.activation(obT_dst[:, h * 2 + j, c:c + 1], pob[:, j, 0:1], AF.Copy, scale=mcol), reads=["psT", "mhncol"], writes=[obT_res])
                    else:
                        P.op("act", lambda e, j=j, mcol=mcol: e.activation(obT_dst[:, h * 2 + j, gs], pob[:, j, :], AF.Copy, scale=mcol), reads=["psT", "mhncol"], writes=[obT_res])

            for step in range(nlc + 2):
                if step < nlc:
                    stageA(step)
                if full and 0 <= step - 1 < nlc:
                    stageB(step - 1)
                if full and 0 <= step - 2 < nlc:
                    stageC(step - 2)
        if mode == "hist":
            P.op("dve", lambda e: e.tensor_scalar(C3, C3, flag[:, 0:1], None, ALU.mult), reads=["C32", "flag"], writes=["C32"])
        elif mode == "own":
            P.dma("sp", O["Cp"][h, :, :].rearrange("(k p) v -> p k v", p=128), C3[:, :, 0:256], reads=["C32"], writes=["Cpout"])
            P.dma("sp", O["np"][h, :].rearrange("(k p o) -> p k o", p=128, o=1), C3[:, :, 256:257], reads=["C32"], writes=["npout"], allow_slow_non_contiguous=True)

    def load_wm(M, h):
        for j, off in enumerate((OFF_MQ, OFF_MK, OFF_MV, OFF_MO)):
            load_w(M["wm"][:, :, j * 256:(j + 1) * 256], I["w_in_b"][9 + h * 4 + j, :, :, :], ("wm_q", "wm_k", "wm_v", "wm_o")[j])

    xn_cols = lambda k, s, n: xnT[:, k, s:s + n]

    P.push()
    M = mlstm_scope(NT, False)
    P.op("dve", lambda e: e.memset(M["MP"][:, 0:1], 0.0), writes=["MP"])
    mlstm_gates(M, lambda k, cs: xnT[:, k, cs], "xnT", "hist")
    P.op("dve", lambda e: e.tensor_scalar(MPh[:, :], M["MP"][:, 16:17], flag[0:4, 0:1], None, ALU.mult), reads=["MP", "flag"], writes=["MPh"])
    for h in range(4):
        for g_ in range(3):
            W_ = WINS[g_]
            P.dma("sp", O["kvs%d" % g_][h, 0:W_ - 1, :], I["ck%d" % g_][h, 1:W_, :], writes=["kvscopy"])
        P.dma("sp", O["convs"][h, 0:2, :], I["sconv"][h, 1:3, :], writes=["convscopy"])
        load_wm(M, h)
        mlstm_head(M, h, xn_cols, "xnT", "hist")
    P.pop()
    phase_done(3)

    phase_norm(NT, False)
    phase_done(4)

    P.push()
    A = attn_scope()
    KTo = P.sb("KTo", [128, 2, 16, 128], BF16)
    QTo = P.sb("QTo", [128, 2, 16, 128], BF16)
    VAo = P.sb("VAo", [128, 16, 4, 65], BF16)
    acc = P.sb("acc", [65, 4, NT], F32)
    accs = P.sb("accs", [65, 4, 4], F32)
    PT = [P.sb("PT%d" % i, [128, 2, 2, 2, 128], BF16) for i in range(2)]
    ckst = [P.sb("ckst%d" % i, [128, 512], F32) for i in range(2)]
    ckb = [P.sb("ckb%d" % i, [128, 256], BF16) for i in range(2)]
    KTs = [P.sb("KTs%d" % i, [128, 2, 128], BF16) for i in range(2)]
    VAs = [P.sb("VAs%d" % i, [128, 4, 65], BF16) for i in range(2)]
    KTn = [P.sb("KTn%d" % i, [128, 2, 128], BF16) for i in range(2)]
    QTn = [P.sb("QTn%d" % i, [128, 2, 128], BF16) for i in range(2)]
    VAn = [P.sb("VAn%d" % i, [128, 4, 65], BF16) for i in range(2)]
    xnSb = [P.sb("xnSb%d" % i, [128, 8, 128], BF16) for i in range(2)]
    P.op("dve", lambda e: e.memset(VAo[:], 1.0), writes=["VAo%d" % k for k in range(16)])
    for i in range(2):
        P.op("dve", lambda e, i=i: e.memset(VAs[i][:], 1.0), writes=["VAs%d" % i])
        P.op("dve", lambda e, i=i: e.memset(VAn[i][:], 1.0), writes=["VAn%d" % i])
        P.op("dve", lambda e, i=i: e.memset(xnSb[i][:], 0.0), writes=["xnSb%d" % i])
    def own_item(g, s, r, it):
        d = DILS[g]
        span = 128 * d
        nsp = NT // span
        ob_ = s * d + r
        start = s * span + r
        kv_out = O["kvp%d" % g][ssl(r, 128, d), :] if s == nsp - 1 else None
        if s == 0:
            hb = HB[g] + r
            KTp, KTpr, VAp, VApr, msk = KTh[:, :, hb, :], "KTh%d" % hb, VAh[:, hb, :, :], "VAh", mask2f
        else:
            pb_ = ob_ - d
            KTp, KTpr, VAp, VApr, msk = KTo[:, :, pb_, :], "KTo%d" % pb_, VAo[:, pb_, :, :], "VAo%d" % pb_, mask2

        def s1():
            if s == 0 and r == 0:
                load_wg(A, g)
            attn_p1(A, it, lambda k: xnT[:, k, ssl(start, 128, d)], "xnT", VAo[:, ob_, :, :], "VAo%d" % ob_)

        def s2():
            attn_p2(A, it, rope[:, RB[g] + d + ob_, :], "rope", kv_out=kv_out)

        def s3():
            attn_p3(A, it, KTo[:, :, ob_, :], "KTo%d" % ob_, QTo[:, :, ob_, :], "QTo%d" % ob_)

        def s4():
            attn_c1(it, QTo[:, :, ob_, :], "QTo%d" % ob_, KTp, KTpr, KTo[:, :, ob_, :], "KTo%d" % ob_, PT)

        def s5():
            attn_c2(it, VAp, VApr, VAo[:, ob_, :, :], "VAo%d" % ob_, msk, PT, acc[:, :, ssl(start, 128, d)], "acc", g == 0, 128)
        return (s1, s2, s3, s4, s5)

    def samp_seq(g, b, it, its):
        d = DILS[g]
        i = its % 2
        ck = I["ck%d" % g]
        kvs = O["kvs%d" % g]
        P.dma("sp", ckst[i][:, :], ck[b, ssl(0, 128, d), :], writes=["ckst%d" % i])
        P.op("act", lambda e: e.copy(VAs[i][:, :, 0:64], ckst[i][:, 256:512].rearrange("p (h d) -> p h d", d=64)), reads=["ckst%d" % i], writes=["VAs%d" % i])
        P.op("dve", lambda e: e.tensor_copy(ckb[i][:, :], ckst[i][:, 0:256]), reads=["ckst%d" % i], writes=["ckb%d" % i])
        for j in range(2):
            P.op("pe", lambda e, j=j: e.transpose(psT[:, 512 + j * 128:512 + (j + 1) * 128], ckb[i][:, j * 128:(j + 1) * 128], ident[:, :]),
                 reads=["ckb%d" % i, "ident"], writes=["psT"], signal=(j == 1))
        P.op("act", lambda e: e.copy(KTs[i][:, :, :], psT[:, 512:768].rearrange("p (j t) -> p j t", j=2)), reads=["psT"], writes=["KTs%d" % i])
        P.op("dve", lambda e: e.tensor_copy(xnSb[i][:, :, 0:1], xnSc[:, :, b:b + 1]), reads=["xnSc", "xnSb%d" % i], writes=["xnSb%d" % i])
        attn_p1(A, it, lambda k: xnSb[i][:, k, :], "xnSb%d" % i, VAn[i][:, :, :], "VAn%d" % i)
        attn_p2(A, it, rope_s[:, :], "rope_s", kv_out=kvs[b, WINS[g] - 1:WINS[g], :], kv_rows=1)
        attn_p3(A, it, KTn[i][:, :, :], "KTn%d" % i, QTn[i][:, :, :], "QTn%d" % i)
        attn_c1(it, QTn[i][:, :, :], "QTn%d" % i, KTs[i][:, :, :], "KTs%d" % i, KTn[i][:, :, :], "KTn%d" % i, PT)
        attn_c2(it, VAs[i][:, :, :], "VAs%d" % i, VAn[i][:, :, :], "VAn%d" % i, mask2, PT, accs[:, :, b:b + 1], "accs", g == 0, 1)

    it = 0
    its = 0
    for g in range(3):
        d = DILS[g]
        items = []
        for s in range(NT // (128 * d)):
            for r in range(d):
                items.append(own_item(g, s, r, it))
                it += 1
        run_pipeline(items)
        for b in range(4):
            samp_seq(g, b, it, its)
            it += 1
            its += 1
    phase_done(4.5)
    rd = [P.sb("rd%d" % i, [64, 512], F32) for i in range(2)]
    n = 0
    for h in range(4):
        for tt in range(4):
            cs = slice(tt * 512, (tt + 1) * 512)
            i = n % 2
            pz, pzr = (ps0, "ps0") if i == 0 else (ps1, "ps1")
            P.op("pe", lambda e, h=h, cs=cs, pz=pz: e.matmul(pz[0:64, :], ones[64:65, 0:64], acc[64:65, h, cs], start=True, stop=True),
                 reads=["ones", "acc"], writes=[pzr])
            P.op("dve", lambda e, pz=pz, i=i: e.reciprocal(rd[i][:, :], pz[0:64, :]), reads=[pzr], writes=["rd%d" % i])
            P.op("dve", lambda e, h=h, cs=cs, i=i: e.tensor_tensor(oaT[:, h, cs], acc[0:64, h, cs], rd[i][:, :], ALU.mult), reads=["acc", "rd%d" % i], writes=["oaT"])
            n += 1
    for h in range(4):
        P.op("pe", lambda e, h=h: e.matmul(ps0[0:64, 0:4], ones[64:65, 0:64], accs[64:65, h, :], start=True, stop=True), reads=["ones", "accs"], writes=["ps0"])
        P.op("dve", lambda e: e.reciprocal(rd[0][:, 0:4], ps0[0:64, 0:4]), reads=["ps0"], writes=["rd0"])
        P.op("dve", lambda e, h=h: e.tensor_tensor(oaTs[:, h, :], accs[0:64, h, :], rd[0][:, 0:4], ALU.mult), reads=["accs", "rd0"], writes=["oaTs"])
    P.pop()
    P.pop()
    phase_done(5)

    P.push()
    obT = P.sb("obT", [128, 8, NT], BF16)
    obTs = P.sb("obTs", [128, 8, 4], BF16)
    P.push()
    M = mlstm_scope(NT, True)
    P.op("dve", lambda e: e.tensor_copy(M["MP"][:, 0:1], MPh[:, :]), reads=["MPh"], writes=["MP"])
    mlstm_gates(M, lambda k, cs: xnT[:, k, cs], "xnT", "own")
    P.dma("sp", O["mp"][:, :], M["MP"][:, 16:17], reads=["MP"], writes=["mpout"])
    for h in range(4):
        load_wm(M, h)
        mlstm_head(M, h, xn_cols, "xnT", "own", obT, "obT")
    P.pop()
    phase_done(6)
    P.push()
    M = mlstm_scope(512, True)
    xnS = P.sb("xnS", [128, 8, 512], BF16)
    P.op("dve", lambda e: e.memset(xnS[:], 0.0), writes=["xnS"])
    P.op("dve", lambda e: e.tensor_copy(xnS[:, :, 0:512:128], xnSc[:, :, :]), reads=["xnSc", "xnS"], writes=["xnS"])
    P.dma("sp", M["MP"][:, 0:4], I["smT"][:, :], writes=["MP"])
    M["scT"] = P.sb("scT", [128, 4, 16, 3], F32)
    P.dma("sp", M["scT"][:], I["sconvT"][:, :, :, :], writes=["scT"])
    mlstm_gates(M, lambda k, cs: xnSc[:, k, :], "xnSc", "samp")
    P.dma("sp", O["ms"][:, :], M["MN"][:, :], reads=["MN"], writes=["msout"])
    for h in range(4):
        load_wm(M, h)
        mlstm_head(M, h, lambda k, s, n: xnS[:, k, s:s + n], "xnS", "samp", obTs, "obTs")
    P.pop()
    phase_done(7)

    P.push()
    wga = P.sb("wga", [128, 8, 2048], BF16)
    wpb = P.sb("wpb", [128, 8, D], BF16)
    wpa = P.sb("wpa", [64, 4, D], BF16)
    mixst = P.sb("mixst", [128, 8, 512], BF16)
    sga = [P.sb("sga%d" % i, [128, 512], F32) for i in range(2)]
    sgb = [P.sb("sgb%d" % i, [128, 512], F32) for i in range(2)]
    t1 = [P.sb("t1_%d" % i, [128, 512], F32) for i in range(2)]
    wpb_v = I["w_pb"].rearrange("(k p) c -> p k c", p=128)
    for j in (0, 4):
        load_w(wga[:, :, j * 256:(j + 1) * 256], I["w_in_b"][25 + j, :, :, :], "wga%d" % j)
    load_w(wpa[:, :, :], I["w_pa"].rearrange("(h p) c -> p h c", p=64), "wpa")
    load_w(wpb[:, :, 0:512], wpb_v[:, :, 0:512], "wpb0")
    for j in (1, 5, 2, 6):
        load_w(wga[:, :, j * 256:(j + 1) * 256], I["w_in_b"][25 + j, :, :, :], "wga%d" % j)
    load_w(wpb[:, :, 512:1024], wpb_v[:, :, 512:1024], "wpb1")
    for j in (3, 7):
        load_w(wga[:, :, j * 256:(j + 1) * 256], I["w_in_b"][25 + j, :, :, :], "wga%d" % j)
    n = 0
    for tt in range(5):
        if tt < 4:
            N = 512
            xs_ = lambda k, tt=tt: xnT[:, k, tt * 512:(tt + 1) * 512]
            oas = lambda h, tt=tt: oaT[:, h, tt * 512:(tt + 1) * 512]
            obs = lambda k, tt=tt: obT[:, k, tt * 512:(tt + 1) * 512]
            rr = ["xnT", "oaT", "obT"]
        else:
            N = 4
            xs_ = lambda k: xnSc[:, k, :]
            oas = lambda h: oaTs[:, h, :]
            obs = lambda k: obTs[:, k, :]
            rr = ["xnSc", "oaTs", "obTs"]
        for c in range(8):
            i = n % 2
            for k in range(8):
                P.op("pe", lambda e, k=k, c=c: e.matmul(ps0[:, 0:N], wga[:, k, c * 128:(c + 1) * 128], xs_(k), start=(k == 0), stop=(k == 7)),
                     reads=["wga%d" % (c // 2)] + rr, writes=["ps0"], signal=(k == 7))
            for k in range(8):
                P.op("pe", lambda e, k=k, c=c: e.matmul(ps1[:, 0:N], wga[:, k, 1024 + c * 128:1024 + (c + 1) * 128], xs_(k), start=(k == 0), stop=(k == 7)),
                     reads=["wga%d" % (4 + c // 2)] + rr, writes=["ps1"], signal=(k == 7))
            for h in range(4):
                P.op("pe", lambda e, h=h, c=c: e.matmul(ps6[:, 0:N], wpa[:, h, c * 128:(c + 1) * 128], oas(h), start=(h == 0), stop=(h == 3)),
                     reads=["wpa"] + rr, writes=["ps6"], signal=(h == 3))
            for k in range(8):
                P.op("pe", lambda e, k=k, c=c: e.matmul(ps7[:, 0:N], wpb[:, k, c * 128:(c + 1) * 128], obs(k), start=(k == 0), stop=(k == 7)),
                     reads=["wpb%d" % (c // 4)] + rr, writes=["ps7"], signal=(k == 7))
            P.op("act", lambda e, i=i: e.activation(sga[i][:, 0:N], ps0[:, 0:N], AF.Sigmoid), reads=["ps0"], writes=["sga%d" % i])
            P.op("act", lambda e, i=i: e.activation(sgb[i][:, 0:N], ps1[:, 0:N], AF.Sigmoid), reads=["ps1"], writes=["sgb%d" % i])
            P.op("dve", lambda e, i=i: e.tensor_tensor(t1[i][:, 0:N], sga[i][:, 0:N], ps6[:, 0:N], ALU.mult), reads=["sga%d" % i, "ps6"], writes=["t1_%d" % i])
            P.op("dve", lambda e, i=i: e.tensor_tensor(sgb[i][:, 0:N], sgb[i][:, 0:N], ps7[:, 0:N], ALU.mult), reads=["sgb%d" % i, "ps7"], writes=["sgb%d" % i])
            if tt < 4:
                dst, dres = mixst[:, c, :], "mixst"
            else:
                dst, dres = mixS[:, c, :], "mixS"
            P.op("dve", lambda e, i=i, dst=dst: e.tensor_tensor(dst, t1[i][:, 0:N], sgb[i][:, 0:N], ALU.add), reads=["t1_%d" % i, "sgb%d" % i], writes=[dres])
            n += 1
        if tt < 4:
            P.op("act", lambda e, tt=tt: e.copy(xnT[:, :, tt * 512:(tt + 1) * 512], mixst[:, :, :]), reads=["mixst"], writes=["xnT"])
    P.pop()
    P.pop()
    P.pop()
    phase_done(8)

    P.push()
    mixT = xnT
    wout = P.sb("wout", [128, 8, D], BF16)
    wpg = P.sb("wpg", [128, 8, D], BF16)
    wpp = P.sb("wpp", [128, 2, D], BF16)
    gffn = P.sb("gffn", [128, D], F32)
    gple = P.sb("gple", [128, D], F32)
    gfin = P.sb("gfin", [128, D], F32)
    h32 = [P.sb("h32_%d" % i, [128, D], F32) for i in range(4)]
    xfT = P.sb("xfT", [128, 8, 512], BF16)
    hidT = P.sb("hidT", [128, NJ, 512], BF16)
    wgu = [P.sb("wgu%d" % i, [128, 8, 256], BF16) for i in range(5)]
    wdn = [P.sb("wdn%d" % i, [128, D], BF16) for i in range(6)]
    sgu = [P.sb("sgu%d" % i, [128, 512], BF16) for i in range(2)]
    sga2 = [P.sb("sga2_%d" % i, [128, 512], F32) for i in range(2)]
    pst = [P.sb("pst%d" % i, [128, 256], F32) for i in range(2)]
    pbf = [P.sb("pbf%d" % i, [128, 256], BF16) for i in range(2)]
    ppT = P.sb("ppT", [128, 2, 512], BF16)
    st = make_stage(with_xt=False)
    ss2 = [P.sb("ss2_%d" % i, [128, 1], F32) for i in range(2)]
    rs2 = [P.sb("rs2_%d" % i, [128, 1], F32) for i in range(2)]
    load_w(wout[:, :, :], I["w_out"].rearrange("(k p) c -> p k c", p=128), "wout")
    load_w(wpg[:, :, :], I["w_pg"].rearrange("(k p) c -> p k c", p=128), "wpg")
    load_w(wpp[:, :, :], I["w_pp"].rearrange("(k p) c -> p k c", p=128), "wpp")
    P.dma("sp", gffn[:], I["g_ffn"][0, :].partition_broadcast(128), writes=["gffn"])
    P.dma("sp", gple[:], I["g_ple"][0, :].partition_broadcast(128), writes=["gple"])
    P.dma("sp", gfin[:], I["g_fin"][0, :].partition_broadcast(128), writes=["gfin"])
    wdn_v = I["w_down"].rearrange("(j p) c -> p j c", p=128)
    nit = 0
    for tt in range(5):
        samp = (tt == 4)
        nb = 1 if samp else 4
        npp = 4 if samp else 128
        ncol = 4 if samp else 512
        def res_item(tb, ni):
            hh = h32[tb]; hr = "h32_%d" % tb
            i = ni % 2
            t = "n%d" % i
            _, junk, ss, rs, xnb = st

            def r1():
                if samp:
                    P.dma("sp", hh[0:4, :], I["xs"][0:4, :], writes=[hr])
                    lhs = lambda k: mixS[:, k, :]
                    lres = "mixS"
                else:
                    r0 = NT + tt * 512 + tb * 128
                    P.dma("sp", hh[:, :], I["x2"][r0:r0 + 128, :], writes=[hr])
                    lhs = lambda k: mixT[:, k, tt * 512 + tb * 128: tt * 512 + (tb + 1) * 128]
                    lres = "xnT"
                for half in range(2):
                    pz, pzr = banks[(tb % 2) * 2 + half]
                    for k in range(8):
                        P.op("pe", lambda e, k=k, pz=pz, half=half: e.matmul(pz[0:npp, :], lhs(k), wout[:, k, half * 512:(half + 1) * 512], start=(k == 0), stop=(k == 7)),
                             reads=[lres, "wout"], writes=[pzr], signal=(k == 7))
                    P.op("dve", lambda e, pz=pz, half=half: e.tensor_tensor(hh[0:npp, half * 512:(half + 1) * 512], hh[0:npp, half * 512:(half + 1) * 512], pz[0:npp, :], ALU.add),
                         reads=[hr, pzr], writes=[hr])

            def r2():
                rms_rstd(hh[0:npp, :], junk[i][0:npp, :], ss[i][0:npp, :], rs[i][0:npp, :], [hr], t)
                P.op("dve", lambda e: e.scalar_tensor_tensor(xnb[i][0:npp, :], hh[0:npp, :], rs[i][0:npp, 0:1], gffn[0:npp, :], ALU.mult, ALU.mult),
                     reads=[hr, "rs" + t, "gffn"], writes=["xnb%d" % i])

            def r3():
                for k in range(8):
                    P.op("pe", lambda e, k=k: e.transpose(psT[:, k * 128:k * 128 + npp], xnb[i][0:npp, k * 128:(k + 1) * 128], ident[0:npp, 0:npp]),
                         reads=["xnb%d" % i, "ident"], writes=["psT"], signal=(k == 7))
                P.op("act", lambda e: e.copy(xfT[:, :, tb * 128:tb * 128 + npp], psT[:, :].rearrange("p (k t) -> p k t", k=8)[:, :, 0:npp]),
                     reads=["psT"], writes=["xfT"])
            return (r1, r2, r3)

        items = []
        for tb in range(nb):
            items.append(res_item(tb, nit))
            nit += 1
        run_pipeline(items)
        for j in range(NJ):
            ws = j % 5
            load_w(wgu[ws][:, :, :], I["wgu_b"][j, :, :, :], "wgu%d" % ws)
            i = j % 2
            (pga, pgar), (pua, puar) = (banks[0], banks[1]) if i == 0 else (banks[2], banks[3])
            for k in range(8):
                P.op("pe", lambda e, k=k, ws=ws, pga=pga: e.matmul(pga[:, 0:ncol], wgu[ws][:, k, 0:128], xfT[:, k, 0:ncol], start=(k == 0), stop=(k == 7)),
                     reads=["wgu%d" % ws, "xfT"], writes=[pgar], signal=(k == 7))
            for k in range(8):
                P.op("pe", lambda e, k=k, ws=ws, pua=pua: e.matmul(pua[:, 0:ncol], wgu[ws][:, k, 128:256], xfT[:, k, 0:ncol], start=(k == 0), stop=(k == 7)),
                     reads=["wgu%d" % ws, "xfT"], writes=[puar], signal=(k == 7))
            P.op("act", lambda e, i=i, pga=pga: e.activation(sgu[i][:, 0:ncol], pga[:, 0:ncol], AF.Silu), reads=[pgar], writes=["sgu%d" % i])
            P.op("dve", lambda e, i=i, j=j, pua=pua: e.tensor_tensor(hidT[:, j, 0:ncol], sgu[i][:, 0:ncol], pua[:, 0:ncol], ALU.mult), reads=["sgu%d" % i, puar], writes=["hidT"])
        outs = [(tb, half) for tb in range(nb) for half in range(2)]
        for j in range(NJ):
            ws = j % 6
            load_w(wdn[ws][:, :], wdn_v[:, j, :], "wdn%d" % ws)
            for oi, (tb, half) in enumerate(outs):
                pz, pzr = banks[oi]
                P.op("pe", lambda e, j=j, ws=ws, pz=pz, tb=tb, half=half: e.matmul(pz[0:npp, 0:512], hidT[:, j, tb * 128:tb * 128 + npp], wdn[ws][:, half * 512:(half + 1) * 512],
                                                                                 start=(j == 0), stop=(j == NJ - 1), skip_group_check=True),
                     reads=["hidT", "wdn%d" % ws], writes=[pzr], signal=(j == NJ - 1 or oi == len(outs) - 1))
        for oi, (tb, half) in enumerate(outs):
            pz, pzr = banks[oi]
            hh = h32[tb]; hr = "h32_%d" % tb
            P.op("dve", lambda e, pz=pz, half=half, hh=hh: e.tensor_tensor(hh[0:npp, half * 512:(half + 1) * 512], hh[0:npp, half * 512:(half + 1) * 512], pz[0:npp, 0:512], ALU.add),
                 reads=[hr, pzr], writes=[hr])
        def ple_item(tb, ni):
            hh = h32[tb]; hr = "h32_%d" % tb
            i = ni % 2
            t = "n%d" % i
            pi = ni % 2
            _, junk, ss, rs, xnb = st

            def p1():
                rms_rstd(hh[0:npp, :], junk[i][0:npp, :], ss[i][0:npp, :], rs[i][0:npp, :], [hr], t)
                P.op("dve", lambda e: e.scalar_tensor_tensor(xnb[i][0:npp, :], hh[0:npp, :], rs[i][0:npp, 0:1], gple[0:npp, :], ALU.mult, ALU.mult),
                     reads=[hr, "rs" + t, "gple"], writes=["xnb%d" % i])
                if samp:
                    P.dma("sp", pst[pi][0:4, :], I["psm"][0:4, :], writes=["pst%d" % pi])
                else:
                    r0 = tt * 512 + tb * 128
                    P.dma("sp", pst[pi][:, :], I["po"][r0:r0 + 128, :], writes=["pst%d" % pi])
                P.op("dve", lambda e: e.tensor_copy(pbf[pi][0:npp, :], pst[pi][0:npp, :]), reads=["pst%d" % pi], writes=["pbf%d" % pi])

            def p2():
                for k in range(8):
                    P.op("pe", lambda e, k=k: e.transpose(psT[:, k * 128:k * 128 + npp], xnb[i][0:npp, k * 128:(k + 1) * 128], ident[0:npp, 0:npp]),
                         reads=["xnb%d" % i, "ident"], writes=["psT"], signal=(k == 7))
                P.op("act", lambda e: e.copy(xfT[:, :, tb * 128:tb * 128 + npp], psT[:, :].rearrange("p (k t) -> p k t", k=8)[:, :, 0:npp]),
                     reads=["psT"], writes=["xfT"])
                for k in range(2):
                    P.op("pe", lambda e, k=k: e.transpose(psT[:, k * 128:k * 128 + npp], pbf[pi][0:npp, k * 128:(k + 1) * 128], ident[0:npp, 0:npp]),
                         reads=["pbf%d" % pi, "ident"], writes=["psT"], signal=(k == 1))
                P.op("act", lambda e: e.copy(ppT[:, :, tb * 128:tb * 128 + npp], psT[:, 0:256].rearrange("p (k t) -> p k t", k=2)[:, :, 0:npp]),
                     reads=["psT"], writes=["ppT"])

            def p3():
                for half in range(2):
                    pg, pgr = banks[half * 2]
                    pp, ppr = banks[half * 2 + 1]
                    for k in range(8):
                        P.op("pe", lambda e, k=k, pg=pg, half=half: e.matmul(pg[0:npp, :], xfT[:, k, tb * 128:tb * 128 + npp], wpg[:, k, half * 512:(half + 1) * 512], start=(k == 0), stop=(k == 7)),
                             reads=["xfT", "wpg"], writes=[pgr], signal=(k == 7))
                    for k in range(2):
                        P.op("pe", lambda e, k=k, pp=pp, half=half: e.matmul(pp[0:npp, :], ppT[:, k, tb * 128:tb * 128 + npp], wpp[:, k, half * 512:(half + 1) * 512], start=(k == 0), stop=(k == 1)),
                             reads=["ppT", "wpp"], writes=[ppr], signal=(k == 1))
                    si = half
                    P.op("act", lambda e, pg=pg, si=si: e.activation(sga2[si][0:npp, :], pg[0:npp, :], AF.Sigmoid), reads=[pgr], writes=["sga2_%d" % si])
                    P.op("dve", lambda e, pp=pp, si=si: e.tensor_tensor(sga2[si][0:npp, :], sga2[si][0:npp, :], pp[0:npp, :], ALU.mult), reads=["sga2_%d" % si, ppr], writes=["sga2_%d" % si])
                    P.op("dve", lambda e, si=si, half=half: e.tensor_tensor(hh[0:npp, half * 512:(half + 1) * 512], hh[0:npp, half * 512:(half + 1) * 512], sga2[si][0:npp, :], ALU.add),
                         reads=[hr, "sga2_%d" % si], writes=[hr])

            def p4():
                rms_rstd(hh[0:npp, :], junk[i][0:npp, :], ss2[i][0:npp, :], rs2[i][0:npp, :], [hr], "f%d" % i)
                P.op("dve", lambda e: e.scalar_tensor_tensor(hh[0:npp, :], hh[0:npp, :], rs2[i][0:npp, 0:1], gfin[0:npp, :], ALU.mult, ALU.mult),
                     reads=[hr, "rsf%d" % i, "gfin"], writes=[hr])
                if samp:
                    P.dma("sp", O["ys"][0:4, :], hh[0:4, :], reads=[hr], writes=["ysout"])
                else:
                    r0 = tt * 512 + tb * 128
                    P.dma("sp", O["y"][r0:r0 + 128, :], hh[:, :], reads=[hr], writes=["yout"])
            return (p1, p2, p3, p4)

        items = []
        for tb in range(nb):
            items.append(ple_item(tb, nit))
            nit += 1
        run_pipeline(items)
    P.pop()
    P.pop()
    P.finish()


_NC = None


def _rope_tab(pos):
    inv = np.power(np.float32(500000.0), -np.arange(8, dtype=np.float32) / np.float32(8)).astype(np.float32)
    ang = pos.astype(np.float32)[:, None] * inv[None, :]
    return np.concatenate([np.cos(ang), np.sin(ang)], axis=1).astype(np.float32)


def _consts(half):
    j = np.arange(128)
    c = {}
    c["ident"] = np.eye(128, dtype=np.float32)
    c["mprev"] = (j[:, None] >= j[None, :]).astype(np.float32)
    c["mcur"] = (j[:, None] <= j[None, :]).astype(np.float32)
    c["mbias"] = np.where(j[:, None] <= j[None, :], 0.0, NEGPAD).astype(np.float32)
    sel = np.zeros((4, 4, 128), np.float32)
    for h in range(4):
        sel[h, h, :] = 1.0
    c["sel4"] = sel.reshape(4, 512)
    own0 = half * NT
    rope = np.zeros((128, 69, 16), np.float32)
    for g, d in enumerate(DILS):
        span = 128 * d
        for r in range(d):
            pos = own0 - span + r + d * j
            rope[:, RB[g] + r, :] = _rope_tab(np.maximum(pos, 0))
        for s in range(NT // span):
            for r in range(d):
                pos = own0 + s * span + r + d * j
                rope[:, RB[g] + d + s * d + r, :] = _rope_tab(pos)
    c["rope"] = rope
    c["rope_s"] = _rope_tab(np.full((128,), PAST))
    c["flag"] = np.full((128, 1), float(half), np.float32)
    return c


def _block_w_in(w):
    offs = []
    for g in range(3):
        offs += [OFF_AQ + g * 256, OFF_AK + g * 256, OFF_AV + g * 256]
    for h in range(4):
        offs += [OFF_MQ + h * 256, OFF_MK + h * 256, OFF_MV + h * 256, OFF_MO + h * 256]
    offs += [OFF_GA + j * 256 for j in range(8)]
    out = np.empty((33, 128, 8, 256), np.float32)
    for i, o in enumerate(offs):
        out[i] = w[:, o:o + 256].reshape(8, 128, 256).transpose(1, 0, 2)
    return out


def _block_gu(wg, wu):
    out = np.empty((NJ, 128, 8, 256), np.float32)
    out[:, :, :, 0:128] = wg.reshape(8, 128, NJ, 128).transpose(2, 1, 0, 3)
    out[:, :, :, 128:256] = wu.reshape(8, 128, NJ, 128).transpose(2, 1, 0, 3)
    return out


def kernel(x_prompt, x_sample, cache_kv_w128, cache_kv_w512, cache_kv_w2048, state_conv, state_C, state_n, state_m,
           p_prompt, p_sample, norm_mix, w_in, conv_w, conv_b, b_igate, b_fgate, mh_norm, w_proj_a, w_proj_b, w_out,
           norm_ffn, w_gate, w_up, w_down, norm_ple, w_ple_gate, w_ple_proj, norm_final):
    global _NC
    f = lambda a: np.ascontiguousarray(np.asarray(a, dtype=np.float32))
    x_prompt, x_sample, p_prompt, p_sample = f(x_prompt), f(x_sample), f(p_prompt), f(p_sample)
    caches = [f(cache_kv_w128), f(cache_kv_w512), f(cache_kv_w2048)]
    state_conv, state_C, state_n, state_m = f(state_conv), f(state_C), f(state_n), f(state_m)
    shared = {
        "w_in_b": _block_w_in(f(w_in)[0]), "w_gates": f(f(w_in)[0][:, OFF_MI:OFF_MI + 8].reshape(8, 128, 8).transpose(1, 0, 2)),
        "wgu_b": _block_gu(f(w_gate)[0], f(w_up)[0]), "cw": f(f(conv_w)[0].reshape(4, 16, 128).transpose(2, 1, 0)),
        "cb": f(f(conv_b)[0].reshape(16, 128).T), "b_i": f(b_igate)[0].reshape(4, 1), "b_f": f(b_fgate)[0].reshape(4, 1),
        "mhn_col": f(f(mh_norm)[0].reshape(8, 128).T), "w_pa": f(w_proj_a)[0], "w_pb": f(w_proj_b)[0], "w_out": f(w_out)[0],
        "g_mix": f(norm_mix)[0].reshape(1, D), "g_ffn": f(norm_ffn)[0].reshape(1, D), "g_ple": f(norm_ple)[0].reshape(1, D),
        "g_fin": f(norm_final).reshape(1, D), "w_down": f(w_down)[0],
        "w_pg": f(w_ple_gate)[0], "w_pp": f(w_ple_proj)[0],
    }
    cons = [_consts(0), _consts(1)]
    in_maps = []
    for c in range(8):
        b, half = c // 2, c % 2
        own = x_prompt[b, half * NT:(half + 1) * NT]
        hist = x_prompt[b, 0:NT]
        m = dict(shared)
        m.update(cons[half])
        m["x2"] = f(np.concatenate([hist, own], axis=0))
        xs = np.zeros((128, D), np.float32); xs[0:4] = x_sample[4 * c:4 * c + 4, 0]
        m["xs"] = xs
        m["po"] = f(p_prompt[0, b, half * NT:(half + 1) * NT])
        psm = np.zeros((128, 256), np.float32); psm[0:4] = p_sample[0, 4 * c:4 * c + 4, 0]
        m["psm"] = psm
        for g in range(3):
            m["ck%d" % g] = f(caches[g][0, 4 * c:4 * c + 4].reshape(4, WINS[g], 512))
        sc = state_conv[0, 4 * c:4 * c + 4]
        m["sconv"] = f(sc)
        m["sconvT"] = f(sc.reshape(4, 3, 16, 128).transpose(3, 0, 2, 1))
        m["sC"] = f(state_C[0, 4 * c:4 * c + 4])
        m["sn"] = f(state_n[0, 4 * c:4 * c + 4])
        m["smT"] = f(state_m[0, 4 * c:4 * c + 4].T)
        in_maps.append(m)
    if _NC is None:
        _NC = build()
    res = run_bass_kernel_spmd(_NC, in_maps, core_ids=list(range(8))).results

    y_prompt = np.zeros((4, SEQ, D), np.float32)
    y_sample = np.zeros((32, 1, D), np.float32)
    kvp = [np.zeros((1, 4, WINS[g], 2, 4, 64), np.float32) for g in range(3)]
    kvs = [np.zeros((1, 32, WINS[g], 2, 4, 64), np.float32) for g in range(3)]
    conv_p = np.zeros((1, 4, 3, 2048), np.float32); conv_s = np.zeros((1, 32, 3, 2048), np.float32)
    C_p = np.zeros((1, 4, 4, 256, 256), np.float32); C_s = np.zeros((1, 32, 4, 256, 256), np.float32)
    n_p = np.zeros((1, 4, 4, 256), np.float32); n_s = np.zeros((1, 32, 4, 256), np.float32)
    m_p = np.zeros((1, 4, 4), np.float32); m_s = np.zeros((1, 32, 4), np.float32)
    for c in range(8):
        b, half = c // 2, c % 2
        r = res[c]
        y_prompt[b, half * NT:(half + 1) * NT] = r["y"]
        y_sample[4 * c:4 * c + 4, 0] = r["ys"]
        for g in range(3):
            kvs[g][0, 4 * c:4 * c + 4] = r["kvs%d" % g].reshape(4, WINS[g], 2, 4, 64)
        conv_s[0, 4 * c:4 * c + 4] = r["convs"]
        C_s[0, 4 * c:4 * c + 4] = r["Cs"]
        n_s[0, 4 * c:4 * c + 4] = r["ns"]
        m_s[0, 4 * c:4 * c + 4] = r["ms"].T
        if half == 1:
            for g in range(3):
                kvp[g][0, b] = r["kvp%d" % g].reshape(WINS[g], 2, 4, 64)
            conv_p[0, b] = r["convp"]
            C_p[0, b] = r["Cp"]
            n_p[0, b] = r["np"]
            m_p[0, b] = r["mp"][:, 0]
    return (y_prompt, y_sample, kvp[0], kvs[0], kvp[1], kvs[1], kvp[2], kvs[2], conv_p, conv_s, C_p, C_s, n_p, n_s, m_p, m_s)
```

```python
from contextlib import ExitStack
import numpy as np
import concourse.bass as bass
import concourse.mybir as mybir
from concourse.bass_utils import run_bass_kernel_spmd

F32 = mybir.dt.float32
BF16 = mybir.dt.bfloat16
AF = mybir.ActivationFunctionType
ALU = mybir.AluOpType

COMPUTE = ("pe", "act", "dve", "pool")
ROLL = 24000

D = 1024
SEQ = 4096
NT = 2048
HD = 64
DILS = (1, 4, 16)
WINS = (128, 512, 2048)
PAST = 16384
OFF_AQ, OFF_AK, OFF_AV = 0, 768, 1536
OFF_MQ, OFF_MK, OFF_MV, OFF_MO = 2304, 3328, 4352, 5376
OFF_MI, OFF_MF, OFF_GA, OFF_GB = 6400, 6404, 6408, 7432
IN_COLS = 8456
DFF = 2816
NJ = DFF // 128
EPS = 1e-6
NEGPAD = -30000.0
HB = (0, 1, 5)
RB = (0, 17, 37)


class Res:
    __slots__ = ("name", "w", "rd", "psum")

    def __init__(self, name, psum=False):
        self.name = name
        self.w = None
        self.rd = []
        self.psum = psum


class _Rec:
    def __init__(self):
        self.call = None

    def __getattr__(self, name):
        def f(*a, **k):
            self.call = (name, a, k)
            return None
        return f


class Prog:
    def __init__(self, nc, n_dma_sems=12):
        self.nc = nc
        self.scopes = [ExitStack()]
        self.streams = {e: [] for e in ("pe", "act", "dve", "pool", "sp")}
        self.sem = {}
        self.cnt = {}
        self.seen = {e: {} for e in self.streams}
        self.nsem = 0
        for e in COMPUTE:
            self._new_sem(e)
        self.dma_sems = {}
        self.dma_tot = {}
        self.dma_rr = {}
        for q in ("sp", "pool"):
            self.dma_sems[q] = [self._alloc_sem("d%s%d" % (q, i)) for i in range(n_dma_sems)]
            self.dma_tot[q] = [0] * n_dma_sems
            self.dma_rr[q] = 0
        self.res = {}
        self.nops = 0

    def _alloc_sem(self, name):
        self.nsem += 1
        return self.scopes[0].enter_context(self.nc.semaphore(name))

    def _new_sem(self, e):
        self.sem[e] = self._alloc_sem("c%s%d" % (e, self.nsem))
        self.cnt[e] = 0

    def sb(self, name, shape, dt):
        self.ntens = getattr(self, "ntens", 0) + 1
        return self.scopes[-1].enter_context(self.nc.sbuf_tensor("s%d_%s" % (self.ntens, name), list(shape), dt))

    def ps(self, name, shape, dt):
        return self.scopes[0].enter_context(self.nc.psum_tensor("p_" + name, list(shape), dt))

    def push(self):
        self.scopes.append(ExitStack())

    def pop(self):
        self.flush()
        self.scopes.pop().close()

    def R(self, name, psum=False):
        r = self.res.get(name)
        if r is None:
            r = Res(name, psum)
            self.res[name] = r
        return r

    def _need(self, e, tok, deps, same_raw):
        if tok is None:
            return
        we, sem, val = tok
        if we == e and e == "pe":
            return
        deps.append((sem, val))

    def _waits(self, e, reads, writes):
        deps = []
        for r in reads:
            self._need(e, r.w, deps, True)
            if r.psum:
                for t in r.rd:
                    self._need(e, t, deps, False)
        for w in writes:
            self._need(e, w.w, deps, w.psum)
            for t in w.rd:
                self._need(e, t, deps, False)
        seen = self.seen[e]
        best = {}
        for sem, val in deps:
            k = id(sem)
            if seen.get(k, 0) >= val:
                continue
            if k not in best or best[k][1] < val:
                best[k] = (sem, val)
        for k, (sem, val) in best.items():
            seen[k] = val
            self.streams[e].append(("wait", sem, val))

    def _commit(self, tok, reads, writes):
        for r in reads:
            r.rd = [t for t in r.rd if not (t[1] is tok[1])] + [tok]
        for w in writes:
            w.w = tok
            w.rd = []

    def _rl(self, xs):
        return [self.R(x) if isinstance(x, str) else x for x in xs]

    def op(self, e, fn, reads=(), writes=(), signal=True):
        reads = self._rl(reads)
        writes = self._rl(writes)
        self._waits(e, reads, writes)
        self.nops += 1
        rec = _Rec()
        fn(rec)
        fn = rec.call
        if signal:
            if self.cnt[e] >= ROLL:
                self._new_sem(e)
            self.cnt[e] += 1
            tok = (e, self.sem[e], self.cnt[e])
            self.streams[e].append(("op", fn, self.sem[e]))
        else:
            tok = (e, self.sem[e], self.cnt[e] + 1)
            self.streams[e].append(("op", fn, None))
        self._commit(tok, reads, writes)
        return tok

    def dma(self, q, out_ap, in_ap, reads=(), writes=(), **kw):
        reads = self._rl(reads)
        writes = self._rl(writes)
        self._waits(q, reads, writes)
        i = self.dma_rr[q]
        self.dma_rr[q] = (i + 1) % len(self.dma_sems[q])
        sem = self.dma_sems[q][i]
        prev = self.dma_tot[q][i]
        if prev > 0 and self.seen[q].get(id(sem), 0) < prev:
            self.seen[q][id(sem)] = prev
            self.streams[q].append(("wait", sem, prev))
        self.dma_tot[q][i] = prev + 16
        tok = ("dma", sem, prev + 16)
        self.streams[q].append(("dma", out_ap, in_ap, sem, kw))
        self._commit(tok, reads, writes)
        return tok

    def barrier(self):
        toks = []
        for e in COMPUTE:
            if self.cnt[e] > 0:
                toks.append((self.sem[e], self.cnt[e]))
        for q in self.dma_sems:
            for sem, tot in zip(self.dma_sems[q], self.dma_tot[q]):
                if tot > 0:
                    toks.append((sem, tot))
        for e in self.streams:
            for sem, val in toks:
                if e in COMPUTE and sem is self.sem[e]:
                    continue
                if self.seen[e].get(id(sem), 0) < val:
                    self.seen[e][id(sem)] = val
                    self.streams[e].append(("wait", sem, val))

    def flush(self):
        self.barrier()
        nc = self.nc
        streams = self.streams

        def replay(eng, items):
            for it in items:
                if it[0] == "wait":
                    eng.wait_ge(it[1], it[2])
                elif it[0] == "op":
                    ins = getattr(eng, it[1][0])(*it[1][1], **it[1][2])
                    if it[2] is not None:
                        ins.then_inc(it[2], 1)
                else:
                    _, o, i, sem, kw = it
                    eng.dma_start(out=o, in_=i, **kw).then_inc(sem, 16)

        with nc.Block() as block:
            @block.sync
            def _(sync):
                replay(sync, streams["sp"])

            @block.tensor
            def _(t):
                replay(t, streams["pe"])

            @block.scalar
            def _(s):
                replay(s, streams["act"])

            @block.vector
            def _(v):
                replay(v, streams["dve"])

            @block.gpsimd
            def _(g):
                replay(g, streams["pool"])
        for e in streams:
            streams[e] = []

    def finish(self):
        self.flush()
        while self.scopes:
            self.scopes.pop().close()


def ssl(start, n, step):
    return slice(start, start + (n - 1) * step + 1, step)


class _Stop(Exception):
    pass


STOP_AFTER = None


def build():
    nc = bass.Bass("TRN2", target_bir_lowering=False)
    state = {}
    try:
        _build_inner(nc, state)
    except _Stop:
        state["P"].finish()
    return nc


def _build_inner(nc, state):
    I, O = {}, {}

    def din(n, shape):
        I[n] = nc.dram_tensor(n, list(shape), F32, kind="ExternalInput").ap()

    def dout(n, shape):
        O[n] = nc.dram_tensor(n, list(shape), F32, kind="ExternalOutput").ap()

    din("x2", [2 * NT, D]); din("xs", [128, D]); din("po", [NT, 256]); din("psm", [128, 256])
    din("ck0", [4, 128, 512]); din("ck1", [4, 512, 512]); din("ck2", [4, 2048, 512])
    din("sconvT", [128, 4, 16, 3]); din("sconv", [4, 3, 2048])
    din("sC", [4, 4, 256, 256]); din("sn", [4, 4, 256]); din("smT", [4, 4])
    din("w_in_b", [33, 128, 8, 256]); din("w_gates", [128, 8, 8]); din("cw", [128, 16, 4]); din("cb", [128, 16])
    din("b_i", [4, 1]); din("b_f", [4, 1]); din("mhn_col", [128, 8])
    din("w_pa", [256, D]); din("w_pb", [D, D]); din("w_out", [D, D])
    din("g_mix", [1, D]); din("g_ffn", [1, D]); din("g_ple", [1, D]); din("g_fin", [1, D])
    din("wgu_b", [NJ, 128, 8, 256]); din("w_down", [DFF, D])
    din("w_pg", [D, D]); din("w_pp", [256, D])
    din("ident", [128, 128]); din("mprev", [128, 128]); din("mcur", [128, 128]); din("mbias", [128, 128])
    din("sel4", [4, 512]); din("rope", [128, 69, 16]); din("rope_s", [128, 16]); din("flag", [128, 1])

    dout("y", [NT, D]); dout("ys", [4, D])
    dout("kvp0", [128, 512]); dout("kvp1", [512, 512]); dout("kvp2", [2048, 512])
    dout("kvs0", [4, 128, 512]); dout("kvs1", [4, 512, 512]); dout("kvs2", [4, 2048, 512])
    dout("convp", [3, 2048]); dout("convs", [4, 3, 2048])
    dout("Cp", [4, 256, 256]); dout("np", [4, 256]); dout("mp", [4, 1])
    dout("Cs", [4, 4, 256, 256]); dout("ns", [4, 4, 256]); dout("ms", [4, 4])

    P = Prog(nc)
    state["P"] = P

    def phase_done(i):
        if STOP_AFTER is not None and i >= STOP_AFTER:
            raise _Stop()

    ps0 = P.ps("ps0", [128, 512], F32)
    ps1 = P.ps("ps1", [128, 512], F32)
    psA = P.ps("psA", [128, 1024], F32)
    ps4 = P.ps("ps4", [128, 512], F32)
    psTf = P.ps("psT", [128, 512], F32)
    psT = psTf[:, :].bitcast(BF16)
    ps6 = P.ps("ps6", [128, 512], F32)
    ps7 = P.ps("ps7", [128, 512], F32)
    for n in ("ps0", "ps1", "psA0", "psA1", "ps4", "psT", "ps6", "ps7"):
        P.R(n, True)
    banks = [(ps0, "ps0"), (ps1, "ps1"), (psA[:, 0:512], "psA0"), (psA[:, 512:1024], "psA1"),
             (ps4, "ps4"), (ps6, "ps6"), (ps7, "ps7"), (psTf, "psT")]

    identf = P.sb("identf", [128, 128], F32)
    ident = P.sb("ident", [128, 128], BF16)
    mask2 = P.sb("mask2", [128, 2, 128], BF16)
    mask2f = P.sb("mask2f", [128, 2, 128], BF16)
    mbias = P.sb("mbias", [128, 128], BF16)
    sel4 = P.sb("sel4", [4, 4, 128], F32)
    rope = P.sb("rope", [128, 69, 16], F32)
    rope_s = P.sb("rope_s", [128, 16], F32)
    flag = P.sb("flag", [128, 1], F32)
    ones = P.sb("ones", [128, 512], F32)
    C32 = P.sb("C32", [128, 4, 2, 257], F32)
    cbnd = P.sb("cbnd", [128, 4, 4, 3], F32)
    MPh = P.sb("MPh", [4, 1], F32)
    cw = P.sb("cw", [128, 16, 4], F32)
    cb = P.sb("cb", [128, 16], F32)
    b_i = P.sb("b_i", [4, 1], F32)
    nb_f = P.sb("nb_f", [4, 1], F32)
    wgt8 = P.sb("wgt8", [128, 8, 8], BF16)
    nln16 = P.sb("nln16", [4, 1], F32)

    P.push()
    mtmp = P.sb("mtmp", [128, 2, 128], F32)
    P.dma("sp", identf[:], I["ident"][:, :], writes=["identf"])
    P.dma("sp", mtmp[:, 0, :], I["mprev"][:, :], writes=["mtmp"])
    P.dma("sp", mtmp[:, 1, :], I["mcur"][:, :], writes=["mtmp"])
    P.dma("pool", mbias[:], I["mbias"][:, :], writes=["mbias"])
    P.dma("sp", sel4[:], I["sel4"].rearrange("k (h m) -> k h m", h=4), writes=["sel4"])
    P.dma("sp", rope[:], I["rope"][:, :, :], writes=["rope"])
    P.dma("sp", rope_s[:], I["rope_s"][:, :], writes=["rope_s"])
    P.dma("sp", flag[:], I["flag"][:, :], writes=["flag"])
    P.dma("sp", cw[:], I["cw"][:, :, :], writes=["cw"])
    P.dma("sp", cb[:], I["cb"][:, :], writes=["cb"])
    P.dma("sp", b_i[:], I["b_i"][:, :], writes=["b_i"])
    P.dma("sp", nb_f[:], I["b_f"][:, :], writes=["nb_f"])
    P.dma("pool", wgt8[:], I["w_gates"][:, :, :], writes=["wgt8"])
    P.op("dve", lambda e: e.tensor_copy(ident[:], identf[:]), reads=["identf"], writes=["ident"])
    P.op("dve", lambda e: e.tensor_copy(mask2[:], mtmp[:]), reads=["mtmp"], writes=["mask2"])
    P.op("dve", lambda e: e.tensor_copy(mask2f[:, 1, :], mtmp[:, 1, :]), reads=["mtmp"], writes=["mask2f"])
    P.op("dve", lambda e: e.tensor_scalar(mask2f[:, 0, :], mtmp[:, 0, :], flag[:, 0:1], None, ALU.mult),
         reads=["mtmp", "flag", "mask2f"], writes=["mask2f"])
    P.op("dve", lambda e: e.memset(ones[:], 1.0), writes=["ones"])
    P.op("dve", lambda e: e.memset(nln16[:], -2.7725887222), writes=["nln16"])
    P.op("dve", lambda e: e.tensor_scalar(nb_f[:], nb_f[:], -1.0, None, ALU.mult), reads=["nb_f"], writes=["nb_f"])
    P.op("dve", lambda e: e.memset(C32[:], 0.0), writes=["C32"])
    P.op("dve", lambda e: e.memset(cbnd[:], 0.0), writes=["cbnd"])
    P.pop()


    def load_w(dst, src_view, res):
        P.dma("pool", dst, src_view, writes=[res])

    def rms_rstd(src_ap, junk, ss, rs, srcres, tag):
        P.op("act", lambda e: e.activation(junk, src_ap, AF.Square, accum_out=ss), reads=srcres, writes=["junk" + tag, "ss" + tag])
        P.op("dve", lambda e: e.tensor_scalar(rs, ss, 1.0 / D, EPS, ALU.mult, ALU.add), reads=["ss" + tag], writes=["rs" + tag])
        P.op("act", lambda e: e.activation(rs, rs, AF.Sqrt), reads=["rs" + tag], writes=["rs" + tag])
        P.op("dve", lambda e: e.reciprocal(rs, rs), reads=["rs" + tag], writes=["rs" + tag])

    def run_pipeline(items):
        ns = max(len(t) for t in items)
        for n_ in range(len(items) + ns - 1):
            for k in range(ns):
                idx = n_ - k
                if 0 <= idx < len(items) and k < len(items[idx]):
                    items[idx][k]()

    def make_stage(with_xt=True):
        xt = [P.sb("xt%d" % i, [128, D], F32) for i in range(2)] if with_xt else None
        junk = [P.sb("junk%d" % i, [128, D], BF16) for i in range(2)]
        ss = [P.sb("ss%d" % i, [128, 1], F32) for i in range(2)]
        rs = [P.sb("rs%d" % i, [128, 1], F32) for i in range(2)]
        xnb = [P.sb("xnb%d" % i, [128, D], BF16) for i in range(2)]
        return xt, junk, ss, rs, xnb

    def norm_T(st, i, src, srcres, np_, gbc, gres, dstT, dst_res):
        _, junk, ss, rs, xnb = st
        t = "n%d" % i
        rms_rstd(src, junk[i][0:np_, :], ss[i][0:np_, :], rs[i][0:np_, :], [srcres], t)
        P.op("dve", lambda e: e.scalar_tensor_tensor(xnb[i][0:np_, :], src, rs[i][0:np_, 0:1], gbc[0:np_, :], ALU.mult, ALU.mult),
             reads=[srcres, "rs" + t, gres], writes=["xnb%d" % i])
        for k in range(8):
            P.op("pe", lambda e, k=k: e.transpose(psT[:, k * 128:k * 128 + np_], xnb[i][0:np_, k * 128:(k + 1) * 128], ident[0:np_, 0:np_]),
                 reads=["xnb%d" % i, "ident"], writes=["psT"], signal=(k == 7))
        P.op("act", lambda e: e.copy(dstT, psT[:, :].rearrange("p (k t) -> p k t", k=8)[:, :, 0:np_]),
             reads=["psT"], writes=[dst_res])

    P.push()
    xnT = P.sb("xnT", [128, 8, NT], BF16)
    xnSc = P.sb("xnSc", [128, 8, 4], BF16)
    mixS = P.sb("mixS", [128, 8, 4], BF16)
    P.push()
    oaT = P.sb("oaT", [64, 4, NT], BF16)
    oaTs = P.sb("oaTs", [64, 4, 4], BF16)
    P.push()
    KTh = P.sb("KTh", [128, 2, 21, 128], BF16)
    VAh = P.sb("VAh", [128, 21, 4, 65], BF16)
    P.op("dve", lambda e: e.memset(VAh[:], 1.0), writes=["VAh"])

    def phase_norm(row0, with_samples):
        P.push()
        gmix = P.sb("gmix", [128, D], F32)
        P.dma("sp", gmix[:], I["g_mix"][0, :].partition_broadcast(128), writes=["gmix"])
        st = make_stage()
        def nitem(blk):
            i = blk % 2
            _, junk, ss, rs, xnb = st
            t = "n%d" % i

            def n1():
                P.dma("sp", st[0][i][:, :], I["x2"][row0 + blk * 128:row0 + (blk + 1) * 128, :], writes=["xt%d" % i])
                rms_rstd(st[0][i][:, :], junk[i][:, :], ss[i][:, :], rs[i][:, :], ["xt%d" % i], t)
                P.op("dve", lambda e: e.scalar_tensor_tensor(xnb[i][:, :], st[0][i][:, :], rs[i][:, 0:1], gmix[:, :], ALU.mult, ALU.mult),
                     reads=["xt%d" % i, "rs" + t, "gmix"], writes=["xnb%d" % i])

            def n2():
                for k in range(8):
                    P.op("pe", lambda e, k=k: e.transpose(psT[:, k * 128:(k + 1) * 128], xnb[i][:, k * 128:(k + 1) * 128], ident[:, :]),
                         reads=["xnb%d" % i, "ident"], writes=["psT"], signal=(k == 7))
                P.op("act", lambda e: e.copy(xnT[:, :, blk * 128:(blk + 1) * 128], psT[:, :].rearrange("p (k t) -> p k t", k=8)),
                     reads=["psT"], writes=["xnT"])
            return (n1, n2)
        run_pipeline([nitem(blk) for blk in range(16)])
        if with_samples:
            P.dma("sp", st[0][0][0:4, :], I["xs"][0:4, :], writes=["xt0"])
            norm_T(st, 0, st[0][0][0:4, :], "xt0", 4, gmix, "gmix", xnSc[:, :, :], "xnSc")
        P.pop()

    phase_norm(0, True)
    def attn_scope():
        A = {}
        A["wg"] = P.sb("wg", [128, 8, 768], BF16)
        A["qkv32"] = [P.sb("qkv32_%d" % i, [128, 768], F32) for i in range(3)]
        A["qkb"] = [P.sb("qkb%d" % i, [128, 512], BF16) for i in range(2)]
        A["rt"] = [P.sb("rt%d" % i, [128, 4, 8, 8], F32) for i in range(2)]
        return A

    def load_wg(A, g):
        for j, off in enumerate((OFF_AQ, OFF_AK, OFF_AV)):
            load_w(A["wg"][:, :, j * 256:(j + 1) * 256], I["w_in_b"][g * 3 + j, :, :, :], "wg")

    def attn_p1(A, it, lhs_fn, lhs_res, VA_dst, VA_res):
        i3 = it % 3
        i = it % 2
        wg = A["wg"]
        qkv32 = A["qkv32"][i3]; qres = "qkv32_%d" % i3
        pqk, pqkr = (ps0, "ps0") if i == 0 else (ps6, "ps6")
        pvv, pvr = (ps1, "ps1") if i == 0 else (ps7, "ps7")
        for k in range(8):
            P.op("pe", lambda e, k=k: e.matmul(pqk[:, :], lhs_fn(k), wg[:, k, 0:512], start=(k == 0), stop=(k == 7)),
                 reads=[lhs_res, "wg"], writes=[pqkr], signal=(k == 7))
        for k in range(8):
            P.op("pe", lambda e, k=k: e.matmul(pvv[:, 0:256], lhs_fn(k), wg[:, k, 512:768], start=(k == 0), stop=(k == 7)),
                 reads=[lhs_res, "wg"], writes=[pvr], signal=(k == 7))
        P.op("act", lambda e: e.copy(qkv32[:, 0:512], pqk[:, :]), reads=[pqkr], writes=[qres])
        P.op("act", lambda e: e.copy(qkv32[:, 512:768], pvv[:, 0:256]), reads=[pvr], writes=[qres])
        P.op("act", lambda e: e.copy(VA_dst[:, :, 0:64], pvv[:, 0:256].rearrange("p (h d) -> p h d", d=64)), reads=[pvr], writes=[VA_res])

    def attn_p2(A, it, ropetab, roperes, kv_out=None, kv_rows=128):
        i3 = it % 3
        i = it % 2
        qkv32 = A["qkv32"][i3]; qres = "qkv32_%d" % i3
        qkb = A["qkb"][i]; bres = "qkb%d" % i
        rt = A["rt"][i]; rres = "rt%d" % i
        cosb = ropetab[:, 0:8].unsqueeze(1).broadcast_to([128, 8, 8])
        sinb = ropetab[:, 8:16].unsqueeze(1).broadcast_to([128, 8, 8])
        q3 = qkv32[:, 0:512].rearrange("p (h d) -> p h d", d=64)
        x1 = q3[:, :, 0:8]
        x2 = q3[:, :, 8:16]
        P.op("dve", lambda e: e.tensor_tensor(rt[:, 0], x1, cosb, ALU.mult), reads=[qres, roperes], writes=[rres])
        P.op("dve", lambda e: e.tensor_tensor(rt[:, 1], x2, sinb, ALU.mult), reads=[qres, roperes], writes=[rres])
        P.op("dve", lambda e: e.tensor_tensor(rt[:, 2], x2, cosb, ALU.mult), reads=[qres, roperes], writes=[rres])
        P.op("dve", lambda e: e.tensor_tensor(rt[:, 3], x1, sinb, ALU.mult), reads=[qres, roperes], writes=[rres])
        P.op("dve", lambda e: e.tensor_tensor(x1, rt[:, 0], rt[:, 1], ALU.subtract), reads=[rres, qres], writes=[qres])
        P.op("dve", lambda e: e.tensor_tensor(x2, rt[:, 2], rt[:, 3], ALU.add), reads=[rres, qres], writes=[qres])
        P.op("dve", lambda e: e.tensor_copy(qkb[:, :], qkv32[:, 0:512]), reads=[qres], writes=[bres])
        if kv_out is not None:
            P.dma("sp", kv_out, qkv32[0:kv_rows, 256:768], reads=[qres], writes=["kvout"])

    def attn_p3(A, it, KT_dst, KT_res, QT_dst=None, QT_res=None):
        i = it % 2
        qkb = A["qkb"][i]; bres = "qkb%d" % i
        for j in range(4):
            P.op("pe", lambda e, j=j: e.transpose(psT[:, j * 128:(j + 1) * 128], qkb[:, j * 128:(j + 1) * 128], ident[:, :]),
                 reads=[bres, "ident"], writes=["psT"], signal=(j == 3))
        pT3 = psT[:, 0:512].rearrange("p (j t) -> p j t", j=4)
        if QT_dst is not None:
            P.op("act", lambda e: e.copy(QT_dst, pT3[:, 0:2, :]), reads=["psT"], writes=[QT_res])
        P.op("act", lambda e: e.copy(KT_dst, pT3[:, 2:4, :]), reads=["psT"], writes=[KT_res])

    def attn_c1(it, QT, QT_res, KTp, KTp_res, KTc, KTc_res, PT):
        i = it % 2
        pt = PT[i]; pres = "PT%d" % i
        pS = psA[:, :].rearrange("p (a c b q) -> p a c b q", a=2, c=2, b=2)
        for h in range(4):
            hp, po = h // 2, (h % 2) * 64
            bank = "psA%d" % (h % 2)
            P.op("pe", lambda e, h=h, hp=hp, po=po: e.matmul(pS[:, h % 2, h // 2, 0, :], KTp[po:po + 64, hp, :], QT[po:po + 64, hp, :], start=True, stop=True),
                 reads=[KTp_res, QT_res], writes=[bank], signal=False)
            P.op("pe", lambda e, h=h, hp=hp, po=po: e.matmul(pS[:, h % 2, h // 2, 1, :], KTc[po:po + 64, hp, :], QT[po:po + 64, hp, :], start=True, stop=True),
                 reads=[KTc_res, QT_res], writes=[bank], signal=(h >= 2))
        P.op("act", lambda e: e.activation(pt[:, 0, :, :, :], pS[:, 0, :, :, :], AF.Exp, scale=0.125), reads=["psA0"], writes=[pres])
        P.op("act", lambda e: e.activation(pt[:, 1, :, :, :], pS[:, 1, :, :, :], AF.Exp, scale=0.125), reads=["psA1"], writes=[pres])

    def attn_c2(it, VAp, VAp_res, VAc, VAc_res, msk, PT, acc_dst, acc_res, first, ncol):
        i = it % 2
        pt = PT[i]; pres = "PT%d" % i
        mb = msk[:, :, :].unsqueeze(1).broadcast_to([128, 4, 2, 128])
        pt4 = pt[:, :, :, :, :].rearrange("p a c b q -> p (a c) b q")
        P.op("dve", lambda e: e.tensor_tensor(pt4, pt4, mb, ALU.mult), reads=[pres, "mask2", "mask2f"], writes=[pres])
        pO = ps4[0:65, :].rearrange("p (h q) -> p h q", h=4)
        for h in range(4):
            P.op("pe", lambda e, h=h: e.matmul(pO[:, h, :], VAp[:, h, :], pt[:, h % 2, h // 2, 0, :], start=True, stop=False),
                 reads=[VAp_res, pres], writes=["ps4"], signal=False)
            P.op("pe", lambda e, h=h: e.matmul(pO[:, h, :], VAc[:, h, :], pt[:, h % 2, h // 2, 1, :], start=False, stop=True),
                 reads=[VAc_res, pres], writes=["ps4"], signal=(h == 3))
        if first:
            P.op("act", lambda e: e.copy(acc_dst, pO[:, :, 0:ncol]), reads=["ps4"], writes=[acc_res])
        else:
            P.op("dve", lambda e: e.tensor_tensor(acc_dst, acc_dst, pO[:, :, 0:ncol], ALU.add), reads=["ps4", acc_res], writes=[acc_res])

    P.push()
    A = attn_scope()

    def hist_item(g, r, it):
        d = DILS[g]
        hb = HB[g] + r
        start = NT - 128 * d + r

        def s1():
            if r == 0:
                load_wg(A, g)
            attn_p1(A, it, lambda k: xnT[:, k, ssl(start, 128, d)], "xnT", VAh[:, hb, :, :], "VAh")

        def s2():
            attn_p2(A, it, rope[:, RB[g] + r, :], "rope")

        def s3():
            attn_p3(A, it, KTh[:, :, hb, :], "KTh%d" % hb)
        return (s1, s2, s3)

    items = []
    it = 0
    for g in range(3):
        for r in range(DILS[g]):
            items.append(hist_item(g, r, it))
            it += 1
    run_pipeline(items)
    P.pop()
    phase_done(2)


    def mlstm_scope(nt, full):
        nch = nt // 128
        M = {"nt": nt, "nch": nch, "full": full}
        M["wm"] = P.sb("wm", [128, 8, 1024], BF16)
        for n in ("IG", "LF", "Bc", "bb", "Aa", "Mr", "WK", "tmpr"):
            M[n] = P.sb(n, [4, 512], F32)
        M["MP"] = P.sb("MP", [4, nch + 1], F32)
        M["MN"] = P.sb("MN", [4, nch], F32)
        M["DEC"] = P.sb("DEC", [4, nch], F32)
        M["COL"] = P.sb("COL", [128, nch, 3, 4], F32)
        M["DECB"] = P.sb("DECB", [128, 4, nch], F32)
        nloc = min(nt, 1024)
        M["nloc"] = nloc
        M["kT"] = P.sb("kT", [128, 2, nloc], BF16)
        nfc = 4 if full else 2
        M["xpre"] = P.sb("xpre", [128, nfc, 515], F32)
        M["bnd"] = P.sb("bnd", [128, nfc, 3], F32)
        M["cacc"] = [P.sb("cacc%d" % i, [128, 512], F32) for i in range(2)]
        M["ktmp"] = P.sb("ktmp", [128, 512], F32)
        M["Vaug"] = P.sb("Vaug", [128, nloc // 128, 257], BF16)
        M["kwa"] = P.sb("kwa", [128, 8, 256], BF16)
        if full:
            M["EM"] = P.sb("EM", [4, 512], F32)
            M["NEGM"] = P.sb("NEGM", [4, nt], F32)
            M["WI"] = P.sb("WI", [4, nt], F32)
            M["Cbf"] = [P.sb("Cbf%d" % i, [128, 2, 257], BF16) for i in range(2)]
            M["wTa"] = P.sb("wTa", [128, 8, 128], BF16)
            M["qpTa"] = P.sb("qpTa", [128, 8, 2, 128], BF16)
            if nt == 512:
                M["Csb"] = [P.sb("Csb%d" % i, [128, 2, 257], F32) for i in range(4)]
            M["qT"] = P.sb("qT", [128, 2, nloc], BF16)
            M["G"] = P.sb("G", [128, nloc // 128, 256], BF16)
            M["mhn"] = P.sb("mhncol", [128, 8], F32)
            M["Dt"] = [P.sb("Dt%d" % i, [128, 128], F32) for i in range(2)]
            M["sm"] = [P.sb("sm%d" % i, [128, 8], F32) for i in range(2)]
            M["jk"] = [P.sb("jk%d" % i, [128, 256], BF16) for i in range(2)]
            M["ob"] = [P.sb("ob%d" % i, [128, 256], BF16) for i in range(2)]
            P.dma("sp", M["mhn"][:], I["mhn_col"][:, :], writes=["mhncol"])
        P.op("dve", lambda e: e.memset(M["Vaug"][:], 1.0), writes=["Vaug"])
        return M

    def mlstm_gates(M, Xf, Xres, mode):
        nt, nch, full = M["nt"], M["nch"], M["full"]
        IG, LF, Bc, bb, Aa, Mr, WK, tmpr = M["IG"], M["LF"], M["Bc"], M["bb"], M["Aa"], M["Mr"], M["WK"], M["tmpr"]
        MP, MN, DEC = M["MP"], M["MN"], M["DEC"]
        samp = (mode == "samp")
        B3 = Bc[:, :].rearrange("p (c t) -> p c t", t=128)
        b3 = bb[:, :].rearrange("p (c t) -> p c t", t=128)
        A3 = Aa[:, :].rearrange("p (c t) -> p c t", t=128)
        M3 = Mr[:, :].rearrange("p (c t) -> p c t", t=128)
        t3 = tmpr[:, :].rearrange("p (c t) -> p c t", t=128)
        for tt in range(nt // 512):
            c0 = tt * 4
            cs = slice(tt * 512, (tt + 1) * 512)
            ncol = 4 if samp else 512
            for k in range(8):
                P.op("pe", lambda e, k=k: e.matmul(ps0[0:4, 0:ncol], wgt8[:, k, 0:4], Xf(k, cs), start=(k == 0), stop=(k == 7)),
                     reads=["wgt8", Xres], writes=["ps0"], signal=(k == 7))
            for k in range(8):
                P.op("pe", lambda e, k=k: e.matmul(ps1[0:4, 0:ncol], wgt8[:, k, 4:8], Xf(k, cs), start=(k == 0), stop=(k == 7)),
                     reads=["wgt8", Xres], writes=["ps1"], signal=(k == 7))
            if samp:
                P.op("dve", lambda e: e.memset(IG[:], NEGPAD), writes=["IG"])
                P.op("dve", lambda e: e.memset(LF[:], 0.0), writes=["LF"])
                P.op("act", lambda e: e.activation(IG[:, 0:512:128], ps0[0:4, 0:4], AF.Identity, bias=b_i[:, 0:1]), reads=["ps0", "b_i", "IG"], writes=["IG"])
                P.op("act", lambda e: e.activation(tmpr[:, 0:4], ps1[0:4, 0:4], AF.Exp, bias=nb_f[:, 0:1], scale=-1.0), reads=["ps1", "nb_f"], writes=["tmpr"])
                P.op("act", lambda e: e.activation(tmpr[:, 0:4], tmpr[:, 0:4], AF.Ln, bias=1.0), reads=["tmpr"], writes=["tmpr"])
                P.op("dve", lambda e: e.tensor_scalar(LF[:, 0:512:128], tmpr[:, 0:4], -1.0, None, ALU.mult), reads=["tmpr", "LF"], writes=["LF"])
            else:
                P.op("act", lambda e: e.activation(IG[:, :], ps0[0:4, :], AF.Identity, bias=b_i[:, 0:1]), reads=["ps0", "b_i"], writes=["IG"])
                P.op("act", lambda e: e.activation(tmpr[:, :], ps1[0:4, :], AF.Exp, bias=nb_f[:, 0:1], scale=-1.0), reads=["ps1", "nb_f"], writes=["tmpr"])
                P.op("act", lambda e: e.activation(tmpr[:, :], tmpr[:, :], AF.Ln, bias=1.0), reads=["tmpr"], writes=["tmpr"])
                P.op("dve", lambda e: e.tensor_scalar(LF[:, :], tmpr[:, :], -1.0, None, ALU.mult), reads=["tmpr"], writes=["LF"])
            P.op("dve", lambda e: e.tensor_tensor_scan(Bc[:, :], ones[0:4, 0:512], LF[:, :], 0.0, ALU.mult, ALU.add),
                 reads=["LF", "ones"], writes=["Bc"])
            P.op("dve", lambda e: e.tensor_copy(b3[:, 0, :], B3[:, 0, :]), reads=["Bc"], writes=["bb"])
            P.op("dve", lambda e: e.tensor_tensor(b3[:, 1:4, :], B3[:, 1:4, :], B3[:, 0:3, 127:128].broadcast_to([4, 3, 128]), ALU.subtract),
                 reads=["Bc", "bb"], writes=["bb"])
            P.op("dve", lambda e: e.tensor_tensor(Aa[:, :], IG[:, :], bb[:, :], ALU.subtract), reads=["IG", "bb"], writes=["Aa"])
            for cl in range(4):
                c = c0 + cl
                P.op("dve", lambda e, c=c, cl=cl: e.tensor_tensor_scan(M3[:, cl, :], A3[:, cl, :], A3[:, cl, :], MP[:, c:c + 1], ALU.max, ALU.max),
                     reads=["Aa", "MP", "Mr"], writes=["Mr"])
                dst = MN[:, c:c + 1] if samp else MP[:, c + 1:c + 2]
                P.op("dve", lambda e, cl=cl, dst=dst: e.tensor_tensor(dst, b3[:, cl, 127:128], M3[:, cl, 127:128], ALU.add),
                     reads=["bb", "Mr", "MP", "MN"], writes=["MN" if samp else "MP"])
            if not samp:
                P.op("dve", lambda e, c0=c0: e.tensor_copy(MN[:, c0:c0 + 4], MP[:, c0 + 1:c0 + 5]), reads=["MP", "MN"], writes=["MN"])
            mpb = MP[:, c0:c0 + 4].unsqueeze(2).broadcast_to([4, 4, 128])
            mlb = M3[:, :, 127:128].broadcast_to([4, 4, 128])
            if full:
                P.op("dve", lambda e, cs=cs: e.tensor_scalar(M["NEGM"][:, cs], Mr[:, :], -1.0, None, ALU.mult), reads=["Mr", "NEGM"], writes=["NEGM"])
                P.op("dve", lambda e: e.tensor_tensor(t3, mpb, M3, ALU.subtract), reads=["MP", "Mr", "tmpr"], writes=["tmpr"])
                P.op("act", lambda e, cs=cs: e.activation(M["WI"][:, cs], tmpr[:, :], AF.Exp), reads=["tmpr", "WI"], writes=["WI"])
                P.op("dve", lambda e: e.tensor_tensor(tmpr[:, :], bb[:, :], Mr[:, :], ALU.add), reads=["bb", "Mr", "tmpr"], writes=["tmpr"])
                P.op("act", lambda e: e.activation(M["EM"][:, :], tmpr[:, :], AF.Exp, scale=-2.0), reads=["tmpr"], writes=["EM"])
            P.op("dve", lambda e: e.tensor_tensor(t3, A3, mlb, ALU.subtract), reads=["Aa", "Mr", "tmpr"], writes=["tmpr"])
            P.op("act", lambda e: e.activation(WK[:, :], tmpr[:, :], AF.Exp, bias=nln16[:, 0:1]), reads=["tmpr", "nln16"], writes=["WK"])
            P.op("dve", lambda e, c0=c0: e.tensor_tensor(DEC[:, c0:c0 + 4], MP[:, c0:c0 + 4], M3[:, :, 127], ALU.subtract), reads=["MP", "Mr", "DEC"], writes=["DEC"])
            P.op("act", lambda e, c0=c0: e.activation(DEC[:, c0:c0 + 4], DEC[:, c0:c0 + 4], AF.Exp), reads=["DEC"], writes=["DEC"])
            pc = ps0[:, 0:48].rearrange("p (c j h) -> p c j h", j=3, h=4)
            srcs = (Aa, WK, M["EM"]) if full else (Aa, WK)
            nj = len(srcs)
            n = 0
            for cl in range(4):
                for j in range(nj):
                    n += 1
                    P.op("pe", lambda e, cl=cl, j=j: e.transpose(pc[:, cl, j, :], srcs[j][0:4, cl * 128:(cl + 1) * 128], identf[0:4, 0:4]),
                         reads=["Aa", "WK", "EM", "identf"], writes=["ps0"], signal=(n == 4 * nj))
            P.op("act", lambda e, c0=c0: e.copy(M["COL"][:, c0:c0 + 4, 0:nj, :], pc[:, :, 0:nj, :]), reads=["ps0"], writes=["COL"])
            P.op("dve", lambda e, c0=c0: e.tensor_scalar(M["COL"][:, c0:c0 + 4, 0, :], M["COL"][:, c0:c0 + 4, 0, :], -2.7725887222, None, ALU.add), reads=["COL"], writes=["COL"])
            pd = ps1[:, 0:16].rearrange("p (h c) -> p h c", h=4)
            for h in range(4):
                P.op("pe", lambda e, h=h, c0=c0: e.matmul(pd[:, h, :], sel4[0:4, h, :], DEC[0:4, c0:c0 + 4], start=True, stop=True),
                     reads=["sel4", "DEC"], writes=["ps1"], signal=(h == 3))
            P.op("act", lambda e, c0=c0: e.copy(M["DECB"][:, :, c0:c0 + 4], pd), reads=["ps1"], writes=["DECB"])

    def mlstm_head(M, h, Xcols, Xres, mode, obT_dst=None, obT_res=None):
        nt, nch, full, nloc = M["nt"], M["nch"], M["full"], M["nloc"]
        samp = mode == "samp"
        wm = M["wm"]
        kT = M["kT"]; xpre = M["xpre"]; bnd = M["bnd"]; Vaug = M["Vaug"]; COL = M["COL"]; DECB = M["DECB"]
        nfc = 4 if full else 2
        W = 128 if samp else 512
        C3 = C32[:, h, :, :]
        C3 = C32[:, h, :, :]
        if mode == "hist":
            for fc in (0, 1):
                for k in range(8):
                    P.op("pe", lambda e, k=k, fc=fc: e.matmul(ps0[:, 0:128], wm[:, k, fc * 128:(fc + 1) * 128], Xcols(k, nt - 128, 128), start=(k == 0), stop=(k == 7)),
                         reads=["wm_q", Xres], writes=["ps0"], signal=(k == 7))
                P.op("act", lambda e, fc=fc: e.copy(cbnd[:, h, fc, :], ps0[:, 125:128]), reads=["ps0"], writes=["cbnd"])
        else:
            cst = M["ktmp"]
            npr = 4 if samp else 3
            lsel = (lambda k: xnSc[:, k, :]) if samp else (lambda k: Xcols(k, nt - 3, 3))
            for k in range(8):
                P.op("pe", lambda e, k=k: e.matmul(ps0[0:npr, :], lsel(k), wm[:, k, 0:512], start=(k == 0), stop=(k == 7)),
                     reads=["wm_q", "wm_k", Xres], writes=["ps0"], signal=(k == 7))
            P.op("act", lambda e: e.copy(cst[0:npr, :], ps0[0:npr, :]), reads=["ps0"], writes=["ktmp"])
            if samp:
                P.dma("sp", O["convs"][:, 2, h * 256:(h + 1) * 256], cst[0:4, 0:256], reads=["ktmp"], writes=["convsout"])
                P.dma("sp", O["convs"][:, 2, 1024 + h * 256:1024 + (h + 1) * 256], cst[0:4, 256:512], reads=["ktmp"], writes=["convsout"])
            else:
                P.dma("sp", O["convp"][:, h * 256:(h + 1) * 256], cst[0:3, 0:256], reads=["ktmp"], writes=["convpout"])
                P.dma("sp", O["convp"][:, 1024 + h * 256:1024 + (h + 1) * 256], cst[0:3, 256:512], reads=["ktmp"], writes=["convpout"])
        for hf in range(nt // nloc):
            def b_chunks(cls):
                for cl in cls:
                    c = hf * (nloc // 128) + cl
                    pz, pzr = (ps0, "ps0") if cl % 2 == 0 else (ps1, "ps1")
                    ncols = 512 if full else 256
                    for k in range(8):
                        P.op("pe", lambda e, k=k, c=c, pz=pz: e.matmul(pz[:, 0:ncols], Xcols(k, c * 128, 128), wm[:, k, 512:512 + ncols], start=(k == 0), stop=(k == 7)),
                             reads=["wm_v", "wm_o", Xres], writes=[pzr], signal=(k == 7))
                    P.op("act", lambda e, cl=cl, pz=pz: e.copy(Vaug[:, cl, 0:256], pz[:, 0:256]), reads=[pzr], writes=["Vaug"])
                    if full:
                        P.op("act", lambda e, pz=pz, cl=cl: e.activation(M["G"][:, cl, :], pz[:, 256:512], AF.Sigmoid), reads=[pzr], writes=["G"])

            for tl in range(nloc // W):
                tt = hf * (nloc // W) + tl
                t0 = tt * W
                l0 = tl * W
                for fi in range(nfc):
                    fc = fi + (4 - nfc)
                    chunk = (fc // 2) * 8 + h * 2 + fc % 2
                    pz, pzr = (ps0, "ps0") if fi % 2 == 0 else (ps1, "ps1")
                    xp = xpre[:, fi, :]
                    xr = "xpre%d" % fi
                    wres = "wm_q" if fc < 2 else "wm_k"
                    if samp:
                        P.op("dve", lambda e, xp=xp, chunk=chunk, tt=tt: e.tensor_copy(xp[:, 0:3], M["scT"][:, tt, chunk, :]), reads=["scT"], writes=[xr])
                    elif tt > 0:
                        P.op("dve", lambda e, fi=fi, xp=xp: e.tensor_copy(bnd[:, fi, :], xp[:, 512:515]), reads=[xr], writes=["bnd%d" % fi])
                        P.op("dve", lambda e, fi=fi, xp=xp: e.tensor_copy(xp[:, 0:3], bnd[:, fi, :]), reads=["bnd%d" % fi], writes=[xr])
                    elif mode == "hist":
                        P.op("dve", lambda e, xp=xp: e.memset(xp[:, 0:3], 0.0), writes=[xr])
                    else:
                        P.op("dve", lambda e, xp=xp, fc=fc: e.tensor_scalar(xp[:, 0:3], cbnd[:, h, fc, :], flag[:, 0:1], None, ALU.mult),
                             reads=["cbnd", "flag"], writes=[xr])
                    for k in range(8):
                        P.op("pe", lambda e, k=k, pz=pz, fc=fc, t0=t0: e.matmul(pz[:, 0:W], wm[:, k, fc * 128:(fc + 1) * 128], Xcols(k, t0, W), start=(k == 0), stop=(k == 7)),
                             reads=[wres, Xres], writes=[pzr], signal=(k == 7))
                    P.op("act", lambda e, pz=pz, xp=xp: e.copy(xp[:, 3:3 + W], pz[:, 0:W]), reads=[pzr], writes=[xr])
                b_chunks(range(tl * (W // 128), (tl + 1) * (W // 128)))
                for fi in range(nfc):
                    fc = fi + (4 - nfc)
                    chunk = (fc // 2) * 8 + h * 2 + fc % 2
                    xp = xpre[:, fi, :]
                    xr = "xpre%d" % fi
                    ca = M["cacc"][fi % 2]; car = "cacc%d" % (fi % 2)
                    P.op("dve", lambda e, xp=xp, chunk=chunk, ca=ca: e.tensor_scalar(ca[:, 0:W], xp[:, 3:3 + W], cw[:, chunk, 3:4], cb[:, chunk:chunk + 1], ALU.mult, ALU.add),
                         reads=[xr, "cw", "cb"], writes=[car])
                    for j in (2, 1, 0):
                        P.op("dve", lambda e, xp=xp, chunk=chunk, j=j, ca=ca: e.scalar_tensor_tensor(ca[:, 0:W], xp[:, j:j + W], cw[:, chunk, j:j + 1], ca[:, 0:W], ALU.mult, ALU.add),
                             reads=[xr, "cw", car], writes=[car])
                    if fc < 2:
                        P.op("act", lambda e, fc=fc, l0=l0, ca=ca: e.activation(M["qT"][:, fc, l0:l0 + W], ca[:, 0:W], AF.Silu), reads=[car], writes=["qT"])
                    else:
                        P.op("act", lambda e, ca=ca, fc=fc, l0=l0: e.activation(kT[:, fc - 2, l0:l0 + W], ca[:, 0:W], AF.Silu), reads=[car], writes=["kT"])
                    if mode == "hist" and t0 + W == nt:
                        P.op("dve", lambda e, xp=xp, fc=fc: e.tensor_copy(cbnd[:, h, fc, :], xp[:, 512:515]), reads=[xr], writes=["cbnd"])
            nlc = nloc // 128
            for cl in range(nlc):
                c = hf * nlc + cl
                ls = slice(cl * 128, (cl + 1) * 128)
                gs = slice(c * 128, (c + 1) * 128)
                par = cl % 2
                if full:
                    qT = M["qT"]
                    pS, pSr = (psA[:, 0:128], "psA0") if par == 0 else (ps6[:, 0:128], "ps6")
                    pB, pBr = (psA[:, 512:640], "psA1") if par == 0 else (ps7[:, 0:128], "ps7")
                    pW, pWr = (ps0[:, 0:128], "ps0") if par == 0 else (ps1[:, 0:128], "ps1")
                    Dt = M["Dt"][par]
                    for k2 in range(2):
                        P.op("pe", lambda e, k2=k2: e.matmul(pS, kT[:, k2, ls], qT[:, k2, ls], start=(k2 == 0), stop=(k2 == 1)),
                             reads=["kT", "qT"], writes=[pSr], signal=(k2 == 1))
                    P.op("pe", lambda e: e.matmul(pB, sel4[0:4, h, :], M["NEGM"][0:4, gs], start=True, stop=False),
                         reads=["sel4", "NEGM"], writes=[pBr], signal=False)
                    P.op("pe", lambda e: e.matmul(pB, ident[:, :], mbias[:, :], start=False, stop=True),
                         reads=["ident", "mbias"], writes=[pBr])
                    P.op("pe", lambda e: e.matmul(pW, sel4[0:4, h, :], M["WI"][0:4, gs], start=True, stop=True),
                         reads=["sel4", "WI"], writes=[pWr])
                    P.op("act", lambda e, c=c: e.activation(Dt[:, :], pB, AF.Exp, bias=COL[:, c, 0, h:h + 1]), reads=[pBr, "COL"], writes=["Dt%d" % par])
                    P.op("dve", lambda e, cl=cl: e.tensor_tensor(M["wTa"][:, cl, :], Dt[:, :], pS, ALU.mult), reads=["Dt%d" % par, pSr], writes=["wT%d" % cl])
                    for k2 in range(2):
                        P.op("dve", lambda e, k2=k2, cl=cl: e.tensor_tensor(M["qpTa"][:, cl, k2, :], qT[:, k2, ls], pW, ALU.mult), reads=["qT", pWr], writes=["qpT%d" % cl])
                for k2 in range(2):
                    P.op("pe", lambda e, k2=k2: e.transpose(psT[:, par * 256 + k2 * 128:par * 256 + (k2 + 1) * 128], kT[:, k2, ls], ident[:, :]),
                         reads=["kT", "ident"], writes=["psT"], signal=(k2 == 1))
                P.op("act", lambda e, c=c, cl=cl: e.activation(M["kwa"][:, cl, :], psT[:, par * 256:(par + 1) * 256], AF.Copy, scale=COL[:, c, 1, h:h + 1]),
                     reads=["psT", "COL"], writes=["kw%d" % cl])
            def stageA(cl):
                c = hf * nlc + cl
                i = cl % 2
                if samp:
                    C3 = M["Csb"][cl % 4][:, :, :]
                    C3r = "Csb%d" % (cl % 4)
                    P.dma("sp", C3[:, :, 0:256], I["sC"][c, h, :, :].rearrange("(k p) v -> p k v", p=128), writes=[C3r])
                    P.dma("sp", C3[:, :, 256:257], I["sn"][c, h, :].rearrange("(k p o) -> p k o", p=128, o=1), writes=[C3r], allow_slow_non_contiguous=True)
                else:
                    C3 = C32[:, h, :, :]
                    C3r = "C32"
                pCs = ((ps6, "ps6"), (ps7, "ps7")) if i == 0 else ((ps0, "ps0"), (ps1, "ps1"))
                for k2, (pC, pCr) in enumerate(pCs):
                    P.op("pe", lambda e, k2=k2, pC=pC: e.matmul(pC[:, 0:257], M["kwa"][:, cl, k2 * 128:(k2 + 1) * 128], Vaug[:, cl, :], start=True, stop=True),
                         reads=["kw%d" % cl, "Vaug"], writes=[pCr])
                if full:
                    Cbf = M["Cbf"][i]; Cbr = "Cbf%d" % i
                    pN, pNr = (ps4[:, 0:257], "ps4") if i == 0 else (psA[:, 0:257], "psA0")
                    P.op("act", lambda e: e.copy(Cbf[:, :, :], C3), reads=[C3r], writes=[Cbr])
                    P.op("pe", lambda e: e.matmul(pN, M["wTa"][:, cl, :], Vaug[:, cl, :], start=True, stop=False), reads=["wT%d" % cl, "Vaug"], writes=[pNr], signal=False)
                    P.op("pe", lambda e: e.matmul(pN, M["qpTa"][:, cl, 0, :], Cbf[:, 0, :], start=False, stop=False), reads=["qpT%d" % cl, Cbr], writes=[pNr], signal=False)
                    P.op("pe", lambda e: e.matmul(pN, M["qpTa"][:, cl, 1, :], Cbf[:, 1, :], start=False, stop=True), reads=["qpT%d" % cl, Cbr], writes=[pNr])
                for k2, (pC, pCr) in enumerate(pCs):
                    P.op("dve", lambda e, k2=k2, pC=pC: e.scalar_tensor_tensor(C3[:, k2, :], C3[:, k2, :], DECB[:, h, c:c + 1], pC[:, 0:257], ALU.mult, ALU.add),
                         reads=[C3r, "DECB", pCr], writes=[C3r])
                if samp:
                    P.dma("sp", O["Cs"][c, h, :, :].rearrange("(k p) v -> p k v", p=128), C3[:, :, 0:256], reads=[C3r], writes=["Csout"])
                    P.dma("sp", O["ns"][c, h, :].rearrange("(k p o) -> p k o", p=128, o=1), C3[:, :, 256:257], reads=[C3r], writes=["nsout"], allow_slow_non_contiguous=True)

            def stageB(cl):
                c = hf * nlc + cl
                i = cl % 2
                sm = M["sm"][i]; jk = M["jk"][i]; ob = M["ob"][i]
                smr = "sm%d" % i
                pN, pNr = (ps4[:, 0:257], "ps4") if i == 0 else (psA[:, 0:257], "psA0")
                P.op("act", lambda e: e.activation(jk[:, :], pN[:, 0:256], AF.Square, scale=1.0 / 16.0, accum_out=sm[:, 0:1]),
                     reads=[pNr], writes=["jk%d" % i, smr])
                P.op("dve", lambda e: e.scalar_tensor_tensor(sm[:, 1:2], pN[:, 256:257], pN[:, 256:257], COL[:, c, 2, h:h + 1], ALU.mult, ALU.max),
                     reads=[pNr, "COL", smr], writes=[smr])
                P.op("dve", lambda e: e.scalar_tensor_tensor(sm[:, 2:3], sm[:, 1:2], EPS, sm[:, 0:1], ALU.mult, ALU.add), reads=[smr], writes=[smr])
                P.op("act", lambda e: e.activation(sm[:, 2:3], sm[:, 2:3], AF.Sqrt), reads=[smr], writes=[smr])
                P.op("dve", lambda e: e.reciprocal(sm[:, 3:4], sm[:, 2:3]), reads=[smr], writes=[smr])
                P.op("dve", lambda e: e.scalar_tensor_tensor(ob[:, :], pN[:, 0:256], sm[:, 3:4], M["G"][:, cl, :], ALU.mult, ALU.mult),
                     reads=[pNr, smr, "G"], writes=["ob%d" % i])

            def stageC(cl):
                c = hf * nlc + cl
                i = cl % 2
                ob = M["ob"][i]
                gs = slice(c * 128, (c + 1) * 128)
                for j in range(2):
                    P.op("pe", lambda e, j=j: e.transpose(psT[:, 512 + i * 256 + j * 128:512 + i * 256 + (j + 1) * 128], ob[:, j * 128:(j + 1) * 128], ident[:, :]),
                         reads=["ob%d" % i, "ident"], writes=["psT"], signal=(j == 1))
                pob = psT[:, 512 + i * 256:512 + (i + 1) * 256].rearrange("p (j t) -> p j t", j=2)
                for j in range(2):
                    mcol = M["mhn"][:, h * 2 + j:h * 2 + j + 1]
                    if samp:
                        P.op("act", lambda e, j=j, mcol=mcol: e.activation(obT_dst[:, h * 2 + j, c:c + 1], pob[:, j, 0:1], AF.Copy, scale=mcol), reads=["psT", "mhncol"], writes=[obT_res])
                    else:
                        P.op("act", lambda e, j=j, mcol=mcol: e.activation(obT_dst[:, h * 2 + j, gs], pob[:, j, :], AF.Copy, scale=mcol), reads=["psT", "mhncol"], writes=[obT_res])

            for step in range(nlc + 2):
                if step < nlc:
                    stageA(step)
                if full and 0 <= step - 1 < nlc:
                    stageB(step - 1)
                if full and 0 <= step - 2 < nlc:
                    stageC(step - 2)
        if mode == "hist":
            P.op("dve", lambda e: e.tensor_scalar(C3, C3, flag[:, 0:1], None, ALU.mult), reads=["C32", "flag"], writes=["C32"])
        elif mode == "own":
            P.dma("sp", O["Cp"][h, :, :].rearrange("(k p) v -> p k v", p=128), C3[:, :, 0:256], reads=["C32"], writes=["Cpout"])
            P.dma("sp", O["np"][h, :].rearrange("(k p o) -> p k o", p=128, o=1), C3[:, :, 256:257], reads=["C32"], writes=["npout"], allow_slow_non_contiguous=True)

    def load_wm(M, h):
        for j, off in enumerate((OFF_MQ, OFF_MK, OFF_MV, OFF_MO)):
            load_w(M["wm"][:, :, j * 256:(j + 1) * 256], I["w_in_b"][9 + h * 4 + j, :, :, :], ("wm_q", "wm_k", "wm_v", "wm_o")[j])

    xn_cols = lambda k, s, n: xnT[:, k, s:s + n]

    P.push()
    M = mlstm_scope(NT, False)
    P.op("dve", lambda e: e.memset(M["MP"][:, 0:1], 0.0), writes=["MP"])
    mlstm_gates(M, lambda k, cs: xnT[:, k, cs], "xnT", "hist")
    P.op("dve", lambda e: e.tensor_scalar(MPh[:, :], M["MP"][:, 16:17], flag[0:4, 0:1], None, ALU.mult), reads=["MP", "flag"], writes=["MPh"])
    for h in range(4):
        for g_ in range(3):
            W_ = WINS[g_]
            P.dma("sp", O["kvs%d" % g_][h, 0:W_ - 1, :], I["ck%d" % g_][h, 1:W_, :], writes=["kvscopy"])
        P.dma("sp", O["convs"][h, 0:2, :], I["sconv"][h, 1:3, :], writes=["convscopy"])
        load_wm(M, h)
        mlstm_head(M, h, xn_cols, "xnT", "hist")
    P.pop()
    phase_done(3)

    phase_norm(NT, False)
    phase_done(4)

    P.push()
    A = attn_scope()
    KTo = P.sb("KTo", [128, 2, 16, 128], BF16)
    QTo = P.sb("QTo", [128, 2, 16, 128], BF16)
    VAo = P.sb("VAo", [128, 16, 4, 65], BF16)
    acc = P.sb("acc", [65, 4, NT], F32)
    accs = P.sb("accs", [65, 4, 4], F32)
    PT = [P.sb("PT%d" % i, [128, 2, 2, 2, 128], BF16) for i in range(2)]
    ckst = [P.sb("ckst%d" % i, [128, 512], F32) for i in range(2)]
    ckb = [P.sb("ckb%d" % i, [128, 256], BF16) for i in range(2)]
    KTs = [P.sb("KTs%d" % i, [128, 2, 128], BF16) for i in range(2)]
    VAs = [P.sb("VAs%d" % i, [128, 4, 65], BF16) for i in range(2)]
    KTn = [P.sb("KTn%d" % i, [128, 2, 128], BF16) for i in range(2)]
    QTn = [P.sb("QTn%d" % i, [128, 2, 128], BF16) for i in range(2)]
    VAn = [P.sb("VAn%d" % i, [128, 4, 65], BF16) for i in range(2)]
    xnSb = [P.sb("xnSb%d" % i, [128, 8, 128], BF16) for i in range(2)]
    P.op("dve", lambda e: e.memset(VAo[:], 1.0), writes=["VAo%d" % k for k in range(16)])
    for i in range(2):
        P.op("dve", lambda e, i=i: e.memset(VAs[i][:], 1.0), writes=["VAs%d" % i])
        P.op("dve", lambda e, i=i: e.memset(VAn[i][:], 1.0), writes=["VAn%d" % i])
        P.op("dve", lambda e, i=i: e.memset(xnSb[i][:], 0.0), writes=["xnSb%d" % i])
    def own_item(g, s, r, it):
        d = DILS[g]
        span = 128 * d
        nsp = NT // span
        ob_ = s * d + r
        start = s * span + r
        kv_out = O["kvp%d" % g][ssl(r, 128, d), :] if s == nsp - 1 else None
        if s == 0:
            hb = HB[g] + r
            KTp, KTpr, VAp, VApr, msk = KTh[:, :, hb, :], "KTh%d" % hb, VAh[:, hb, :, :], "VAh", mask2f
        else:
            pb_ = ob_ - d
            KTp, KTpr, VAp, VApr, msk = KTo[:, :, pb_, :], "KTo%d" % pb_, VAo[:, pb_, :, :], "VAo%d" % pb_, mask2

        def s1():
            if s == 0 and r == 0:
                load_wg(A, g)
            attn_p1(A, it, lambda k: xnT[:, k, ssl(start, 128, d)], "xnT", VAo[:, ob_, :, :], "VAo%d" % ob_)

        def s2():
            attn_p2(A, it, rope[:, RB[g] + d + ob_, :], "rope", kv_out=kv_out)

        def s3():
            attn_p3(A, it, KTo[:, :, ob_, :], "KTo%d" % ob_, QTo[:, :, ob_, :], "QTo%d" % ob_)

        def s4():
            attn_c1(it, QTo[:, :, ob_, :], "QTo%d" % ob_, KTp, KTpr, KTo[:, :, ob_, :], "KTo%d" % ob_, PT)

        def s5():
            attn_c2(it, VAp, VApr, VAo[:, ob_, :, :], "VAo%d" % ob_, msk, PT, acc[:, :, ssl(start, 128, d)], "acc", g == 0, 128)
        return (s1, s2, s3, s4, s5)

    def samp_seq(g, b, it, its):
        d = DILS[g]
        i = its % 2
        ck = I["ck%d" % g]
        kvs = O["kvs%d" % g]
        P.dma("sp", ckst[i][:, :], ck[b, ssl(0, 128, d), :], writes=["ckst%d" % i])
        P.op("act", lambda e: e.copy(VAs[i][:, :, 0:64], ckst[i][:, 256:512].rearrange("p (h d) -> p h d", d=64)), reads=["ckst%d" % i], writes=["VAs%d" % i])
        P.op("dve", lambda e: e.tensor_copy(ckb[i][:, :], ckst[i][:, 0:256]), reads=["ckst%d" % i], writes=["ckb%d" % i])
        for j in range(2):
            P.op("pe", lambda e, j=j: e.transpose(psT[:, 512 + j * 128:512 + (j + 1) * 128], ckb[i][:, j * 128:(j + 1) * 128], ident[:, :]),
                 reads=["ckb%d" % i, "ident"], writes=["psT"], signal=(j == 1))
        P.op("act", lambda e: e.copy(KTs[i][:, :, :], psT[:, 512:768].rearrange("p (j t) -> p j t", j=2)), reads=["psT"], writes=["KTs%d" % i])
        P.op("dve", lambda e: e.tensor_copy(xnSb[i][:, :, 0:1], xnSc[:, :, b:b + 1]), reads=["xnSc", "xnSb%d" % i], writes=["xnSb%d" % i])
        attn_p1(A, it, lambda k: xnSb[i][:, k, :], "xnSb%d" % i, VAn[i][:, :, :], "VAn%d" % i)
        attn_p2(A, it, rope_s[:, :], "rope_s", kv_out=kvs[b, WINS[g] - 1:WINS[g], :], kv_rows=1)
        attn_p3(A, it, KTn[i][:, :, :], "KTn%d" % i, QTn[i][:, :, :], "QTn%d" % i)
        attn_c1(it, QTn[i][:, :, :], "QTn%d" % i, KTs[i][:, :, :], "KTs%d" % i, KTn[i][:, :, :], "KTn%d" % i, PT)
        attn_c2(it, VAs[i][:, :, :], "VAs%d" % i, VAn[i][:, :, :], "VAn%d" % i, mask2, PT, accs[:, :, b:b + 1], "accs", g == 0, 1)

    it = 0
    its = 0
    for g in range(3):
        d = DILS[g]
        items = []
        for s in range(NT // (128 * d)):
            for r in range(d):
                items.append(own_item(g, s, r, it))
                it += 1
        run_pipeline(items)
        for b in range(4):
            samp_seq(g, b, it, its)
            it += 1
            its += 1
    phase_done(4.5)
    rd = [P.sb("rd%d" % i, [64, 512], F32) for i in range(2)]
    n = 0
    for h in range(4):
        for tt in range(4):
            cs = slice(tt * 512, (tt + 1) * 512)
            i = n % 2
            pz, pzr = (ps0, "ps0") if i == 0 else (ps1, "ps1")
            P.op("pe", lambda e, h=h, cs=cs, pz=pz: e.matmul(pz[0:64, :], ones[64:65, 0:64], acc[64:65, h, cs], start=True, stop=True),
                 reads=["ones", "acc"], writes=[pzr])
            P.op("dve", lambda e, pz=pz, i=i: e.reciprocal(rd[i][:, :], pz[0:64, :]), reads=[pzr], writes=["rd%d" % i])
            P.op("dve", lambda e, h=h, cs=cs, i=i: e.tensor_tensor(oaT[:, h, cs], acc[0:64, h, cs], rd[i][:, :], ALU.mult), reads=["acc", "rd%d" % i], writes=["oaT"])
            n += 1
    for h in range(4):
        P.op("pe", lambda e, h=h: e.matmul(ps0[0:64, 0:4], ones[64:65, 0:64], accs[64:65, h, :], start=True, stop=True), reads=["ones", "accs"], writes=["ps0"])
        P.op("dve", lambda e: e.reciprocal(rd[0][:, 0:4], ps0[0:64, 0:4]), reads=["ps0"], writes=["rd0"])
        P.op("dve", lambda e, h=h: e.tensor_tensor(oaTs[:, h, :], accs[0:64, h, :], rd[0][:, 0:4], ALU.mult), reads=["accs", "rd0"], writes=["oaTs"])
    P.pop()
    P.pop()
    phase_done(5)

    P.push()
    obT = P.sb("obT", [128, 8, NT], BF16)
    obTs = P.sb("obTs", [128, 8, 4], BF16)
    P.push()
    M = mlstm_scope(NT, True)
    P.op("dve", lambda e: e.tensor_copy(M["MP"][:, 0:1], MPh[:, :]), reads=["MPh"], writes=["MP"])
    mlstm_gates(M, lambda k, cs: xnT[:, k, cs], "xnT", "own")
    P.dma("sp", O["mp"][:, :], M["MP"][:, 16:17], reads=["MP"], writes=["mpout"])
    for h in range(4):
        load_wm(M, h)
        mlstm_head(M, h, xn_cols, "xnT", "own", obT, "obT")
    P.pop()
    phase_done(6)
    P.push()
    M = mlstm_scope(512, True)
    xnS = P.sb("xnS", [128, 8, 512], BF16)
    P.op("dve", lambda e: e.memset(xnS[:], 0.0), writes=["xnS"])
    P.op("dve", lambda e: e.tensor_copy(xnS[:, :, 0:512:128], xnSc[:, :, :]), reads=["xnSc", "xnS"], writes=["xnS"])
    P.dma("sp", M["MP"][:, 0:4], I["smT"][:, :], writes=["MP"])
    M["scT"] = P.sb("scT", [128, 4, 16, 3], F32)
    P.dma("sp", M["scT"][:], I["sconvT"][:, :, :, :], writes=["scT"])
    mlstm_gates(M, lambda k, cs: xnSc[:, k, :], "xnSc", "samp")
    P.dma("sp", O["ms"][:, :], M["MN"][:, :], reads=["MN"], writes=["msout"])
    for h in range(4):
        load_wm(M, h)
        mlstm_head(M, h, lambda k, s, n: xnS[:, k, s:s + n], "xnS", "samp", obTs, "obTs")
    P.pop()
    phase_done(7)

    P.push()
    wga = P.sb("wga", [128, 8, 2048], BF16)
    wpb = P.sb("wpb", [128, 8, D], BF16)
    wpa = P.sb("wpa", [64, 4, D], BF16)
    mixst = P.sb("mixst", [128, 8, 512], BF16)
    sga = [P.sb("sga%d" % i, [128, 512], F32) for i in range(2)]
    sgb = [P.sb("sgb%d" % i, [128, 512], F32) for i in range(2)]
    t1 = [P.sb("t1_%d" % i, [128, 512], F32) for i in range(2)]
    wpb_v = I["w_pb"].rearrange("(k p) c -> p k c", p=128)
    for j in (0, 4):
        load_w(wga[:, :, j * 256:(j + 1) * 256], I["w_in_b"][25 + j, :, :, :], "wga%d" % j)
    load_w(wpa[:, :, :], I["w_pa"].rearrange("(h p) c -> p h c", p=64), "wpa")
    load_w(wpb[:, :, 0:512], wpb_v[:, :, 0:512], "wpb0")
    for j in (1, 5, 2, 6):
        load_w(wga[:, :, j * 256:(j + 1) * 256], I["w_in_b"][25 + j, :, :, :], "wga%d" % j)
    load_w(wpb[:, :, 512:1024], wpb_v[:, :, 512:1024], "wpb1")
    for j in (3, 7):
        load_w(wga[:, :, j * 256:(j + 1) * 256], I["w_in_b"][25 + j, :, :, :], "wga%d" % j)
    n = 0
    for tt in range(5):
        if tt < 4:
            N = 512
            xs_ = lambda k, tt=tt: xnT[:, k, tt * 512:(tt + 1) * 512]
            oas = lambda h, tt=tt: oaT[:, h, tt * 512:(tt + 1) * 512]
            obs = lambda k, tt=tt: obT[:, k, tt * 512:(tt + 1) * 512]
            rr = ["xnT", "oaT", "obT"]
        else:
            N = 4
            xs_ = lambda k: xnSc[:, k, :]
            oas = lambda h: oaTs[:, h, :]
            obs = lambda k: obTs[:, k, :]
            rr = ["xnSc", "oaTs", "obTs"]
        for c in range(8):
            i = n % 2
            for k in range(8):
                P.op("pe", lambda e, k=k, c=c: e.matmul(ps0[:, 0:N], wga[:, k, c * 128:(c + 1) * 128], xs_(k), start=(k == 0), stop=(k == 7)),
                     reads=["wga%d" % (c // 2)] + rr, writes=["ps0"], signal=(k == 7))
            for k in range(8):
                P.op("pe", lambda e, k=k, c=c: e.matmul(ps1[:, 0:N], wga[:, k, 1024 + c * 128:1024 + (c + 1) * 128], xs_(k), start=(k == 0), stop=(k == 7)),
                     reads=["wga%d" % (4 + c // 2)] + rr, writes=["ps1"], signal=(k == 7))
            for h in range(4):
                P.op("pe", lambda e, h=h, c=c: e.matmul(ps6[:, 0:N], wpa[:, h, c * 128:(c + 1) * 128], oas(h), start=(h == 0), stop=(h == 3)),
                     reads=["wpa"] + rr, writes=["ps6"], signal=(h == 3))
            for k in range(8):
                P.op("pe", lambda e, k=k, c=c: e.matmul(ps7[:, 0:N], wpb[:, k, c * 128:(c + 1) * 128], obs(k), start=(k == 0), stop=(k == 7)),
                     reads=["wpb%d" % (c // 4)] + rr, writes=["ps7"], signal=(k == 7))
            P.op("act", lambda e, i=i: e.activation(sga[i][:, 0:N], ps0[:, 0:N], AF.Sigmoid), reads=["ps0"], writes=["sga%d" % i])
            P.op("act", lambda e, i=i: e.activation(sgb[i][:, 0:N], ps1[:, 0:N], AF.Sigmoid), reads=["ps1"], writes=["sgb%d" % i])
            P.op("dve", lambda e, i=i: e.tensor_tensor(t1[i][:, 0:N], sga[i][:, 0:N], ps6[:, 0:N], ALU.mult), reads=["sga%d" % i, "ps6"], writes=["t1_%d" % i])
            P.op("dve", lambda e, i=i: e.tensor_tensor(sgb[i][:, 0:N], sgb[i][:, 0:N], ps7[:, 0:N], ALU.mult), reads=["sgb%d" % i, "ps7"], writes=["sgb%d" % i])
            if tt < 4:
                dst, dres = mixst[:, c, :], "mixst"
            else:
                dst, dres = mixS[:, c, :], "mixS"
            P.op("dve", lambda e, i=i, dst=dst: e.tensor_tensor(dst, t1[i][:, 0:N], sgb[i][:, 0:N], ALU.add), reads=["t1_%d" % i, "sgb%d" % i], writes=[dres])
            n += 1
        if tt < 4:
            P.op("act", lambda e, tt=tt: e.copy(xnT[:, :, tt * 512:(tt + 1) * 512], mixst[:, :, :]), reads=["mixst"], writes=["xnT"])
    P.pop()
    P.pop()
    P.pop()
    phase_done(8)

    P.push()
    mixT = xnT
    wout = P.sb("wout", [128, 8, D], BF16)
    wpg = P.sb("wpg", [128, 8, D], BF16)
    wpp = P.sb("wpp", [128, 2, D], BF16)
    gffn = P.sb("gffn", [128, D], F32)
    gple = P.sb("gple", [128, D], F32)
    gfin = P.sb("gfin", [128, D], F32)
    h32 = [P.sb("h32_%d" % i, [128, D], F32) for i in range(4)]
    xfT = P.sb("xfT", [128, 8, 512], BF16)
    hidT = P.sb("hidT", [128, NJ, 512], BF16)
    wgu = [P.sb("wgu%d" % i, [128, 8, 256], BF16) for i in range(5)]
    wdn = [P.sb("wdn%d" % i, [128, D], BF16) for i in range(6)]
    sgu = [P.sb("sgu%d" % i, [128, 512], BF16) for i in range(2)]
    sga2 = [P.sb("sga2_%d" % i, [128, 512], F32) for i in range(2)]
    pst = [P.sb("pst%d" % i, [128, 256], F32) for i in range(2)]
    pbf = [P.sb("pbf%d" % i, [128, 256], BF16) for i in range(2)]
    ppT = P.sb("ppT", [128, 2, 512], BF16)
    st = make_stage(with_xt=False)
    ss2 = [P.sb("ss2_%d" % i, [128, 1], F32) for i in range(2)]
    rs2 = [P.sb("rs2_%d" % i, [128, 1], F32) for i in range(2)]
    load_w(wout[:, :, :], I["w_out"].rearrange("(k p) c -> p k c", p=128), "wout")
    P.dma("sp", gffn[:], I["g_ffn"][0, :].partition_broadcast(128), writes=["gffn"])
    P.dma("sp", gple[:], I["g_ple"][0, :].partition_broadcast(128), writes=["gple"])
    P.dma("sp", gfin[:], I["g_fin"][0, :].partition_broadcast(128), writes=["gfin"])
    wdn_v = I["w_down"].rearrange("(j p) c -> p j c", p=128)
    nit = 0
    for tt in range(5):
        samp = (tt == 4)
        nb = 1 if samp else 4
        npp = 4 if samp else 128
        ncol = 4 if samp else 512
        def res_item(tb, ni):
            hh = h32[tb]; hr = "h32_%d" % tb
            i = ni % 2
            t = "n%d" % i
            _, junk, ss, rs, xnb = st

            def r1():
                if samp:
                    P.dma("sp", hh[0:4, :], I["xs"][0:4, :], writes=[hr])
                    lhs = lambda k: mixS[:, k, :]
                    lres = "mixS"
                else:
                    r0 = NT + tt * 512 + tb * 128
                    P.dma("sp", hh[:, :], I["x2"][r0:r0 + 128, :], writes=[hr])
                    lhs = lambda k: mixT[:, k, tt * 512 + tb * 128: tt * 512 + (tb + 1) * 128]
                    lres = "xnT"
                for half in range(2):
                    pz, pzr = banks[(tb % 2) * 2 + half]
                    for k in range(8):
                        P.op("pe", lambda e, k=k, pz=pz, half=half: e.matmul(pz[0:npp, :], lhs(k), wout[:, k, half * 512:(half + 1) * 512], start=(k == 0), stop=(k == 7)),
                             reads=[lres, "wout"], writes=[pzr], signal=(k == 7))
                    P.op("dve", lambda e, pz=pz, half=half: e.tensor_tensor(hh[0:npp, half * 512:(half + 1) * 512], hh[0:npp, half * 512:(half + 1) * 512], pz[0:npp, :], ALU.add),
                         reads=[hr, pzr], writes=[hr])

            def r2():
                rms_rstd(hh[0:npp, :], junk[i][0:npp, :], ss[i][0:npp, :], rs[i][0:npp, :], [hr], t)
                P.op("dve", lambda e: e.scalar_tensor_tensor(xnb[i][0:npp, :], hh[0:npp, :], rs[i][0:npp, 0:1], gffn[0:npp, :], ALU.mult, ALU.mult),
                     reads=[hr, "rs" + t, "gffn"], writes=["xnb%d" % i])

            def r3():
                for k in range(8):
                    P.op("pe", lambda e, k=k: e.transpose(psT[:, k * 128:k * 128 + npp], xnb[i][0:npp, k * 128:(k + 1) * 128], ident[0:npp, 0:npp]),
                         reads=["xnb%d" % i, "ident"], writes=["psT"], signal=(k == 7))
                P.op("act", lambda e: e.copy(xfT[:, :, tb * 128:tb * 128 + npp], psT[:, :].rearrange("p (k t) -> p k t", k=8)[:, :, 0:npp]),
                     reads=["psT"], writes=["xfT"])
            return (r1, r2, r3)

        items = []
        for tb in range(nb):
            items.append(res_item(tb, nit))
            nit += 1
        run_pipeline(items)
        for j in range(NJ):
            ws = j % 5
            load_w(wgu[ws][:, :, :], I["wgu_b"][j, :, :, :], "wgu%d" % ws)
            i = j % 2
            (pga, pgar), (pua, puar) = (banks[0], banks[1]) if i == 0 else (banks[2], banks[3])
            for k in range(8):
                P.op("pe", lambda e, k=k, ws=ws, pga=pga: e.matmul(pga[:, 0:ncol], wgu[ws][:, k, 0:128], xfT[:, k, 0:ncol], start=(k == 0), stop=(k == 7)),
                     reads=["wgu%d" % ws, "xfT"], writes=[pgar], signal=(k == 7))
            for k in range(8):
                P.op("pe", lambda e, k=k, ws=ws, pua=pua: e.matmul(pua[:, 0:ncol], wgu[ws][:, k, 128:256], xfT[:, k, 0:ncol], start=(k == 0), stop=(k == 7)),
                     reads=["wgu%d" % ws, "xfT"], writes=[puar], signal=(k == 7))
            P.op("act", lambda e, i=i, pga=pga: e.activation(sgu[i][:, 0:ncol], pga[:, 0:ncol], AF.Silu), reads=[pgar], writes=["sgu%d" % i])
            P.op("dve", lambda e, i=i, j=j, pua=pua: e.tensor_tensor(hidT[:, j, 0:ncol], sgu[i][:, 0:ncol], pua[:, 0:ncol], ALU.mult), reads=["sgu%d" % i, puar], writes=["hidT"])
        outs = [(tb, half) for tb in range(nb) for half in range(2)]
        for j in range(NJ):
            ws = j % 6
            load_w(wdn[ws][:, :], wdn_v[:, j, :], "wdn%d" % ws)
            for oi, (tb, half) in enumerate(outs):
                pz, pzr = banks[oi]
                P.op("pe", lambda e, j=j, ws=ws, pz=pz, tb=tb, half=half: e.matmul(pz[0:npp, 0:512], hidT[:, j, tb * 128:tb * 128 + npp], wdn[ws][:, half * 512:(half + 1) * 512],
                                                                                 start=(j == 0), stop=(j == NJ - 1), skip_group_check=True),
                     reads=["hidT", "wdn%d" % ws], writes=[pzr], signal=(j == NJ - 1 or oi == len(outs) - 1))
        for oi, (tb, half) in enumerate(outs):
            pz, pzr = banks[oi]
            hh = h32[tb]; hr = "h32_%d" % tb
            P.op("dve", lambda e, pz=pz, half=half, hh=hh: e.tensor_tensor(hh[0:npp, half * 512:(half + 1) * 512], hh[0:npp, half * 512:(half + 1) * 512], pz[0:npp, 0:512], ALU.add),
                 reads=[hr, pzr], writes=[hr])
        if tt == 0:
            load_w(wpg[:, :, :], I["w_pg"].rearrange("(k p) c -> p k c", p=128), "wpg")
            load_w(wpp[:, :, :], I["w_pp"].rearrange("(k p) c -> p k c", p=128), "wpp")
        def ple_item(tb, ni):
            hh = h32[tb]; hr = "h32_%d" % tb
            i = ni % 2
            t = "n%d" % i
            pi = ni % 2
            _, junk, ss, rs, xnb = st

            def p1():
                rms_rstd(hh[0:npp, :], junk[i][0:npp, :], ss[i][0:npp, :], rs[i][0:npp, :], [hr], t)
                P.op("dve", lambda e: e.scalar_tensor_tensor(xnb[i][0:npp, :], hh[0:npp, :], rs[i][0:npp, 0:1], gple[0:npp, :], ALU.mult, ALU.mult),
                     reads=[hr, "rs" + t, "gple"], writes=["xnb%d" % i])
                if samp:
                    P.dma("sp", pst[pi][0:4, :], I["psm"][0:4, :], writes=["pst%d" % pi])
                else:
                    r0 = tt * 512 + tb * 128
                    P.dma("sp", pst[pi][:, :], I["po"][r0:r0 + 128, :], writes=["pst%d" % pi])
                P.op("dve", lambda e: e.tensor_copy(pbf[pi][0:npp, :], pst[pi][0:npp, :]), reads=["pst%d" % pi], writes=["pbf%d" % pi])

            def p2():
                for k in range(8):
                    P.op("pe", lambda e, k=k: e.transpose(psT[:, k * 128:k * 128 + npp], xnb[i][0:npp, k * 128:(k + 1) * 128], ident[0:npp, 0:npp]),
                         reads=["xnb%d" % i, "ident"], writes=["psT"], signal=(k == 7))
                P.op("act", lambda e: e.copy(xfT[:, :, tb * 128:tb * 128 + npp], psT[:, :].rearrange("p (k t) -> p k t", k=8)[:, :, 0:npp]),
                     reads=["psT"], writes=["xfT"])
                for k in range(2):
                    P.op("pe", lambda e, k=k: e.transpose(psT[:, k * 128:k * 128 + npp], pbf[pi][0:npp, k * 128:(k + 1) * 128], ident[0:npp, 0:npp]),
                         reads=["pbf%d" % pi, "ident"], writes=["psT"], signal=(k == 1))
                P.op("act", lambda e: e.copy(ppT[:, :, tb * 128:tb * 128 + npp], psT[:, 0:256].rearrange("p (k t) -> p k t", k=2)[:, :, 0:npp]),
                     reads=["psT"], writes=["ppT"])

            def p3():
                for half in range(2):
                    pg, pgr = banks[half * 2]
                    pp, ppr = banks[half * 2 + 1]
                    for k in range(8):
                        P.op("pe", lambda e, k=k, pg=pg, half=half: e.matmul(pg[0:npp, :], xfT[:, k, tb * 128:tb * 128 + npp], wpg[:, k, half * 512:(half + 1) * 512], start=(k == 0), stop=(k == 7)),
                             reads=["xfT", "wpg"], writes=[pgr], signal=(k == 7))
                    for k in range(2):
                        P.op("pe", lambda e, k=k, pp=pp, half=half: e.matmul(pp[0:npp, :], ppT[:, k, tb * 128:tb * 128 + npp], wpp[:, k, half * 512:(half + 1) * 512], start=(k == 0), stop=(k == 1)),
                             reads=["ppT", "wpp"], writes=[ppr], signal=(k == 1))
                    si = half
                    P.op("act", lambda e, pg=pg, si=si: e.activation(sga2[si][0:npp, :], pg[0:npp, :], AF.Sigmoid), reads=[pgr], writes=["sga2_%d" % si])
                    P.op("dve", lambda e, pp=pp, si=si: e.tensor_tensor(sga2[si][0:npp, :], sga2[si][0:npp, :], pp[0:npp, :], ALU.mult), reads=["sga2_%d" % si, ppr], writes=["sga2_%d" % si])
                    P.op("dve", lambda e, si=si, half=half: e.tensor_tensor(hh[0:npp, half * 512:(half + 1) * 512], hh[0:npp, half * 512:(half + 1) * 512], sga2[si][0:npp, :], ALU.add),
                         reads=[hr, "sga2_%d" % si], writes=[hr])

            def p4():
                rms_rstd(hh[0:npp, :], junk[i][0:npp, :], ss2[i][0:npp, :], rs2[i][0:npp, :], [hr], "f%d" % i)
                P.op("dve", lambda e: e.scalar_tensor_tensor(hh[0:npp, :], hh[0:npp, :], rs2[i][0:npp, 0:1], gfin[0:npp, :], ALU.mult, ALU.mult),
                     reads=[hr, "rsf%d" % i, "gfin"], writes=[hr])
                if samp:
                    P.dma("sp", O["ys"][0:4, :], hh[0:4, :], reads=[hr], writes=["ysout"])
                else:
                    r0 = tt * 512 + tb * 128
                    P.dma("sp", O["y"][r0:r0 + 128, :], hh[:, :], reads=[hr], writes=["yout"])
            return (p1, p2, p3, p4)

        items = []
        for tb in range(nb):
            items.append(ple_item(tb, nit))
            nit += 1
        run_pipeline(items)
    P.pop()
    P.pop()
    P.finish()


_NC = None


def _rope_tab(pos):
    inv = np.power(np.float32(500000.0), -np.arange(8, dtype=np.float32) / np.float32(8)).astype(np.float32)
    ang = pos.astype(np.float32)[:, None] * inv[None, :]
    return np.concatenate([np.cos(ang), np.sin(ang)], axis=1).astype(np.float32)


def _consts(half):
    j = np.arange(128)
    c = {}
    c["ident"] = np.eye(128, dtype=np.float32)
    c["mprev"] = (j[:, None] >= j[None, :]).astype(np.float32)
    c["mcur"] = (j[:, None] <= j[None, :]).astype(np.float32)
    c["mbias"] = np.where(j[:, None] <= j[None, :], 0.0, NEGPAD).astype(np.float32)
    sel = np.zeros((4, 4, 128), np.float32)
    for h in range(4):
        sel[h, h, :] = 1.0
    c["sel4"] = sel.reshape(4, 512)
    own0 = half * NT
    rope = np.zeros((128, 69, 16), np.float32)
    for g, d in enumerate(DILS):
        span = 128 * d
        for r in range(d):
            pos = own0 - span + r + d * j
            rope[:, RB[g] + r, :] = _rope_tab(np.maximum(pos, 0))
        for s in range(NT // span):
            for r in range(d):
                pos = own0 + s * span + r + d * j
                rope[:, RB[g] + d + s * d + r, :] = _rope_tab(pos)
    c["rope"] = rope
    c["rope_s"] = _rope_tab(np.full((128,), PAST))
    c["flag"] = np.full((128, 1), float(half), np.float32)
    return c


def _block_w_in(w):
    offs = []
    for g in range(3):
        offs += [OFF_AQ + g * 256, OFF_AK + g * 256, OFF_AV + g * 256]
    for h in range(4):
        offs += [OFF_MQ + h * 256, OFF_MK + h * 256, OFF_MV + h * 256, OFF_MO + h * 256]
    offs += [OFF_GA + j * 256 for j in range(8)]
    out = np.empty((33, 128, 8, 256), np.float32)
    for i, o in enumerate(offs):
        out[i] = w[:, o:o + 256].reshape(8, 128, 256).transpose(1, 0, 2)
    return out


def _block_gu(wg, wu):
    out = np.empty((NJ, 128, 8, 256), np.float32)
    out[:, :, :, 0:128] = wg.reshape(8, 128, NJ, 128).transpose(2, 1, 0, 3)
    out[:, :, :, 128:256] = wu.reshape(8, 128, NJ, 128).transpose(2, 1, 0, 3)
    return out


def kernel(x_prompt, x_sample, cache_kv_w128, cache_kv_w512, cache_kv_w2048, state_conv, state_C, state_n, state_m,
           p_prompt, p_sample, norm_mix, w_in, conv_w, conv_b, b_igate, b_fgate, mh_norm, w_proj_a, w_proj_b, w_out,
           norm_ffn, w_gate, w_up, w_down, norm_ple, w_ple_gate, w_ple_proj, norm_final):
    global _NC
    f = lambda a: np.ascontiguousarray(np.asarray(a, dtype=np.float32))
    x_prompt, x_sample, p_prompt, p_sample = f(x_prompt), f(x_sample), f(p_prompt), f(p_sample)
    caches = [f(cache_kv_w128), f(cache_kv_w512), f(cache_kv_w2048)]
    state_conv, state_C, state_n, state_m = f(state_conv), f(state_C), f(state_n), f(state_m)
    shared = {
        "w_in_b": _block_w_in(f(w_in)[0]), "w_gates": f(f(w_in)[0][:, OFF_MI:OFF_MI + 8].reshape(8, 128, 8).transpose(1, 0, 2)),
        "wgu_b": _block_gu(f(w_gate)[0], f(w_up)[0]), "cw": f(f(conv_w)[0].reshape(4, 16, 128).transpose(2, 1, 0)),
        "cb": f(f(conv_b)[0].reshape(16, 128).T), "b_i": f(b_igate)[0].reshape(4, 1), "b_f": f(b_fgate)[0].reshape(4, 1),
        "mhn_col": f(f(mh_norm)[0].reshape(8, 128).T), "w_pa": f(w_proj_a)[0], "w_pb": f(w_proj_b)[0], "w_out": f(w_out)[0],
        "g_mix": f(norm_mix)[0].reshape(1, D), "g_ffn": f(norm_ffn)[0].reshape(1, D), "g_ple": f(norm_ple)[0].reshape(1, D),
        "g_fin": f(norm_final).reshape(1, D), "w_down": f(w_down)[0],
        "w_pg": f(w_ple_gate)[0], "w_pp": f(w_ple_proj)[0],
    }
    cons = [_consts(0), _consts(1)]
    in_maps = []
    for c in range(8):
        b, half = c // 2, c % 2
        own = x_prompt[b, half * NT:(half + 1) * NT]
        hist = x_prompt[b, 0:NT]
        m = dict(shared)
        m.update(cons[half])
        m["x2"] = f(np.concatenate([hist, own], axis=0))
        xs = np.zeros((128, D), np.float32); xs[0:4] = x_sample[4 * c:4 * c + 4, 0]
        m["xs"] = xs
        m["po"] = f(p_prompt[0, b, half * NT:(half + 1) * NT])
        psm = np.zeros((128, 256), np.float32); psm[0:4] = p_sample[0, 4 * c:4 * c + 4, 0]
        m["psm"] = psm
        for g in range(3):
            m["ck%d" % g] = f(caches[g][0, 4 * c:4 * c + 4].reshape(4, WINS[g], 512))
        sc = state_conv[0, 4 * c:4 * c + 4]
        m["sconv"] = f(sc)
        m["sconvT"] = f(sc.reshape(4, 3, 16, 128).transpose(3, 0, 2, 1))
        m["sC"] = f(state_C[0, 4 * c:4 * c + 4])
        m["sn"] = f(state_n[0, 4 * c:4 * c + 4])
        m["smT"] = f(state_m[0, 4 * c:4 * c + 4].T)
        in_maps.append(m)
    if _NC is None:
        _NC = build()
    res = run_bass_kernel_spmd(_NC, in_maps, core_ids=list(range(8))).results

    y_prompt = np.zeros((4, SEQ, D), np.float32)
    y_sample = np.zeros((32, 1, D), np.float32)
    kvp = [np.zeros((1, 4, WINS[g], 2, 4, 64), np.float32) for g in range(3)]
    kvs = [np.zeros((1, 32, WINS[g], 2, 4, 64), np.float32) for g in range(3)]
    conv_p = np.zeros((1, 4, 3, 2048), np.float32); conv_s = np.zeros((1, 32, 3, 2048), np.float32)
    C_p = np.zeros((1, 4, 4, 256, 256), np.float32); C_s = np.zeros((1, 32, 4, 256, 256), np.float32)
    n_p = np.zeros((1, 4, 4, 256), np.float32); n_s = np.zeros((1, 32, 4, 256), np.float32)
    m_p = np.zeros((1, 4, 4), np.float32); m_s = np.zeros((1, 32, 4), np.float32)
    for c in range(8):
        b, half = c // 2, c % 2
        r = res[c]
        y_prompt[b, half * NT:(half + 1) * NT] = r["y"]
        y_sample[4 * c:4 * c + 4, 0] = r["ys"]
        for g in range(3):
            kvs[g][0, 4 * c:4 * c + 4] = r["kvs%d" % g].reshape(4, WINS[g], 2, 4, 64)
        conv_s[0, 4 * c:4 * c + 4] = r["convs"]
        C_s[0, 4 * c:4 * c + 4] = r["Cs"]
        n_s[0, 4 * c:4 * c + 4] = r["ns"]
        m_s[0, 4 * c:4 * c + 4] = r["ms"].T
        if half == 1:
            for g in range(3):
                kvp[g][0, b] = r["kvp%d" % g].reshape(WINS[g], 2, 4, 64)
            conv_p[0, b] = r["convp"]
            C_p[0, b] = r["Cp"]
            n_p[0, b] = r["np"]
            m_p[0, b] = r["mp"][:, 0]
    return (y_prompt, y_sample, kvp[0], kvs[0], kvp[1], kvs[1], kvp[2], kvs[2], conv_p, conv_s, C_p, C_s, n_p, n_s, m_p, m_s)
```
